# Optimizing a Trainium2 kernel written in Bass

```python
import math
import jax, jax.numpy as jnp
from jax import lax
import numpy as np

D_MODEL = 1024
BATCH = 32
SEQ = 256
DEPTH = 1
DEC_BATCH = 8
DEC_SEQ = 1024
PAST_LEN = 256

GRID_W = 64
D_S5 = D_MODEL
S5_GROUP = 16
N_S5_GROUPS = D_S5 // S5_GROUP
S5_STATE = 64
D_POOL = D_MODEL // 2
POOL_WINDOWS = (2, 4, 8, 16)
N_POOL_GROUPS = len(POOL_WINDOWS)
POOL_GROUP = D_POOL // N_POOL_GROUPS
D_IN = D_S5 + D_POOL + 2 * D_MODEL
D_FF = 4 * D_MODEL
N_MOD = 6
DEEPNORM_ALPHA = (2.0 * DEPTH) ** 0.25
DEEPNORM_BETA = (8.0 * DEPTH) ** -0.25
LN_EPS = 1e-6

kernel_name = 'hybrid_s5_pool_diffusion_step'


def layer_norm(x, g=None, b=None):
    xf = x.astype(jnp.float32)
    mu = jnp.mean(xf, axis=-1, keepdims=True)
    var = jnp.mean(jnp.square(xf - mu), axis=-1, keepdims=True)
    y = (xf - mu) * lax.rsqrt(var + LN_EPS)
    if g is not None:
        y = y * g.astype(jnp.float32) + b.astype(jnp.float32)
    return y.astype(x.dtype)


def adaln(cvec, w, b):
    m = jax.nn.silu(cvec) @ w + b
    return m.reshape(cvec.shape[0], N_MOD, 1, D_MODEL)


def _combine(e1, e2):
    a1, b1 = e1
    a2, b2 = e2
    return a1 * a2, a2 * b1 + b2


def s5_direction(u, s0, lam_re, lam_im, log_dt, b_re, b_im, c_re, c_im, d, reverse):
    f32 = jnp.float32
    bsz, n, _ = u.shape
    uf = u.astype(f32).reshape(bsz, n, N_S5_GROUPS, S5_GROUP)
    lam = lax.complex(lam_re.astype(f32), lam_im.astype(f32))
    dt = jnp.exp(log_dt.astype(f32))[:, None]
    lam_bar = jnp.exp(lam * dt)
    b_mat = lax.complex(b_re.astype(f32), b_im.astype(f32))
    c_mat = lax.complex(c_re.astype(f32), c_im.astype(f32))
    b_bar = ((lam_bar - 1.0) / lam)[:, :, None] * b_mat
    bu = jnp.einsum('blgh,gph->blgp', uf.astype(jnp.complex64), b_bar)
    first, last = (n - 1, 0) if reverse else (0, n - 1)
    if s0 is not None:
        bu = bu.at[:, first].add(lam_bar * s0)
    a = jnp.broadcast_to(lam_bar, bu.shape)
    _, s = lax.associative_scan(_combine, (a, bu), reverse=reverse, axis=1)
    y = jnp.einsum('blgp,ghp->blgh', s, c_mat).real + d.astype(f32).reshape(N_S5_GROUPS, S5_GROUP) * uf
    return y.reshape(bsz, n, D_S5), s[:, last]


def pool_mix(u, w_pool, pool_scale, grid_w):
    bsz, n_tok, _ = u.shape
    if grid_w is not None:
        rows = n_tok // grid_w
        v = u.reshape(bsz, rows, grid_w, D_POOL)
    else:
        v = u
    vf = v.astype(jnp.float32)
    n = vf.shape[-2]
    cs = jnp.cumsum(vf, axis=-2)
    cs = jnp.concatenate([jnp.zeros_like(cs[..., :1, :]), cs], axis=-2)
    t = jnp.arange(n)
    outs = []
    for gi, w in enumerate(POOL_WINDOWS):
        lo = jnp.clip(t - w // 2, 0, n)
        hi = jnp.clip(t - w // 2 + w, 0, n)
        cg = cs[..., gi * POOL_GROUP:(gi + 1) * POOL_GROUP]
        mean = (jnp.take(cg, hi, axis=-2) - jnp.take(cg, lo, axis=-2)) / (hi - lo).astype(jnp.float32)[:, None]
        outs.append(mean - vf[..., gi * POOL_GROUP:(gi + 1) * POOL_GROUP])
    p = jnp.stack(outs, axis=-2)
    p = jnp.einsum('...gi,gio->...go', p, w_pool).reshape(vf.shape) * pool_scale
    return p.reshape(bsz, n_tok, D_POOL)


def token_mixer(h, s0_f, s0_b, grid_w, lp):
    z = h @ lp['w_in'] + lp['b_in']
    u_a = z[..., :D_S5]
    u_b = z[..., D_S5:D_S5 + D_POOL]
    g_a = z[..., D_S5 + D_POOL:D_S5 + D_POOL + D_MODEL]
    g_b = z[..., D_S5 + D_POOL + D_MODEL:]
    s5 = lp['s5']
    y_f, sf = s5_direction(u_a, s0_f, *[p[0] for p in s5], reverse=False)
    y_b, sb = s5_direction(u_a, s0_b, *[p[1] for p in s5], reverse=True)
    v = jax.nn.gelu(y_f + y_b)
    v = v * jax.nn.sigmoid(v @ lp['w_glu'] + lp['b_glu'])
    ya = v @ lp['w_proj_a']
    yb = pool_mix(u_b, lp['w_pool'], lp['pool_scale'], grid_w) @ lp['w_proj_b']
    merged = jax.nn.sigmoid(g_a) * ya + jax.nn.sigmoid(g_b) * yb
    return merged @ lp['w_out'] + lp['b_out'], sf, sb


def trunk_layer(x, mod, s0_f, s0_b, grid_w, lp):
    h = layer_norm(x) * (1.0 + mod[:, 1]) + mod[:, 0]
    tm, sf, sb = token_mixer(h, s0_f, s0_b, grid_w, lp)
    x = layer_norm(DEEPNORM_ALPHA * x + mod[:, 2] * tm, lp['ln1_g'], lp['ln1_b'])
    h2 = layer_norm(x) * (1.0 + mod[:, 4]) + mod[:, 3]
    f = jnp.square(jax.nn.relu(h2 @ lp['w_mlp1'] + lp['b_mlp1'])) @ lp['w_mlp2'] + lp['b_mlp2']
    x = layer_norm(DEEPNORM_ALPHA * x + mod[:, 5] * f, lp['ln2_g'], lp['ln2_b'])
    return x, sf, sb


def setup_inputs(seed: int = 0) -> dict:
    key = jax.random.key(seed)
    ks = jax.random.split(key, 40)
    f32 = jnp.float32

    def nrm(k, shape, scale):
        return jax.random.normal(k, shape, f32) * scale

    L, G, P, H = DEPTH, N_S5_GROUPS, S5_STATE, S5_GROUP
    lam_im_base = (math.pi * jnp.arange(P, dtype=f32))[None, None, None, :]
    return {
        'x_prompt': nrm(ks[0], (BATCH, SEQ, D_MODEL), 1.0),
        'x_sample': nrm(ks[1], (DEC_BATCH, DEC_SEQ, D_MODEL), 1.0),
        'state_s5': nrm(ks[2], (DEC_BATCH, L, 2, 2, G, P), 0.1),
        'c': nrm(ks[3], (DEC_BATCH, D_MODEL), 1.0),
        'c_ctx': nrm(ks[4], (D_MODEL,), 1.0),
        'w_ada': nrm(ks[5], (L, D_MODEL, N_MOD * D_MODEL), D_MODEL ** -0.5),
        'b_ada': nrm(ks[6], (L, N_MOD * D_MODEL), 0.01),
        'w_in': nrm(ks[7], (L, D_MODEL, D_IN), D_MODEL ** -0.5),
        'b_in': nrm(ks[8], (L, D_IN), 0.01),
        's5_lam_re': -0.5 + nrm(ks[9], (L, 2, G, P), 0.01),
        's5_lam_im': lam_im_base + nrm(ks[10], (L, 2, G, P), 0.01),
        's5_log_dt': jax.random.uniform(ks[11], (L, 2, G), f32, math.log(1e-3), math.log(1e-1)),
        's5_b_re': nrm(ks[12], (L, 2, G, P, H), (2.0 * H) ** -0.5),
        's5_b_im': nrm(ks[13], (L, 2, G, P, H), (2.0 * H) ** -0.5),
        's5_c_re': nrm(ks[14], (L, 2, G, H, P), P ** -0.5),
        's5_c_im': nrm(ks[15], (L, 2, G, H, P), P ** -0.5),
        's5_d': nrm(ks[16], (L, 2, D_S5), 1.0),
        'w_glu': nrm(ks[17], (L, D_S5, D_S5), D_S5 ** -0.5),
        'b_glu': nrm(ks[18], (L, D_S5), 0.01),
        'w_proj_a': nrm(ks[19], (L, D_S5, D_MODEL), D_S5 ** -0.5),
        'w_pool': nrm(ks[20], (L, N_POOL_GROUPS, POOL_GROUP, POOL_GROUP), POOL_GROUP ** -0.5),
        'pool_scale': 1.0 + nrm(ks[21], (L, D_POOL), 0.1),
        'w_proj_b': nrm(ks[22], (L, D_POOL, D_MODEL), D_POOL ** -0.5),
        'w_out': nrm(ks[23], (L, D_MODEL, D_MODEL), D_MODEL ** -0.5 * DEEPNORM_BETA),
        'b_out': nrm(ks[24], (L, D_MODEL), 0.01),
        'ln1_g': 1.0 + nrm(ks[25], (L, D_MODEL), 0.01),
        'ln1_b': nrm(ks[26], (L, D_MODEL), 0.01),
        'w_mlp1': nrm(ks[27], (L, D_MODEL, D_FF), D_MODEL ** -0.5),
        'b_mlp1': nrm(ks[28], (L, D_FF), 0.01),
        'w_mlp2': nrm(ks[29], (L, D_FF, D_MODEL), D_FF ** -0.5 * DEEPNORM_BETA),
        'b_mlp2': nrm(ks[30], (L, D_MODEL), 0.01),
        'ln2_g': 1.0 + nrm(ks[31], (L, D_MODEL), 0.01),
        'ln2_b': nrm(ks[32], (L, D_MODEL), 0.01),
    }


def reference(x_prompt, x_sample, state_s5, c, c_ctx, w_ada, b_ada, w_in, b_in,
              s5_lam_re, s5_lam_im, s5_log_dt, s5_b_re, s5_b_im, s5_c_re, s5_c_im, s5_d,
              w_glu, b_glu, w_proj_a, w_pool, pool_scale, w_proj_b, w_out, b_out,
              ln1_g, ln1_b, w_mlp1, b_mlp1, w_mlp2, b_mlp2, ln2_g, ln2_b):
    f32 = jnp.float32
    yp = x_prompt
    ys = x_sample
    new_states = []
    for l in range(DEPTH):
        lp = {
            'w_in': w_in[l], 'b_in': b_in[l],
            's5': (s5_lam_re[l], s5_lam_im[l], s5_log_dt[l], s5_b_re[l], s5_b_im[l],
                   s5_c_re[l], s5_c_im[l], s5_d[l]),
            'w_glu': w_glu[l], 'b_glu': b_glu[l], 'w_proj_a': w_proj_a[l],
            'w_pool': w_pool[l], 'pool_scale': pool_scale[l], 'w_proj_b': w_proj_b[l],
            'w_out': w_out[l], 'b_out': b_out[l],
            'ln1_g': ln1_g[l], 'ln1_b': ln1_b[l],
            'w_mlp1': w_mlp1[l], 'b_mlp1': b_mlp1[l], 'w_mlp2': w_mlp2[l], 'b_mlp2': b_mlp2[l],
            'ln2_g': ln2_g[l], 'ln2_b': ln2_b[l],
        }
        mod_ctx = adaln(c_ctx[None, :], w_ada[l], b_ada[l])
        yp, sf, sb = trunk_layer(yp, mod_ctx, None, None, None, lp)
        new_states.append(jnp.stack([jnp.stack([sf.real, sf.imag], axis=1),
                                     jnp.stack([sb.real, sb.imag], axis=1)], axis=1))
        st = state_s5[:, l].astype(f32)
        s0_f = lax.complex(st[:, 0, 0], st[:, 0, 1])
        s0_b = lax.complex(st[:, 1, 0], st[:, 1, 1])
        mod_lat = adaln(c, w_ada[l], b_ada[l])
        ys, _, _ = trunk_layer(ys, mod_lat, s0_f, s0_b, GRID_W, lp)
    new_state_s5 = jnp.stack(new_states, axis=1)
    return (yp, ys, new_state_s5)
```

```python
import math
from contextlib import ExitStack
import numpy as np
import concourse.bass as bass
import concourse.mybir as mybir
from concourse.bass_utils import run_bass_kernel_spmd

F32 = mybir.dt.float32
BF16 = mybir.dt.bfloat16
AF = mybir.ActivationFunctionType
ALU = mybir.AluOpType
NCORES = 8
D = 1024
ALPHA = 2.0 ** 0.25
EPS = 1e-6
PI = math.pi


class Tok:
    __slots__ = ("sem", "val", "eng", "key")

    def __init__(self, sem, val, eng, key):
        self.sem, self.val, self.eng, self.key = sem, val, eng, key


class Buf:
    __slots__ = ("w", "r")

    def __init__(self):
        self.w = {}
        self.r = {}


class Eng:
    def __init__(self, name, h, sem):
        self.name, self.h, self.sem = name, h, sem
        self.n = 0
        self.waited = {}


class KB:
    def __init__(self, nc, es):
        self.nc = nc
        self.es = es
        self.E = {}
        for name, h in (("pe", nc.tensor), ("act", nc.scalar), ("dve", nc.vector),
                        ("pool", nc.gpsimd), ("sp", nc.sync)):
            self.E[name] = Eng(name, h, es.enter_context(nc.semaphore("sem_" + name)))
        self.rings = {}
        for q in ("sp", "pool"):
            self.rings[q] = [[es.enter_context(nc.semaphore("dq_%s_%d" % (q, i))), 0, None]
                             for i in range(12)]
        self.ring_pos = {"sp": 0, "pool": 0}
        self.bufs = {}
        self.nbank = 0

    def B(self, *key):
        b = self.bufs.get(key)
        if b is None:
            b = self.bufs[key] = Buf()
        return b

    def _wait(self, e, t):
        if t is None:
            return
        if t.eng is e and e.name in ("pe", "sp"):
            return
        if e.waited.get(t.key, 0) >= t.val:
            return
        e.h.wait_ge(t.sem, t.val)
        e.waited[t.key] = t.val

    def _deps(self, e, r, w, extra):
        for b in r:
            for t in b.w.values():
                self._wait(e, t)
        for b in w:
            for t in list(b.w.values()) + list(b.r.values()):
                self._wait(e, t)
        for t in extra:
            self._wait(e, t)

    def _post(self, tok, r, w):
        for b in r:
            b.r[tok.key] = tok
        for b in w:
            b.w[tok.key] = tok
            b.r = {}

    def op(self, en, fn, r=(), w=(), inc=True, extra=()):
        e = self.E[en]
        self._deps(e, r, w, extra)
        inst = fn(e.h)
        if inc:
            e.n += 1
            inst.then_inc(e.sem, 1)
            tok = Tok(e.sem, e.n, e, en)
        else:
            tok = Tok(e.sem, e.n + 1, e, en)
        self._post(tok, r, w)
        return tok

    def dma(self, q, out, in_, r=(), w=(), extra=(), **kw):
        e = self.E[q]
        self._deps(e, r, w, extra)
        ring = self.rings[q]
        i = self.ring_pos[q]
        self.ring_pos[q] = (i + 1) % len(ring)
        slot = ring[i]
        self._wait(e, slot[2])
        inst = e.h.dma_start(out=out, in_=in_, **kw)
        slot[1] += 16
        inst.then_inc(slot[0], 16)
        tok = Tok(slot[0], slot[1], None, "d%s%d" % (q, i))
        slot[2] = tok
        self._post(tok, r, w)
        return tok

    def barrier(self, final=False, dma=True):
        toks = []
        for e in self.E.values():
            if e.n > 0:
                toks.append(Tok(e.sem, e.n, e, e.name))
        for q, ring in self.rings.items():
            if not dma:
                continue
            if q == "pool" and not final:
                continue
            for slot in ring:
                if slot[2] is not None:
                    toks.append(slot[2])
        for e in self.E.values():
            for t in toks:
                if t.eng is e:
                    continue
                self._wait(e, t)

    def bank(self):
        i = self.nbank % 8
        self.nbank += 1
        return self.ps[i], self.B("ps", i)


class Stop(Exception):
    pass


LIMIT = None
NOREV = False
DUMPS = []


def build_program():
    nc = bass.Bass("TRN2", target_bir_lowering=False)
    es = ExitStack()
    try:
        _build(nc, es)
    except Stop:
        return nc
    es.close()
    return nc


def _build(nc, es):

    def din(name, shape):
        return nc.dram_tensor(name, list(shape), F32, kind="ExternalInput").ap()

    xin = din("xin", [2048, D])
    st0 = din("st0", [2, 2, 64, 64])
    cvec = din("cvec", [2, D])
    w_ada = din("w_ada", [D, 6 * D])
    b_ada = din("b_ada", [6 * D])
    w_in = din("w_in", [D, 3584])
    b_in = din("b_in", [3584])
    lam_re = din("lam_re", [2, 64, 64])
    lam_im = din("lam_im", [2, 64, 64])
    log_dt = din("log_dt", [2, 64])
    b_re = din("b_re", [2, 64, 64, 16])
    b_im = din("b_im", [2, 64, 64, 16])
    c_re = din("c_re", [2, 64, 16, 64])
    c_im = din("c_im", [2, 64, 16, 64])
    s5_d = din("s5_d", [2, D])
    w_glu = din("w_glu", [D, D])
    b_glu = din("b_glu", [D])
    w_pa = din("w_pa", [D, D])
    w_pool = din("w_pool", [4, 128, 128])
    pool_scale = din("pool_scale", [512])
    w_pb = din("w_pb", [512, D])
    w_out = din("w_out", [D, D])
    b_out = din("b_out", [D])
    ln1_g = din("ln1_g", [1, D])
    ln1_b = din("ln1_b", [1, D])
    w1 = din("w1", [D, 4 * D])
    b1 = din("b1", [4 * D])
    w2 = din("w2", [4 * D, D])
    b2 = din("b2", [D])
    ln2_g = din("ln2_g", [1, D])
    ln2_b = din("ln2_b", [1, D])
    c_pc = din("c_pc", [128, 8 * 240])
    c_maskf = din("c_maskf", [128, 128])
    c_maskb = din("c_maskb", [128, 128])
    c_ident = din("c_ident", [128, 128])
    c_bandp = din("c_bandp", [128, 4 * 4 * 128])
    c_bands = din("c_bands", [128, 4 * 128])

    yout = nc.dram_tensor("yout", [2048, D], F32, kind="ExternalOutput").ap()
    nsout = nc.dram_tensor("nsout", [4, 2, 2, 64, 64], F32, kind="ExternalOutput").ap()
    x1s = nc.dram_tensor("x1s", [2048, D], F32, kind="Internal").ap()
    wzt_d = nc.dram_tensor("wzt_d", [128, 2, 64, 128], BF16, kind="Internal").ap()
    wyb_d = nc.dram_tensor("wyb_d", [128, 2, 64, 128], BF16, kind="Internal").ap()
    w0b_d = nc.dram_tensor("w0b_d", [128, 64, 128], BF16, kind="Internal").ap()

    kb = KB(nc, es)
    op, dma, B = kb.op, kb.dma, kb.B

    def cut(n):
        if LIMIT == n:
            kb.barrier()
            raise Stop()

    def dump(name, ap, bname):
        if LIMIT is None:
            return
        shp = list(ap.shape)
        dt_ = nc.dram_tensor("dbg_" + name, shp, ap.dtype, kind="ExternalOutput").ap()
        dma("sp", dt_, ap, r=[(B(*x) if isinstance(x, tuple) else B(x)) for x in bname], w=[B("dbg_" + name)])
        DUMPS.append("dbg_" + name)

    used_names = {}

    def sb(name, shape, dt, scope=es):
        k = used_names.get(name, 0)
        used_names[name] = k + 1
        if k:
            name = "%s_%d" % (name, k)
        return scope.enter_context(nc.sbuf_tensor(name, list(shape), dt))

    kb.ps = [es.enter_context(nc.psum_tensor("psb%d" % i, [128, 512], F32)) for i in range(8)]

    NW = 5
    wsl = [sb("wsl%d" % i, [128, 4096], BF16) for i in range(NW)]
    wstate = {"i": 0}
    pc = sb("pc", [128, 8, 240], BF16)
    ident = sb("ident", [128, 128], F32)
    identb = sb("identb", [128, 128], BF16)
    maskf = sb("maskf", [128, 128], F32)
    maskb = sb("maskb", [128, 128], F32)
    bandP = sb("bandP", [128, 4, 4, 128], BF16)
    bandS = sb("bandS", [128, 4, 128], BF16)
    bub = sb("bub", [128, 512], F32)
    lnp = sb("lnp", [128, 4, D], F32)
    binT = sb("binT", [128, 28], F32)
    badaT = sb("badaT", [128, 48], F32)
    bgluT = sb("bgluT", [128, 8], F32)
    boutT = sb("boutT", [128, 8], F32)
    b1T = sb("b1T", [128, 32], F32)
    b2T = sb("b2T", [128, 8], F32)
    pscT = sb("pscT", [128, 4], F32)
    modT = sb("modT", [128, 48, 2], F32)
    mder = sb("mder", [128, 4, 8, 2], F32)
    A4 = sb("A4", [128, 4, 64], F32)
    B4 = sb("B4", [128, 4, 64], F32)
    s0t = sb("s0t", [128, 2, 64], F32)
    neghalf = sb("neghalf", [128, 1], F32)
    DEPTH = 4
    xsl = [sb("xsl%d" % i, [128, D], F32) for i in range(DEPTH)]
    NLN = 8
    lnst = [sb("lnst", [128, 2, 6], F32) for _ in range(NLN)]
    lnmv = [sb("lnmv", [128, 2], F32) for _ in range(NLN)]
    lnve = [sb("lnve", [128, 1], F32) for _ in range(NLN)]
    lnrs = [sb("lnrs", [128, 1], F32) for _ in range(NLN)]
    lnnb = [sb("lnnb", [128, 1], F32) for _ in range(NLN)]
    lnstate = {"i": 0}
    class WP:
        def __init__(self):
            self.free = list(range(NW))
            self.queue = []
            self.issued = {}
            self.hist = []

        def plan(self, specs):
            self.queue.extend(specs)
            self.pump()

        def pump(self):
            while self.free and self.queue:
                key = self.queue[0][0]
                merge = len(key) > 1 and key[1] in ("pb", "pa", "ga", "gb")
                cand = [i for i in self.free if i < NW or merge]
                if not cand:
                    break
                key, src, r0, nrows, c0, ncols = self.queue.pop(0)
                i = cand[0]
                self.free.remove(i)
                kc = nrows // 128
                view = wsl[i][:, 0:kc * ncols].rearrange("p (k c) -> p k c", k=kc)
                wb = B("wsl", i)
                if len(self.hist) >= 2:
                    kb._wait(kb.E["pool"], self.hist[-2])
                tok_ = dma("pool", view, src[r0:r0 + nrows, c0:c0 + ncols].rearrange("(k p) c -> p k c", p=128), w=[wb])
                self.hist.append(tok_)
                self.issued[key] = (view, wb, i)

        def get(self, key):
            self.pump()
            assert key in self.issued, ("weight tile not issued (no free slot)", key, self.free, [q[0] for q in self.queue[:3]])
            v_, b_, _ = self.issued[key]
            return v_, b_

        def release(self, key):
            _, _, i = self.issued.pop(key)
            self.free.append(i)
            self.pump()

    wp = WP()

    def ln_part1(src, srcB):
        i = lnstate["i"] % NLN
        lnstate["i"] += 1
        st, mv, ve = lnst[i], lnmv[i], lnve[i]
        stB, mvB, veB = B("lnst", i), B("lnmv", i), B("lnve", i)
        op("dve", lambda h: h.bn_stats(out=st[:, 0, :], in_=src[:, 0:512]), r=[srcB], w=[stB])
        op("dve", lambda h: h.bn_stats(out=st[:, 1, :], in_=src[:, 512:1024]), r=[srcB], w=[stB])
        op("dve", lambda h: h.bn_aggr(out=mv[:], in_=st[:].rearrange("p a b -> p (a b)")), r=[stB], w=[mvB])
        op("act", lambda h: h.activation(out=ve[:], in_=mv[:, 1:2], func=AF.Sqrt, bias=epsT[:, 0:1], scale=1.0), r=[mvB, B("epsT")], w=[veB])
        return i

    def ln_part2(i, dst, src, srcB, dstB):
        mv, ve, rs, nb = lnmv[i], lnve[i], lnrs[i], lnnb[i]
        mvB, veB, rsB, nbB = B("lnmv", i), B("lnve", i), B("lnrs", i), B("lnnb", i)
        op("dve", lambda h: h.reciprocal(out=rs[:], in_=ve[:]), r=[veB], w=[rsB])
        op("dve", lambda h: h.scalar_tensor_tensor(out=nb[:], in0=mv[:, 0:1], scalar=-1.0, in1=rs[:], op0=ALU.mult, op1=ALU.mult),
           r=[mvB, rsB], w=[nbB])
        op("act", lambda h: h.activation(out=dst, in_=src, func=AF.Identity, scale=rs[:, 0:1], bias=nb[:, 0:1]),
           r=[srcB, rsB, nbB], w=[dstB])

    def run_pipelined(gens, depth, filler=None):
        pending = list(gens)
        active = []
        while pending or active:
            while pending and len(active) < depth:
                active.append(pending.pop(0))
            for g in list(active):
                try:
                    next(g)
                except StopIteration:
                    active.remove(g)
            if filler is not None:
                filler()

    def to_feature_major(src, srcB, dstT, dstB_fn, tb, scale_ap, bias_ap, v):
        for hf in range(2):
            bk, bkB = kb.bank()
            for j in range(4):
                ct = hf * 4 + j
                op("pe", lambda h, ct=ct, j=j: h.transpose(bk[:, j * 128:(j + 1) * 128], src[:, ct * 128:(ct + 1) * 128], ident[:]),
                   r=[srcB, B("ident")], w=[bkB], inc=(j == 3))
            for j in range(4):
                ct = hf * 4 + j
                op("act", lambda h, ct=ct, j=j: h.activation(out=dstT[:, ct, tb * 128:(tb + 1) * 128], in_=bk[:, j * 128:(j + 1) * 128],
                                                             func=AF.Identity, scale=scale_ap[:, ct, v:v + 1], bias=bias_ap[:, ct, v:v + 1]),
                   r=[bkB, B("mod"), B("mod2")], w=[dstB_fn(tb // 4)])

    def ld(dst, src, bname, q="sp", **kw):
        return dma(q, dst, src, w=[B(bname)], **kw)

    ld(ident[:], c_ident, "ident")
    ld(maskf[:], c_maskf, "maskf")
    ld(maskb[:], c_maskb, "maskb")
    op("dve", lambda h: h.memset(neghalf[:], -0.5), w=[B("neghalf")])
    epsT = sb("epsT", [128, 1], F32)
    op("dve", lambda h: h.memset(epsT[:], EPS), w=[B("epsT")])

    NXS = 0
    xsc = ExitStack()
    for _ in range(NXS):
        wsl.append(sb("wslx", [128, 4096], BF16, xsc))
    wp.free = list(range(NW + NXS))

    def half_plan(hf_):
        P = []
        P += [((hf_, "ua", wt), w_in, 0, 1024, 512 * wt, 512) for wt in range(2)]
        P += [((hf_, "ub"), w_in, 0, 1024, 1024, 512), ((hf_, "wpool"), w_pool.rearrange("g i o -> (g i) o"), 0, 512, 0, 128)]
        P += [((hf_, "glu", wt), w_glu, 0, 1024, 512 * wt, 512) for wt in range(2)]
        for wt in range(2):
            P += [((hf_, "pb", wt), w_pb, 0, 512, 512 * wt, 512), ((hf_, "pa", wt), w_pa, 0, 1024, 512 * wt, 512),
                  ((hf_, "ga", wt), w_in, 0, 1024, 1536 + 512 * wt, 512), ((hf_, "gb", wt), w_in, 0, 1024, 2560 + 512 * wt, 512)]
        P += [((hf_, "out", wt), w_out, 0, 1024, 512 * wt, 512) for wt in range(2)]
        P += [((hf_, "w1", wt), w1, 0, 1024, 512 * wt, 512) for wt in range(8)]
        P += [((hf_, "w2", ps_, m_), w2, 0, 4096, 128 * m_, 128) for ps_ in range(2) for m_ in range(8)]
        return P

    with ExitStack() as sc:
        def t64(name):
            return sb(name, [128, 64], F32, sc)
        lre, lim, ldt = t64("lre"), t64("lim"), t64("ldt")
        Bre = sb("Bre", [128, 64, 16], F32, sc)
        Bim = sb("Bim", [128, 64, 16], F32, sc)
        Cre = sb("Cre", [128, 64, 16], F32, sc)
        Cim = sb("Cim", [128, 64, 16], F32, sc)
        dsum = t64("dsum")
        Ln = sb("Ln", [64, 2, 2, 64], F32, sc)
        Sn = sb("Sn", [64, 2, 2, 64], F32, sc)
        Cn = sb("Cn", [128, 2, 8, 2, 64], F32, sc)
        Dn = sb("Dn", [64, 2, 16], F32, sc)
        Dn8 = sb("Dn8", [64, 8, 16], F32, sc)
        dma("sp", Ln[:, 0], lam_re.rearrange("d g p -> g d p"), w=[B("Ln")])
        dma("sp", Ln[:, 1], lam_im.rearrange("d g p -> g d p"), w=[B("Ln")])
        for d in range(2):
            sl = slice(64 * d, 64 * d + 64)
            dma("sp", ldt[sl, :], log_dt[d:d + 1, :].partition_broadcast(64).rearrange("p o f -> p (o f)"), w=[B("ldt")])
        dma("sp", Dn[:], s5_d.rearrange("d (g h) -> g d h", h=16), w=[B("Dn")], allow_slow_non_contiguous=True)
        for r_ in range(2):
            dma("sp", Sn[:, r_], st0[:, r_].rearrange("d g p -> g d p"), w=[B("Sn")])
        for d in range(2):
            sl = slice(64 * d, 64 * d + 64)
            dma("sp", Bre[sl], b_re[d].rearrange("g p h -> p g h"), w=[B("Bre")], allow_slow_non_contiguous=True)
            dma("sp", Bim[sl], b_im[d].rearrange("g p h -> p g h"), w=[B("Bim")], allow_slow_non_contiguous=True)
        for ri_, src_ in enumerate((c_re, c_im)):
            for d in range(2):
                tokC = dma("sp", Cn[:, ri_, :, d, :], src_[d].rearrange("(gh gl) h p -> (gl h) gh p", gl=8), w=[B("Cn")])

        def tr_small(dst, src_ap, nrow, srcB, dstB):
            bk, bkB = kb.bank()
            op("pe", lambda h: h.transpose(bk[:, 0:nrow], src_ap, ident[0:nrow, 0:nrow]), r=[B(srcB), B("ident")], w=[bkB])
            op("act", lambda h: h.activation(out=dst, in_=bk[:, 0:nrow], func=AF.Identity), r=[bkB], w=[B(dstB)])

        tr_small(lre[:], Ln[:, 0].rearrange("g d p -> g (d p)"), 64, "Ln", "lre")
        tr_small(lim[:], Ln[:, 1].rearrange("g d p -> g (d p)"), 64, "Ln", "lim")
        tr_small(s0t[:, 0, :], Sn[:, 0].rearrange("g d p -> g (d p)"), 64, "Sn", "s0t")
        tr_small(s0t[:, 1, :], Sn[:, 1].rearrange("g d p -> g (d p)"), 64, "Sn", "s0t")
        op("dve", lambda h: h.tensor_tensor(out=Dn[:, 0, :], in0=Dn[:, 0, :], in1=Dn[:, 1, :], op=ALU.add), r=[B("Dn")], w=[B("Dn")])
        op("dve", lambda h: h.tensor_copy(out=Dn8[:], in_=Dn[:, 0, :].unsqueeze(1).broadcast_to([64, 8, 16])), r=[B("Dn")], w=[B("Dn8")])
        tr_small(dsum[:], Dn8[:].rearrange("g a b -> g (a b)"), 64, "Dn8", "dsum")
        for ri_, dstC in enumerate((Cre, Cim)):
            for gh4 in range(2):
                bk, bkB = kb.bank()
                for j in range(4):
                    gh = gh4 * 4 + j
                    op("pe", lambda h, j=j, gh=gh, ri_=ri_: h.transpose(bk[:, j * 128:(j + 1) * 128], Cn[:, ri_, gh].rearrange("p d x -> p (d x)"), ident[:]),
                       r=[B("Cn"), B("ident")], w=[bkB], inc=(j == 3))
                op("act", lambda h, gh4=gh4, dstC=dstC: h.activation(out=dstC[:, 32 * gh4:32 * gh4 + 32, :].rearrange("p a b -> p (a b)"), in_=bk[:], func=AF.Identity),
                   r=[bkB], w=[B("Cre"), B("Cim")])

        ld(pc[:].rearrange("p a b -> p (a b)"), c_pc, "pc", q="pool")
        ld(identb[:], c_ident, "identb", q="pool")
        ld(bandP[:].rearrange("p a b c -> p (a b c)"), c_bandp, "band", q="pool")
        ld(bandS[:].rearrange("p a b -> p (a b)"), c_bands, "band", q="pool")
        ld(bub[:], b_in[1024:1536].partition_broadcast(128), "bub")
        for i, src in enumerate((ln1_g, ln1_b, ln2_g, ln2_b)):
            ld(lnp[:, i, :], src.partition_broadcast(128).rearrange("p o f -> p (o f)"), "lnp")
        ld(binT[:], b_in.rearrange("(j p) -> p j", p=128), "binT", allow_slow_non_contiguous=True)
        ld(badaT[:], b_ada.rearrange("(j p) -> p j", p=128), "badaT", allow_slow_non_contiguous=True)
        ld(bgluT[:], b_glu.rearrange("(j p) -> p j", p=128), "bgluT", allow_slow_non_contiguous=True)
        ld(boutT[:], b_out.rearrange("(j p) -> p j", p=128), "boutT", allow_slow_non_contiguous=True)
        ld(b1T[:], b1.rearrange("(j p) -> p j", p=128), "b1T", allow_slow_non_contiguous=True)
        ld(b2T[:], b2.rearrange("(j p) -> p j", p=128), "b2T", allow_slow_non_contiguous=True)
        ld(pscT[:], pool_scale.rearrange("(j p) -> p j", p=128), "pscT", allow_slow_non_contiguous=True)
        for tb_ in range(DEPTH):
            dma("sp", xsl[tb_][:], xin[tb_ * 128:(tb_ + 1) * 128, :], w=[B("xs", tb_)])
        kb._wait(kb.E["pool"], tokC)
        wp.plan([(("ada", wt), w_ada, 0, 1024, 512 * wt, 512) for wt in range(5)])
        cnt = {"i": 0}

        def tt(out, a, b_, o, r, w, eng=None):
            en = eng or ("dve" if cnt["i"] % 2 == 0 else "pool")
            cnt["i"] += 1
            return op(en, lambda h: h.tensor_tensor(out=out, in0=a, in1=b_, op=o), r=[B(x) for x in r], w=[B(x) for x in w])

        def tsc(out, a, s1, s2, o0, o1, r, w):
            return op("dve", lambda h: h.tensor_scalar(out=out, in0=a, scalar1=s1, scalar2=s2, op0=o0, op1=o1) if o1 is not None else
                      h.tensor_scalar(out=out, in0=a, scalar1=s1, scalar2=None, op0=o0), r=[B(x) for x in r], w=[B(x) for x in w])

        dtt, a_, th, ea = t64("dtt"), t64("a_"), t64("th"), t64("ea")
        op("act", lambda h: h.activation(out=dtt[:], in_=ldt[:], func=AF.Exp), r=[B("ldt")], w=[B("dtt")])
        tt(a_[:], lre[:], dtt[:], ALU.mult, ["lre", "dtt"], ["a_"], "dve")
        tt(th[:], lim[:], dtt[:], ALU.mult, ["lim", "dtt"], ["th"], "dve")
        op("act", lambda h: h.activation(out=ea[:], in_=a_[:], func=AF.Exp), r=[B("a_")], w=[B("ea")])

        def sin_of(name, shift):
            x, k, r_ = t64(name + "x"), t64(name + "k"), t64(name + "r")
            res = t64(name)
            tsc(x[:], th[:], shift, None, ALU.add, None, ["th"], [name + "x"])
            tsc(k[:], x[:], PI, None, ALU.is_gt, None, [name + "x"], [name + "k"])
            for m in range(1, 6):
                op("dve", lambda h, m=m: h.scalar_tensor_tensor(out=k[:], in0=x[:], scalar=(2 * m + 1) * PI, in1=k[:], op0=ALU.is_gt, op1=ALU.add),
                   r=[B(name + "x"), B(name + "k")], w=[B(name + "k")])
            c1 = float(np.float32(2.0 * PI))
            c2 = 2.0 * PI - c1
            op("dve", lambda h: h.scalar_tensor_tensor(out=r_[:], in0=k[:], scalar=-c1, in1=x[:], op0=ALU.mult, op1=ALU.add),
               r=[B(name + "x"), B(name + "k")], w=[B(name + "r")])
            op("dve", lambda h: h.scalar_tensor_tensor(out=r_[:], in0=k[:], scalar=-c2, in1=r_[:], op0=ALU.mult, op1=ALU.add),
               r=[B(name + "r"), B(name + "k")], w=[B(name + "r")])
            tsc(r_[:], r_[:], -3.1415925, 3.1415925, ALU.max, ALU.min, [name + "r"], [name + "r"])
            op("act", lambda h: h.activation(out=res[:], in_=r_[:], func=AF.Sin), r=[B(name + "r")], w=[B(name)])
            return res

        sn = sin_of("sn", 0.0)
        cs = sin_of("cs", PI / 2)
        PWr = sb("PWr", [128, 9, 64], F32, sc)
        PWi = sb("PWi", [128, 9, 64], F32, sc)
        op("dve", lambda h: h.memset(PWr[:, 0, :], 1.0), w=[B("PW")])
        op("dve", lambda h: h.memset(PWi[:, 0, :], 0.0), w=[B("PW")])
        tt(PWr[:, 1, :], ea[:], cs[:], ALU.mult, ["ea", "cs"], ["PW"], "dve")
        tt(PWi[:, 1, :], ea[:], sn[:], ALU.mult, ["ea", "sn"], ["PW"], "dve")
        t1, t2 = t64("t1"), t64("t2")

        def cmul(outr, outi, ar, ai, br, bi, rn, wn):
            tt(t1[:], ar, br, ALU.mult, rn, ["t1"], "dve")
            tt(t2[:], ai, bi, ALU.mult, rn, ["t2"], "dve")
            tt(outr, t1[:], t2[:], ALU.subtract, ["t1", "t2"], wn, "dve")
            tt(t1[:], ar, bi, ALU.mult, rn, ["t1"], "dve")
            tt(t2[:], ai, br, ALU.mult, rn, ["t2"], "dve")
            tt(outi, t1[:], t2[:], ALU.add, ["t1", "t2"], wn, "dve")

        for k in range(1, 8):
            cmul(PWr[:, k + 1, :], PWi[:, k + 1, :], PWr[:, k, :], PWi[:, k, :], PWr[:, 1, :], PWi[:, 1, :], ["PW"], ["PW"])
        i8r, i8i, m2 = t64("i8r"), t64("i8i"), t64("m2")
        tt(t1[:], PWr[:, 8, :], PWr[:, 8, :], ALU.mult, ["PW"], ["t1"], "dve")
        tt(t2[:], PWi[:, 8, :], PWi[:, 8, :], ALU.mult, ["PW"], ["t2"], "dve")
        tt(m2[:], t1[:], t2[:], ALU.add, ["t1", "t2"], ["m2"], "dve")
        op("dve", lambda h: h.reciprocal(out=m2[:], in_=m2[:]), r=[B("m2")], w=[B("m2")])
        tt(i8r[:], PWr[:, 8, :], m2[:], ALU.mult, ["PW", "m2"], ["i8"], "dve")
        op("dve", lambda h: h.scalar_tensor_tensor(out=i8i[:], in0=PWi[:, 8, :], scalar=-1.0, in1=m2[:], op0=ALU.mult, op1=ALU.mult),
           r=[B("PW"), B("m2")], w=[B("i8")])
        cr, ci, nr = t64("cr"), t64("ci"), t64("nr")
        tsc(nr[:], PWr[:, 1, :], -1.0, None, ALU.add, None, ["PW"], ["nr"])
        tt(t1[:], lre[:], lre[:], ALU.mult, ["lre"], ["t1"], "dve")
        tt(t2[:], lim[:], lim[:], ALU.mult, ["lim"], ["t2"], "dve")
        tt(m2[:], t1[:], t2[:], ALU.add, ["t1", "t2"], ["m2"], "dve")
        op("dve", lambda h: h.reciprocal(out=m2[:], in_=m2[:]), r=[B("m2")], w=[B("m2")])
        tt(t1[:], nr[:], lre[:], ALU.mult, ["nr", "lre"], ["t1"], "dve")
        tt(t2[:], PWi[:, 1, :], lim[:], ALU.mult, ["PW", "lim"], ["t2"], "dve")
        tt(cr[:], t1[:], t2[:], ALU.add, ["t1", "t2"], ["cr"], "dve")
        tt(cr[:], cr[:], m2[:], ALU.mult, ["cr", "m2"], ["cr"], "dve")
        tt(t1[:], PWi[:, 1, :], lre[:], ALU.mult, ["PW", "lre"], ["t1"], "dve")
        tt(t2[:], nr[:], lim[:], ALU.mult, ["nr", "lim"], ["t2"], "dve")
        tt(ci[:], t1[:], t2[:], ALU.subtract, ["t1", "t2"], ["ci"], "dve")
        tt(ci[:], ci[:], m2[:], ALU.mult, ["ci", "m2"], ["ci"], "dve")
        op("dve", lambda h: h.tensor_copy(out=A4[:], in_=PWr[:, 8, :].unsqueeze(1).broadcast_to([128, 4, 64])), r=[B("PW")], w=[B("A4")])
        op("dve", lambda h: h.tensor_copy(out=B4[:], in_=PWi[:, 8, :].unsqueeze(1).broadcast_to([128, 4, 64])), r=[B("PW")], w=[B("A4")])
        pZr = sb("pZr", [128, 64, 8], F32, sc)
        pZi = sb("pZi", [128, 64, 8], F32, sc)
        pYr = sb("pYr", [128, 64, 8], F32, sc)
        pYi = sb("pYi", [128, 64, 8], F32, sc)
        pYn = sb("pYn", [128, 64, 8], F32, sc)
        lo, hi = slice(0, 64), slice(64, 128)
        for i in range(8):
            for (dst, src) in ((pZr, PWr), (pZi, PWi)):
                op("dve", lambda h, dst=dst, src=src, i=i: h.tensor_copy(out=dst[lo, :, i], in_=src[lo, 7 - i, :]), r=[B("PW")], w=[B("pZ")])
                op("dve", lambda h, dst=dst, src=src, i=i: h.tensor_copy(out=dst[hi, :, i], in_=src[hi, i, :]), r=[B("PW")], w=[B("pZ")])
            for (dst, src) in ((pYr, PWr), (pYi, PWi)):
                op("dve", lambda h, dst=dst, src=src, i=i: h.tensor_copy(out=dst[lo, :, i], in_=src[lo, i + 1, :]), r=[B("PW")], w=[B("pY")])
                op("dve", lambda h, dst=dst, src=src, i=i: h.tensor_copy(out=dst[hi, :, i], in_=src[hi, 8 - i, :]), r=[B("PW")], w=[B("pY")])
        tsc(pYn[:], pYi[:], -1.0, None, ALU.mult, None, ["pY"], ["pYn"])
        Bbr = sb("Bbr", [128, 64, 16], F32, sc)
        Bbi = sb("Bbi", [128, 64, 16], F32, sc)
        tb1 = sb("tb1", [128, 64, 16], F32, sc)
        tb2 = sb("tb2", [128, 64, 16], F32, sc)
        crb = cr[:].unsqueeze(2).broadcast_to([128, 64, 16])
        cib = ci[:].unsqueeze(2).broadcast_to([128, 64, 16])
        tt(tb1[:], Bre[:], crb, ALU.mult, ["Bre", "cr"], ["tb1"], "dve")
        tt(tb2[:], Bim[:], cib, ALU.mult, ["Bim", "ci"], ["tb2"], "dve")
        tt(Bbr[:], tb1[:], tb2[:], ALU.subtract, ["tb1", "tb2"], ["Bb"], "dve")
        tt(tb1[:], Bre[:], cib, ALU.mult, ["Bre", "ci"], ["tb1"], "dve")
        tt(tb2[:], Bim[:], crb, ALU.mult, ["Bim", "cr"], ["tb2"], "dve")
        tt(Bbi[:], tb1[:], tb2[:], ALU.add, ["tb1", "tb2"], ["Bb"], "dve")

        GB = 8
        NE = GB * 128
        WZr = sb("WZr", [128, GB, 8, 16], F32, sc)
        WZi = sb("WZi", [128, GB, 8, 16], F32, sc)
        WYr = sb("WYr", [128, GB, 8, 16], F32, sc)
        WYn = sb("WYn", [128, GB, 8, 16], F32, sc)
        X1 = sb("X1", [128, GB, 8, 16], F32, sc)
        X2 = sb("X2", [128, GB, 8, 16], F32, sc)
        WZTb = sb("WZTb", [128, 2, GB, 128], BF16, sc)
        WYb = sb("WYb", [128, 2, GB, 128], BF16, sc)
        W0f = sb("W0f", [128, GB, 128], F32, sc)
        W0b = sb("W0b", [128, GB, 128], BF16, sc)
        shp = [128, GB, 8, 16]
        for blk in range(64 // GB):
            gs = slice(GB * blk, GB * blk + GB)
            Bbrb = Bbr[:, gs, :].unsqueeze(2).broadcast_to(shp)
            Bbib = Bbi[:, gs, :].unsqueeze(2).broadcast_to(shp)
            Creb = Cre[:, gs, :].unsqueeze(2).broadcast_to(shp)
            Cimb = Cim[:, gs, :].unsqueeze(2).broadcast_to(shp)
            pzr = pZr[:, gs, :].unsqueeze(3).broadcast_to(shp)
            pzi = pZi[:, gs, :].unsqueeze(3).broadcast_to(shp)
            pyr = pYr[:, gs, :].unsqueeze(3).broadcast_to(shp)
            pyi = pYi[:, gs, :].unsqueeze(3).broadcast_to(shp)
            pyn = pYn[:, gs, :].unsqueeze(3).broadcast_to(shp)
            i8rb = i8r[:, gs].unsqueeze(2).unsqueeze(3).broadcast_to(shp)
            i8ib = i8i[:, gs].unsqueeze(2).unsqueeze(3).broadcast_to(shp)
            Y1 = tb1[:].rearrange("p (g i) h -> p g i h", g=GB)
            Z8r = W0f[:].rearrange("p g (a b) -> p g a b", a=8)
            tt(WZr[:], Bbrb, pzr, ALU.mult, ["Bb", "pZ"], ["WZr"], "dve")
            tt(X1[:], Bbib, pzi, ALU.mult, ["Bb", "pZ"], ["X1"], "dve")
            tt(WZi[:], Bbrb, pzi, ALU.mult, ["Bb", "pZ"], ["WZi"], "dve")
            tt(X2[:], Bbib, pzr, ALU.mult, ["Bb", "pZ"], ["X2"], "dve")
            tt(WZr[:], WZr[:], X1[:], ALU.subtract, ["WZr", "X1"], ["WZr"], "dve")
            tt(WZi[:], WZi[:], X2[:], ALU.add, ["WZi", "X2"], ["WZi"], "dve")
            for ri, src, sn_ in ((0, WZr, "WZr"), (1, WZi, "WZi")):
                for g4 in range(GB // 4):
                    bk, bkB = kb.bank()
                    for gg in range(4):
                        g = g4 * 4 + gg
                        op("pe", lambda h, g=g, gg=gg, src=src: h.transpose(bk[:, gg * 128:(gg + 1) * 128], src[:, g].rearrange("p a b -> p (a b)"), ident[:]),
                           r=[B(sn_), B("ident")], w=[bkB], inc=(gg == 3))
                    op("act", lambda h, ri=ri, g4=g4: h.activation(out=WZTb[:, ri, g4 * 4:g4 * 4 + 4, :].rearrange("p a b -> p (a b)"), in_=bk[:], func=AF.Identity),
                       r=[bkB], w=[B("WZTb")])
            dma("sp", wzt_d[:, :, gs, :], WZTb[:], r=[B("WZTb")], w=[B("wzt_d")])
            tt(Z8r, WZr[:], i8rb, ALU.mult, ["WZr", "i8"], ["W0f"], "dve")
            tt(X1[:], WZi[:], i8ib, ALU.mult, ["WZi", "i8"], ["X1"], "dve")
            tt(Y1, WZr[:], i8ib, ALU.mult, ["WZr", "i8"], ["tb1"], "dve")
            tt(X2[:], WZi[:], i8rb, ALU.mult, ["WZi", "i8"], ["X2"], "dve")
            tt(Z8r, Z8r, X1[:], ALU.subtract, ["W0f", "X1"], ["W0f"], "dve")
            tt(Y1, Y1, X2[:], ALU.add, ["tb1", "X2"], ["tb1"], "dve")
            tt(WYr[:], Creb, pyr, ALU.mult, ["Cre", "pY"], ["WYr"], "dve")
            tt(X1[:], Cimb, pyi, ALU.mult, ["Cim", "pY"], ["X1"], "dve")
            tt(WYn[:], Creb, pyn, ALU.mult, ["Cre", "pYn"], ["WYn"], "dve")
            tt(X2[:], Cimb, pyr, ALU.mult, ["Cim", "pY"], ["X2"], "dve")
            tt(WYr[:], WYr[:], X1[:], ALU.subtract, ["WYr", "X1"], ["WYr"], "dve")
            tt(WYn[:], WYn[:], X2[:], ALU.subtract, ["WYn", "X2"], ["WYn"], "dve")
            op("act", lambda h: h.activation(out=WYb[:, 0].rearrange("p g b -> p (g b)"), in_=WYr[:].rearrange("p g a b -> p (g a b)"), func=AF.Identity),
               r=[B("WYr")], w=[B("WYb")])
            op("act", lambda h: h.activation(out=WYb[:, 1].rearrange("p g b -> p (g b)"), in_=WYn[:].rearrange("p g a b -> p (g a b)"), func=AF.Identity),
               r=[B("WYn")], w=[B("WYb")])
            dma("sp", wyb_d[:, :, gs, :], WYb[:], r=[B("WYb")], w=[B("wyb_d")])
            for g4 in range(GB // 4):
                bkf, bkfB = kb.bank()
                bkb, bkbB = kb.bank()
                for gg in range(4):
                    g = g4 * 4 + gg
                    for d, (bk_, bB_) in enumerate(((bkf, bkfB), (bkb, bkbB))):
                        ps_ = slice(64 * d, 64 * d + 64)
                        op("pe", lambda h, g=g, gg=gg, bk_=bk_, ps_=ps_: h.matmul(bk_[:, gg * 128:(gg + 1) * 128], lhsT=W0f[ps_, g, :],
                                                                                  rhs=WYr[ps_, g].rearrange("p a b -> p (a b)"), start=True, stop=False),
                           r=[B("W0f"), B("WYr")], w=[bB_], inc=False)
                        op("pe", lambda h, g=g, gg=gg, bk_=bk_, ps_=ps_: h.matmul(bk_[:, gg * 128:(gg + 1) * 128], lhsT=tb1[ps_, GB * g:GB * g + 8, :].rearrange("p a b -> p (a b)"),
                                                                                  rhs=WYn[ps_, g].rearrange("p a b -> p (a b)"), start=False, stop=True),
                           r=[B("tb1"), B("WYn")], w=[bB_], inc=True)
                g4s = slice(4 * g4, 4 * g4 + 4)
                mfb = maskf[:].unsqueeze(1).broadcast_to([128, 4, 128])
                mbb = maskb[:].unsqueeze(1).broadcast_to([128, 4, 128])
                op("dve", lambda h, g4s=g4s, bkf=bkf: h.tensor_tensor(out=X2[:, g4s].rearrange("p g a b -> p g (a b)"), in0=bkf[:].rearrange("p (g c) -> p g c", g=4), in1=mfb, op=ALU.mult),
                   r=[bkfB, B("maskf")], w=[B("X2")])
                op("dve", lambda h, g4s=g4s, bkb=bkb: h.tensor_tensor(out=X1[:, g4s].rearrange("p g a b -> p g (a b)"), in0=bkb[:].rearrange("p (g c) -> p g c", g=4), in1=mbb, op=ALU.mult),
                   r=[bkbB, B("maskb")], w=[B("X1")])
            op("dve", lambda h: h.tensor_tensor(out=X2[:], in0=X2[:], in1=X1[:], op=ALU.add), r=[B("X2"), B("X1")], w=[B("X2")])
            op("dve", lambda h, gs=gs: h.tensor_tensor(out=X1[:].rearrange("p g a b -> p g (a b)"), in0=ident[:].unsqueeze(1).broadcast_to([128, GB, 128]),
                                                       in1=dsum[:, gs].unsqueeze(2).broadcast_to([128, GB, 128]), op=ALU.mult),
               r=[B("ident"), B("dsum"), B("X1")], w=[B("X1")])
            op("dve", lambda h: h.tensor_tensor(out=W0b[:], in0=X2[:].rearrange("p g a b -> p g (a b)"), in1=X1[:].rearrange("p g a b -> p g (a b)"), op=ALU.add),
               r=[B("X2"), B("X1")], w=[B("W0b")])
            dma("sp", w0b_d[:, gs, :], W0b[:], r=[B("W0b")], w=[B("w0b_d")])
            if blk == 0:
                dump("W0b", W0b[:], ["W0b"])
                dump("WYb", WYb[:], ["WYb"])
                dump("WZTb", WZTb[:], ["WZTb"])
        dump("PWr", PWr[:], ["PW"])
        dump("PWi", PWi[:], ["PW"])
        dump("cr", cr[:], ["cr"])
        dump("ci", ci[:], ["ci"])
        dump("dsum", dsum[:], ["dsum"])
        kb.barrier()
        cut(1)

    cTb = sb("cTb", [128, 8, 2], BF16)
    with ExitStack() as sc:
        cT = sb("cT", [128, 8, 2], F32, sc)
        for v in range(2):
            dma("sp", cT[:, :, v], cvec[v, :].rearrange("(k p) -> p k", p=128), w=[B("cT")], allow_slow_non_contiguous=True)
        op("act", lambda h: h.activation(out=cTb[:], in_=cT[:], func=AF.Silu), r=[B("cT")], w=[B("cTb")])
        kb.barrier()

    def mod_part(tiles, j0, nj, modB):
        bk, bkB = kb.bank()
        for wt in tiles:
            wv, wb = wp.get(("ada", wt))
            for mm in range(4):
                j = wt * 4 + mm
                c0 = 2 * (j - j0)
                for k in range(8):
                    op("pe", lambda h, k=k, mm=mm, c0=c0: h.matmul(bk[:, c0:c0 + 2], lhsT=wv[:, k, mm * 128:(mm + 1) * 128], rhs=cTb[:, k, :],
                                                                  start=(k == 0), stop=(k == 7)),
                       r=[wb, B("cTb")], w=[bkB], inc=(k == 7))
            wp.release(("ada", wt))
        return bk, bkB

    def mod_evac(bk, bkB, j0, nj, modB):
        op("dve", lambda h: h.tensor_tensor(out=modT[:, j0:j0 + nj, :], in0=bk[:, 0:2 * nj].rearrange("p (j v) -> p j v", v=2),
                                            in1=badaT[:, j0:j0 + nj].unsqueeze(2).broadcast_to([128, nj, 2]), op=ALU.add),
           r=[bkB, B("badaT")], w=[modB])

    bk_, bkB_ = mod_part(range(4), 0, 16, B("mod"))
    mod_evac(bk_, bkB_, 0, 16, B("mod"))
    op("dve", lambda h: h.tensor_scalar(out=mder[:, 0], in0=modT[:, 8:16, :], scalar1=1.0, scalar2=None, op0=ALU.add), r=[B("mod")], w=[B("mod")])

    def mod_part2_evac(bk, bkB):
        mod_evac(bk, bkB, 16, 32, B("mod2"))
        op("dve", lambda h: h.tensor_scalar(out=mder[:, 1], in0=modT[:, 32:40, :], scalar1=1.0, scalar2=None, op0=ALU.add), r=[B("mod2")], w=[B("mod2")])
        op("dve", lambda h: h.tensor_tensor(out=mder[:, 2], in0=modT[:, 16:24, :], in1=boutT[:].unsqueeze(2).broadcast_to([128, 8, 2]), op=ALU.mult),
           r=[B("mod2"), B("boutT")], w=[B("mod2")])
        op("dve", lambda h: h.tensor_tensor(out=mder[:, 3], in0=modT[:, 40:48, :], in1=b2T[:].unsqueeze(2).broadcast_to([128, 8, 2]), op=ALU.mult),
           r=[B("mod2"), B("b2T")], w=[B("mod2")])
    cut(0)
    wp.free = [i for i in wp.free if i < NW]
    del wsl[NW:]
    xsc.close()
    hp0 = half_plan(0)
    wp.plan(hp0[:2] + [(("ada", wt), w_ada, 0, 1024, 512 * wt, 512) for wt in range(5, 12)] + hp0[2:])
    sh1 = modT[:, 0:8, :]
    sc1p = mder[:, 0]
    gt1 = modT[:, 16:24, :]
    sh2 = modT[:, 24:32, :]
    sc2p = mder[:, 1]
    gt2 = modT[:, 40:48, :]
    gt1b = mder[:, 2]
    gt2b = mder[:, 3]

    hT = sb("hT", [128, 8, 1024], BF16)
    vT = sb("vT", [128, 8, 1024], BF16)
    h2T = vT
    for hf in range(2):
        v = hf
        row0 = 1024 * hf
        nseq = 4 if hf == 0 else 1
        CS = 128 // nseq
        hTB = lambda th: B("hT", th)
        def chainA(tb):
            par = tb % DEPTH
            xs, xB = xsl[par], B("xs", par)
            if not (hf == 0 and tb < DEPTH):
                dma("sp", xs[:], xin[row0 + tb * 128:row0 + (tb + 1) * 128, :], w=[xB])
            yield
            i = ln_part1(xs, xB)
            yield
            ln_part2(i, xs[:], xs[:], xB, xB)
            yield
            to_feature_major(xs, xB, hT, hTB, tb, sc1p, sh1, v)
        run_pipelined([chainA(tb) for tb in range(8)], DEPTH)
        if hf == 0:
            dump("hT", hT[:], [("hT", 0), ("hT", 1)])
            cut(2)

        with ExitStack() as s5sc:
            U = sb("U", [128, 64, 128], BF16, s5sc)
            ZS = sb("ZS", [128, 2, 128, 64], BF16, s5sc)
            s5w = [sb("s5w%d" % i, [128, 3, 8, 128], BF16, s5sc) for i in range(2)]
            with ExitStack() as uasc:
                uaT = sb("uaT", [128, 8, 1024], BF16, uasc)
                for wt in range(2):
                    wv, wb = wp.get((hf, "ua", wt))
                    for mm in range(4):
                        m = wt * 4 + mm
                        for th in range(2):
                            bk, bkB = kb.bank()
                            for k in range(8):
                                op("pe", lambda h, k=k, mm=mm, th=th: h.matmul(bk[:], lhsT=wv[:, k, mm * 128:(mm + 1) * 128], rhs=hT[:, k, th * 512:(th + 1) * 512],
                                                                              start=(k == 0), stop=(k == 7)),
                                   r=[wb, hTB(th)], w=[bkB], inc=(k == 7))
                            op("act", lambda h, m=m, th=th: h.activation(out=uaT[:, m, th * 512:(th + 1) * 512], in_=bk[:], func=AF.Identity, bias=binT[:, m:m + 1], scale=1.0),
                               r=[bkB, B("binT")], w=[B("uaT", m)])
                    wp.release((hf, "ua", wt))
                xrc = [sb("xrc%d" % i, [128, 1024], BF16, uasc) for i in range(2)]
                for ct in range(8):
                    bk, bkB = kb.bank()
                    bkh = bk[:].bitcast(BF16)
                    for i in range(8):
                        op("pe", lambda h, i=i, ct=ct: h.transpose(bkh[:, i * 128:(i + 1) * 128], uaT[:, ct, i::8], identb[:]),
                           r=[B("uaT", ct), B("identb")], w=[bkB], inc=(i == 7))
                    xr = xrc[ct % 2]
                    xrB = B("xrc", ct % 2)
                    op("dve", lambda h: h.tensor_copy(out=xr[:].rearrange("p (g i o) -> p g i o", g=8, i=8), in_=bkh.rearrange("p (i g o) -> p g i o", i=8, g=8)),
                       r=[bkB], w=[xrB])
                    bk2, bk2B = kb.bank()
                    bk2h = bk2[:].bitcast(BF16)
                    for gl in range(8):
                        op("pe", lambda h, gl=gl: h.transpose(bk2h[:, gl * 128:(gl + 1) * 128], xr[:, gl * 128:(gl + 1) * 128], identb[:]),
                           r=[xrB, B("identb")], w=[bk2B], inc=(gl == 7))
                    op("act", lambda h, ct=ct: h.activation(out=U[:, 8 * ct:8 * ct + 8, :].rearrange("p a b -> p (a b)"), in_=bk2h, func=AF.Identity),
                       r=[bk2B], w=[B("U")])
                if hf == 0:
                    dump("uaT", uaT[:], [("uaT", m_) for m_ in range(8)])
                    dump("U", U[:], ["U"])
                kb.barrier()
                if hf == 0:
                    cut(3)

            def rev(ap3):
                if NOREV:
                    return ap3
                if nseq == 1:
                    return ap3[:, ::-1]
                return ap3.rearrange("p (q c) -> p q c", q=nseq)[:, :, ::-1]

            if hf == 0:
                nsT = sb("nsT", [64, 8, 128], F32, s5sc)
                finS = [sb("finS%d" % ri, [128, 256], F32, s5sc) for ri in range(2)]
            for gb in range(8):
                sw = s5w[gb % 2]
                swB = B("s5w", gb % 2)
                dma("sp", sw[:, 0:2], wzt_d[:, :, 8 * gb:8 * gb + 8, :], r=[B("wzt_d")], w=[swB])
                for g4_ in range(2):
                    for ri in range(2):
                        bkF, bkFB = kb.bank()
                        for gg in range(4):
                            gl = 4 * g4_ + gg
                            g = 8 * gb + gl
                            op("pe", lambda h, ri=ri, gl=gl, g=g, gg=gg: h.matmul(bkF[:, gg::4], lhsT=sw[:, ri, gl, :], rhs=U[:, g, :], start=True, stop=True),
                               r=[swB, B("U")], w=[bkFB], inc=(gg == 3))
                        g0 = 8 * gb + 4 * g4_
                        en = "act" if ri == 0 else "dve"
                        if en == "act":
                            op("act", lambda h, g0=g0, ri=ri: h.activation(out=ZS[:, ri, :, g0:g0 + 4], in_=bkF[:].rearrange("p (n g) -> p n g", g=4), func=AF.Identity),
                               r=[bkFB], w=[B("ZSz"), B("ZSs")])
                        else:
                            op("dve", lambda h, g0=g0, ri=ri: h.tensor_copy(out=ZS[:, ri, :, g0:g0 + 4], in_=bkF[:].rearrange("p (n g) -> p n g", g=4)),
                               r=[bkFB], w=[B("ZSz"), B("ZSs")])

            if hf == 0:
                dump("Z", ZS[:], ["ZSz"])
                cut(4)
            with ExitStack() as rsc:
                W = 64 * nseq
                if hf == 0:
                    while kb.nbank % 8 < 4:
                        kb.nbank += 1
                    mbk, mbkB = mod_part(range(4, 12), 16, 32, B("mod2"))
                pst = [kb.ps[i] for i in range(4)]
                St = [[pst[a][:, ri * 256:ri * 256 + W] for ri in range(2)] for a in range(2)]
                rt = [pst[2][:, 0:W], None, pst[2][:, 256:256 + W], None, pst[3][:, 0:W], pst[3][:, 256:256 + W]]
                rts = [sb("rts%d" % i, [128, W], F32, rsc) for i in range(2)]
                rt[1], rt[3] = rts[0][:], rts[1][:]
                Aw = A4[:].rearrange("p q g -> p (q g)") if nseq == 4 else A4[:, 0, :]
                Bw = B4[:].rearrange("p q g -> p (q g)") if nseq == 4 else B4[:, 0, :]
                psB = [B("ps", i) for i in range(4)]
                if hf == 0:
                    op("dve", lambda h: h.memset(pst[0][:], 0.0), w=[psB[0], B("St", 0)])
                else:
                    op("dve", lambda h: h.memset(pst[0][:], 0.0), w=[psB[0], B("St", 0)])
                    op("dve", lambda h: h.tensor_copy(out=St[0][0], in_=s0t[:, 0, :]), r=[B("s0t")], w=[B("St", 0)])
                    op("dve", lambda h: h.tensor_copy(out=St[0][1], in_=s0t[:, 1, :]), r=[B("s0t")], w=[B("St", 0)])
                op("dve", lambda h: h.memset(pst[1][:], 0.0), w=[psB[1], B("St", 1)])
                op("dve", lambda h: h.memset(pst[2][:], 0.0), w=[psB[2], B("tA")])
                op("dve", lambda h: h.memset(pst[3][:], 0.0), w=[psB[3], B("tN", 0), B("tN", 1)])

                def zs_slot(ri, c):
                    if nseq == 4:
                        return ZS[:, ri].rearrange("p (q c) g -> p q c g", q=4)[:, :, c, :]
                    return ZS[:, ri, c, :]

                def st_view(t):
                    if nseq == 4:
                        return t.rearrange("p (q g) -> p q g", q=4)
                    return t

                tA = pst[2][:].rearrange("p (r x) -> p r x", r=2)[:, :, 0:W]
                tBs = sb("tBs", [128, 2, W], F32, rsc)
                tN = pst[3][:].rearrange("p (r x) -> p r x", r=2)[:, :, 0:W]
                Abc = Aw.unsqueeze(1).broadcast_to([128, 2, W])
                Bbc = Bw.unsqueeze(1).broadcast_to([128, 2, W])

                def st2(a_):
                    return pst[a_][:].rearrange("p (r x) -> p r x", r=2)[:, :, 0:W]

                def zs2(c, P_):
                    if nseq == 4:
                        return ZS[P_].rearrange("p r (q c) g -> p r q c g", q=4)[:, :, :, c, :]
                    return ZS[P_, :, c, :]

                def v2(ap, P_):
                    if nseq == 4:
                        return ap[P_].rearrange("p r (q g) -> p r q g", q=4)
                    return ap[P_]

                for c in range(CS):
                    cur, nxt = st2(c % 2), st2((c + 1) % 2)
                    cB, nB = B("St", c % 2), B("St", (c + 1) % 2)
                    op("dve", lambda h: h.tensor_tensor(out=tA, in0=cur, in1=Abc, op=ALU.mult), r=[cB, B("A4")], w=[B("tA")])
                    op("dve", lambda h: h.tensor_tensor(out=tBs[:], in0=cur, in1=Bbc, op=ALU.mult), r=[cB, B("A4")], w=[B("tBs")])
                    op("dve", lambda h: h.tensor_tensor(out=tN[:, 0, :], in0=tA[:, 0, :], in1=tBs[:, 1, :], op=ALU.subtract), r=[B("tA"), B("tBs")], w=[B("tN", 0)])
                    op("dve", lambda h: h.tensor_tensor(out=tN[:, 1, :], in0=tA[:, 1, :], in1=tBs[:, 0, :], op=ALU.add), r=[B("tA"), B("tBs")], w=[B("tN", 1)])
                    for d_ in range(2):
                        P_ = slice(64 * d_, 64 * d_ + 64)
                        cc = c if d_ == 0 else CS - 1 - c
                        zt = op("dve", lambda h, P_=P_, cc=cc: h.tensor_tensor(out=v2(nxt, P_), in0=v2(tN, P_), in1=zs2(cc, P_), op=ALU.add),
                                r=[B("tN", 0), B("tN", 1), B("ZSz")], w=[nB])
                        op("act", lambda h, P_=P_, cc=cc: h.activation(out=zs2(cc, P_), in_=v2(cur, P_), func=AF.Identity), r=[cB], w=[B("ZSs")], extra=[zt])
                if hf == 0:
                    mod_part2_evac(mbk, mbkB)
                fin = St[CS % 2]
                fB = B("St", CS % 2)
                fB1 = B("St", CS % 2)
                if hf == 0:
                    while kb.nbank % 8 < 4:
                        kb.nbank += 1
                    for ri in range(2):
                        op("act", lambda h, ri=ri: h.activation(out=finS[ri][:], in_=fin[ri], func=AF.Identity), r=[fB, fB1], w=[B("finS", ri)])
                        fv = finS[ri][:].rearrange("p (q g) -> p q g", q=4)
                        bk, bkB = kb.bank()
                        for q in range(4):
                            op("pe", lambda h, q=q, fv=fv: h.transpose(bk[0:64, q * 128:(q + 1) * 128], fv[:, q, :], ident[:]),
                               r=[B("finS", ri), B("ident")], w=[bkB], inc=(q == 3))
                        op("act", lambda h, ri=ri: h.activation(out=nsT[:, ri * 4:ri * 4 + 4, :].rearrange("p a b -> p (a b)"), in_=bk[0:64, :], func=AF.Identity),
                           r=[bkB], w=[B("nsT")])
                    for ri in range(2):
                        for q in range(4):
                            dma("sp", nsout[q, :, ri, :, :].rearrange("d g p -> g d p"), nsT[:, ri * 4 + q, :].rearrange("p (d x) -> p d x", d=2),
                                r=[B("nsT")], w=[B("nsout")])
                if hf == 0:
                    dump("S", ZS[:], ["ZSs"])
                kb.barrier(dma=False)
                if hf == 0:
                    cut(5)

            with ExitStack() as ysc:
                Yb = [sb("Yb%d" % i, [128, 8, 128], BF16, ysc) for i in range(2)]
                xrl = [sb("xr%d" % i, [128, 1024], BF16, ysc) for i in range(2)]
                for gb in range(8):
                    sw = s5w[gb % 2]
                    swB = B("s5w", gb % 2)
                    dma("sp", sw[:, 0], w0b_d[:, 8 * gb:8 * gb + 8, :], r=[B("w0b_d")], w=[swB])
                    dma("sp", sw[:, 1:3], wyb_d[:, :, 8 * gb:8 * gb + 8, :], r=[B("wyb_d")], w=[swB])
                    yb = Yb[gb % 2]
                    ybB = B("Yb", gb % 2)
                    for g4 in range(2):
                        bk, bkB = kb.bank()
                        for gg in range(4):
                            gl = 4 * g4 + gg
                            g = 8 * gb + gl
                            cs_ = slice(gg * 128, (gg + 1) * 128)
                            op("pe", lambda h, gl=gl, g=g, cs_=cs_: h.matmul(bk[:, cs_], lhsT=sw[:, 0, gl, :], rhs=U[:, g, :], start=True, stop=False),
                               r=[swB, B("U")], w=[bkB], inc=False)
                            op("pe", lambda h, gl=gl, g=g, cs_=cs_: h.matmul(bk[:, cs_], lhsT=sw[:, 1, gl, :], rhs=ZS[:, 0, :, g], start=False, stop=False),
                               r=[swB, B("ZSs")], w=[bkB], inc=False)
                            op("pe", lambda h, gl=gl, g=g, cs_=cs_: h.matmul(bk[:, cs_], lhsT=sw[:, 2, gl, :], rhs=ZS[:, 1, :, g], start=False, stop=True),
                               r=[swB, B("ZSs")], w=[bkB], inc=True)
                        op("dve", lambda h, g4=g4: h.tensor_copy(out=yb[:, 4 * g4:4 * g4 + 4, :].rearrange("p a b -> p (a b)"), in_=bk[:]),
                           r=[bkB], w=[ybB])
                    ct = gb
                    bk, bkB = kb.bank()
                    bkh = bk[:].bitcast(BF16)
                    for gl in range(8):
                        op("pe", lambda h, gl=gl: h.transpose(bkh[:, gl * 128:(gl + 1) * 128], yb[:, gl, :], identb[:]),
                           r=[ybB, B("identb")], w=[bkB], inc=(gl == 7))
                    xr = xrl[gb % 2]
                    xrB = B("xr", gb % 2)
                    op("dve", lambda h: h.tensor_copy(out=xr[:].rearrange("p (j g o) -> p j g o", j=8, g=8), in_=bkh.rearrange("p (g j o) -> p j g o", g=8, j=8)),
                       r=[bkB], w=[xrB])
                    bk2, bk2B = kb.bank()
                    bk2h = bk2[:].bitcast(BF16)
                    for j in range(8):
                        op("pe", lambda h, j=j: h.transpose(bk2h[:, j * 128:(j + 1) * 128], xr[:, j * 128:(j + 1) * 128], identb[:]),
                           r=[xrB, B("identb")], w=[bk2B], inc=(j == 7))
                    op("act", lambda h, ct=ct: h.activation(out=vT[:, ct, :].rearrange("p (n j) -> p n j", j=8), in_=bk2h.rearrange("p (j n) -> p n j", j=8), func=AF.Gelu_apprx_tanh),
                       r=[bk2B], w=[B("vT", 0), B("vT", 1)])
                if hf == 0:
                    dump("vT", vT[:], [("vT", 0), ("vT", 1)])
                kb.barrier()
                if hf == 0:
                    cut(6)

        with ExitStack() as m1:
            vgT = sb("vgT", [128, 8, 1024], BF16, m1)
            mgT = sb("mgT", [128, 8, 1024], BF16, m1)
            hsc = ExitStack()
            pbT = sb("pbT", [128, 4, 1024], BF16, hsc)
            sig = [sb("sig%d" % i, [128, 512], F32, hsc) for i in range(2)]
            tmpf = [sb("tmpf%d" % i, [128, 512], F32, hsc) for i in range(2)]
            wsl.append(sb("wslx", [128, 4096], BF16, hsc))
            wp.free.append(NW)
            wp.pump()
            with ExitStack() as psc:
                ubt = [sb("ubt", [128, 512], BF16, psc) for _ in range(8)]
                pql = [sb("pq", [128, 512], BF16, psc) for _ in range(2)]
                wv, wb = wp.get((hf, "ub"))
                wpv, wpb = wp.get((hf, "wpool"))
                for tb in range(8):
                    bk, bkB = kb.bank()
                    for k in range(8):
                        op("pe", lambda h, k=k, tb=tb: h.matmul(bk[:], lhsT=hT[:, k, tb * 128:(tb + 1) * 128], rhs=wv[:, k, :], start=(k == 0), stop=(k == 7)),
                           r=[wb, hTB(tb // 4)], w=[bkB], inc=(k == 7))
                    op("dve", lambda h, tb=tb: h.tensor_tensor(out=ubt[tb][:], in0=bk[:], in1=bub[:], op=ALU.add), r=[bkB, B("bub")], w=[B("ubt", tb)])
                wp.release((hf, "ub"))
                pit = 0
                for gi in range(4):
                    gs_ = slice(gi * 128, (gi + 1) * 128)
                    for th in range(2):
                        par = pit % 2
                        pit += 1
                        pq, pqB = pql[par], B("pq", par)
                        bk, bkB = kb.bank()
                        for t4 in range(4):
                            tb = th * 4 + t4
                            cs_ = slice(t4 * 128, (t4 + 1) * 128)
                            if hf == 1:
                                op("pe", lambda h, tb=tb, cs_=cs_: h.matmul(bk[:, cs_], lhsT=ubt[tb][:, gs_], rhs=bandS[:, gi, :], start=True, stop=True),
                                   r=[B("ubt", tb), B("band")], w=[bkB], inc=(t4 == 3))
                            else:
                                k0, k1, ot = (0, 1, tb + 1) if tb % 2 == 0 else (2, 3, tb - 1)
                                op("pe", lambda h, tb=tb, cs_=cs_, k0=k0: h.matmul(bk[:, cs_], lhsT=ubt[tb][:, gs_], rhs=bandP[:, gi, k0, :], start=True, stop=False),
                                   r=[B("ubt", tb), B("band")], w=[bkB], inc=False)
                                op("pe", lambda h, ot=ot, cs_=cs_, k1=k1: h.matmul(bk[:, cs_], lhsT=ubt[ot][:, gs_], rhs=bandP[:, gi, k1, :], start=False, stop=True),
                                   r=[B("ubt", ot), B("band")], w=[bkB], inc=True)
                        op("dve", lambda h: h.tensor_copy(out=pq[:], in_=bk[:]), r=[bkB], w=[pqB])
                        bk2, bk2B = kb.bank()
                        op("pe", lambda h, gi=gi: h.matmul(bk2[:], lhsT=wpv[:, gi, :], rhs=pq[:], start=True, stop=True), r=[wpb, pqB], w=[bk2B])
                        op("act", lambda h, gi=gi, th=th: h.activation(out=pbT[:, gi, th * 512:(th + 1) * 512], in_=bk2[:], func=AF.Identity, scale=pscT[:, gi:gi + 1]),
                           r=[bk2B, B("pscT")], w=[B("pbT", th)])
                wp.release((hf, "wpool"))
                kb.barrier()
            it = 0
            for wt in range(2):
                wv, wb = wp.get((hf, "glu", wt))
                for mm in range(4):
                    m = wt * 4 + mm
                    for th in range(2):
                        bk, bkB = kb.bank()
                        for k in range(8):
                            op("pe", lambda h, k=k, mm=mm, th=th: h.matmul(bk[:], lhsT=wv[:, k, mm * 128:(mm + 1) * 128], rhs=vT[:, k, th * 512:(th + 1) * 512],
                                                                          start=(k == 0), stop=(k == 7)),
                               r=[wb, B("vT", th)], w=[bkB], inc=(k == 7))
                        sg, sgB = sig[it % 2], B("sig", it % 2)
                        it += 1
                        op("act", lambda h, m=m: h.activation(out=sg[:], in_=bk[:], func=AF.Sigmoid, bias=bgluT[:, m:m + 1], scale=1.0),
                           r=[bkB, B("bgluT")], w=[sgB])
                        op("dve", lambda h, m=m, th=th: h.tensor_tensor(out=vgT[:, m, th * 512:(th + 1) * 512], in0=vT[:, m, th * 512:(th + 1) * 512], in1=sg[:], op=ALU.mult),
                           r=[sgB, B("vT", th)], w=[B("vgT", th)])
                wp.release((hf, "glu", wt))
            if hf == 0:
                dump("vgT", vgT[:], [("vgT", 0), ("vgT", 1)])
                cut(7)
            if hf == 0:
                dump("pbT", pbT[:], [("pbT", 0), ("pbT", 1)])
                cut(8)
            for wt in range(2):
                wpbv, wpbB = wp.get((hf, "pb", wt))
                wpa_v, wpaB = wp.get((hf, "pa", wt))
                wga_v, wgaB = wp.get((hf, "ga", wt))
                wgb_v, wgbB = wp.get((hf, "gb", wt))
                for mm in range(4):
                    m = wt * 4 + mm
                    ms = slice(mm * 128, (mm + 1) * 128)
                    for th in range(2):
                        ts_ = slice(th * 512, (th + 1) * 512)
                        bya, byaB = kb.bank()
                        bga, bgaB = kb.bank()
                        byb, bybB = kb.bank()
                        bgb, bgbB = kb.bank()
                        for k in range(8):
                            op("pe", lambda h, k=k: h.matmul(bya[:], lhsT=wpa_v[:, k, ms], rhs=vgT[:, k, ts_], start=(k == 0), stop=(k == 7)),
                               r=[wpaB, B("vgT", th)], w=[byaB], inc=(k == 7))
                        for k in range(8):
                            op("pe", lambda h, k=k: h.matmul(bga[:], lhsT=wga_v[:, k, ms], rhs=hT[:, k, ts_], start=(k == 0), stop=(k == 7)),
                               r=[wgaB, hTB(th)], w=[bgaB], inc=(k == 7))
                        for k in range(4):
                            op("pe", lambda h, k=k: h.matmul(byb[:], lhsT=wpbv[:, k, ms], rhs=pbT[:, k, ts_], start=(k == 0), stop=(k == 3)),
                               r=[wpbB, B("pbT", th)], w=[bybB], inc=(k == 3))
                        for k in range(8):
                            op("pe", lambda h, k=k: h.matmul(bgb[:], lhsT=wgb_v[:, k, ms], rhs=hT[:, k, ts_], start=(k == 0), stop=(k == 7)),
                               r=[wgbB, hTB(th)], w=[bgbB], inc=(k == 7))
                        op("act", lambda h: h.activation(out=sig[0][:], in_=bga[:], func=AF.Sigmoid, bias=binT[:, 12 + m:13 + m], scale=1.0),
                           r=[bgaB, B("binT")], w=[B("sig", 0)])
                        op("act", lambda h: h.activation(out=sig[1][:], in_=bgb[:], func=AF.Sigmoid, bias=binT[:, 20 + m:21 + m], scale=1.0),
                           r=[bgbB, B("binT")], w=[B("sig", 1)])
                        op("dve", lambda h: h.tensor_tensor(out=tmpf[0][:], in0=bya[:], in1=sig[0][:], op=ALU.mult), r=[byaB, B("sig", 0)], w=[B("tmpf", 0)])
                        op("dve", lambda h: h.tensor_tensor(out=tmpf[1][:], in0=byb[:], in1=sig[1][:], op=ALU.mult), r=[bybB, B("sig", 1)], w=[B("tmpf", 1)])
                        op("dve", lambda h: h.tensor_tensor(out=mgT[:, m, ts_], in0=tmpf[0][:], in1=tmpf[1][:], op=ALU.add),
                           r=[B("tmpf", 0), B("tmpf", 1)], w=[B("mgT", th)])
                for nm_ in ("pb", "pa", "ga", "gb"):
                    wp.release((hf, nm_, wt))
            if hf == 0:
                dump("mgT", mgT[:], [("mgT", 0), ("mgT", 1)])
                cut(9)
            kb.barrier()
            assert NW in wp.free
            wp.free.remove(NW)
            del wsl[NW:]
            hsc.close()
            with ExitStack() as jsc:
                tmT = [sb("tmT", [128, 8, 512], F32, jsc),
                       vgT[:].rearrange("p a b -> p (a b)").bitcast(F32).rearrange("p (a b) -> p a b", a=8)]
                rtl = [sb("rtile", [128, D], F32, jsc) for _ in range(DEPTH)]
                x1l = [sb("x1t", [128, D], F32, jsc) for _ in range(DEPTH)]
                wo = [wp.get((hf, "out", wt)) for wt in range(2)]
                for th in range(2):
                    for m in range(8):
                        wv, wb = wo[m // 4]
                        mm = m % 4
                        bk, bkB = kb.bank()
                        for k in range(8):
                            op("pe", lambda h, k=k, mm=mm, wv=wv: h.matmul(bk[:], lhsT=wv[:, k, mm * 128:(mm + 1) * 128], rhs=mgT[:, k, th * 512:(th + 1) * 512],
                                                                          start=(k == 0), stop=(k == 7)),
                               r=[wb, B("mgT", th)], w=[bkB], inc=(k == 7))
                        op("act", lambda h, m=m, th=th: h.activation(out=tmT[th][:, m, :], in_=bk[:], func=AF.Identity, scale=gt1[:, m, v:v + 1], bias=gt1b[:, m, v:v + 1]),
                           r=[bkB, B("mod2")], w=[B("tmT", th)])
                wp.release((hf, "out", 0))
                wp.release((hf, "out", 1))
                def chainJ(tb):
                    th, t4 = divmod(tb, 4)
                    par = tb % DEPTH
                    xs, xB = xsl[par], B("xs", par)
                    rt_, rB_ = rtl[par], B("rtile", par)
                    x1_, x1B_ = x1l[par], B("x1t", par)
                    dma("sp", xs[:], xin[row0 + tb * 128:row0 + (tb + 1) * 128, :], w=[xB])
                    for h2 in range(2):
                        bk, bkB = kb.bank()
                        for j in range(4):
                            ct = h2 * 4 + j
                            op("pe", lambda h, ct=ct, j=j: h.transpose(bk[:, j * 128:(j + 1) * 128], tmT[th][:, ct, t4 * 128:(t4 + 1) * 128], ident[:]),
                               r=[B("tmT", th), B("ident")], w=[bkB], inc=(j == 3))
                        op("dve", lambda h, h2=h2, bk=bk: h.scalar_tensor_tensor(out=rt_[:, h2 * 512:(h2 + 1) * 512], in0=xs[:, h2 * 512:(h2 + 1) * 512], scalar=ALPHA,
                                                                                 in1=bk[:], op0=ALU.mult, op1=ALU.add),
                           r=[bkB, xB], w=[rB_])
                    yield
                    i = ln_part1(rt_, rB_)
                    yield
                    ln_part2(i, x1_[:], rt_[:], rB_, x1B_)
                    yield
                    op("dve", lambda h: h.tensor_tensor(out=x1_[:], in0=x1_[:], in1=lnp[:, 0, :], op=ALU.mult), r=[x1B_, B("lnp")], w=[x1B_])
                    op("dve", lambda h: h.tensor_tensor(out=x1_[:], in0=x1_[:], in1=lnp[:, 1, :], op=ALU.add), r=[x1B_, B("lnp")], w=[x1B_])
                    dma("sp", x1s[row0 + tb * 128:row0 + (tb + 1) * 128, :], x1_[:], r=[x1B_], w=[B("x1s", hf, tb)])
                    i = ln_part1(x1_, x1B_)
                    yield
                    ln_part2(i, rt_[:], x1_[:], x1B_, rB_)
                    yield
                    to_feature_major(rt_, rB_, h2T, lambda th_: B("vT", th_), tb, sc2p, sh2, v)
                run_pipelined([chainJ(tb) for tb in range(8)], DEPTH)
                kb.barrier()
            if hf == 0:
                dump("h2T", h2T[:], [("vT", 0), ("vT", 1)])
            kb.barrier()
            if hf == 0:
                cut(10)

        with ExitStack() as m2:
            hid = sb("hid", [128, 32, 1024], BF16, m2)
            rl = [sb("rl%d" % i, [128, 512], F32, m2) for i in range(2)]
            fT = hT[:].rearrange("p a b -> p (a b)").bitcast(F32).rearrange("p (a b) -> p a b", a=8)
            rtl2 = [sb("rtile2", [128, D], F32, m2) for _ in range(DEPTH)]
            it = 0
            for wt in range(8):
                wv, wb = wp.get((hf, "w1", wt))
                for mm in range(4):
                    m = wt * 4 + mm
                    for th in range(2):
                        bk, bkB = kb.bank()
                        for k in range(8):
                            op("pe", lambda h, k=k, mm=mm, th=th: h.matmul(bk[:], lhsT=wv[:, k, mm * 128:(mm + 1) * 128], rhs=h2T[:, k, th * 512:(th + 1) * 512],
                                                                          start=(k == 0), stop=(k == 7)),
                               r=[wb, B("vT", th)], w=[bkB], inc=(k == 7))
                        r_, rB = rl[it % 2], B("rl", it % 2)
                        it += 1
                        op("act", lambda h, m=m: h.activation(out=r_[:], in_=bk[:], func=AF.Relu, bias=b1T[:, m:m + 1], scale=1.0),
                           r=[bkB, B("b1T")], w=[rB])
                        op("dve", lambda h, m=m, th=th: h.tensor_tensor(out=hid[:, m, th * 512:(th + 1) * 512], in0=r_[:], in1=r_[:], op=ALU.mult),
                           r=[rB], w=[B("hid", th)])
                wp.release((hf, "w1", wt))
            fTl = [fT, h2T[:].rearrange("p a b -> p (a b)").bitcast(F32).rearrange("p (a b) -> p a b", a=8)]
            def mlp2_group(th, m):
                wv, wb = wp.get((hf, "w2", th, m))
                bk, bkB = kb.bank()
                for k in range(32):
                    op("pe", lambda h, k=k: h.matmul(bk[:], lhsT=wv[:, k, :], rhs=hid[:, k, th * 512:(th + 1) * 512], start=(k == 0), stop=(k == 31)),
                       r=[wb, B("hid", th)], w=[bkB], inc=(k == 31))
                op("act", lambda h: h.activation(out=fTl[th][:, m, :], in_=bk[:], func=AF.Identity, scale=gt2[:, m, v:v + 1], bias=gt2b[:, m, v:v + 1]),
                   r=[bkB, B("mod2")], w=[B("fT", th), B("vT", 0), B("vT", 1), B("hT", 0), B("hT", 1)])
                wp.release((hf, "w2", th, m))
                if hf == 0 and th == 1 and m == 4:
                    wp.plan(half_plan(1))
            for m in range(8):
                mlp2_group(0, m)
            def chainF(tb):
                th, t4 = divmod(tb, 4)
                par = tb % DEPTH
                xs, xB = xsl[par], B("xs", par)
                rt_, rB_ = rtl2[par], B("rtile2", par)
                dma("sp", xs[:], x1s[row0 + tb * 128:row0 + (tb + 1) * 128, :], r=[B("x1s", hf, tb)], w=[xB])
                for h2 in range(2):
                    bk, bkB = kb.bank()
                    for j in range(4):
                        ct = h2 * 4 + j
                        op("pe", lambda h, ct=ct, j=j: h.transpose(bk[:, j * 128:(j + 1) * 128], fTl[th][:, ct, t4 * 128:(t4 + 1) * 128], ident[:]),
                           r=[B("fT", th), B("ident")], w=[bkB], inc=(j == 3))
                    op("dve", lambda h, h2=h2, bk=bk: h.scalar_tensor_tensor(out=rt_[:, h2 * 512:(h2 + 1) * 512], in0=xs[:, h2 * 512:(h2 + 1) * 512], scalar=ALPHA,
                                                                             in1=bk[:], op0=ALU.mult, op1=ALU.add),
                       r=[bkB, xB], w=[rB_])
                yield
                i = ln_part1(rt_, rB_)
                yield
                ln_part2(i, rt_[:], rt_[:], rB_, rB_)
                yield
                op("dve", lambda h: h.tensor_tensor(out=rt_[:], in0=rt_[:], in1=lnp[:, 2, :], op=ALU.mult), r=[rB_, B("lnp")], w=[rB_])
                op("dve", lambda h: h.tensor_tensor(out=rt_[:], in0=rt_[:], in1=lnp[:, 3, :], op=ALU.add), r=[rB_, B("lnp")], w=[rB_])
                dma("sp", yout[row0 + tb * 128:row0 + (tb + 1) * 128, :], rt_[:], r=[rB_], w=[B("yout")])
            todo = list(range(8))

            def fill():
                for _ in range(2):
                    if todo:
                        mlp2_group(1, todo.pop(0))
            run_pipelined([chainF(tb) for tb in range(4)], DEPTH, filler=fill)
            while todo:
                mlp2_group(1, todo.pop(0))
            run_pipelined([chainF(tb) for tb in range(4, 8)], DEPTH)
            kb.barrier()
            if hf == 0:
                cut(11)

    kb.barrier(final=True)


_CACHE = {}


def _consts():
    pcm = np.zeros((128, 8, 240), np.float32)
    for a in range(8):
        for h in range(16):
            pcm[16 * a + h, a, 112 + h] = 1.0
    i_of = np.arange(128) // 16
    maskf = (i_of[:, None] <= i_of[None, :]).astype(np.float32)
    maskb = (i_of[:, None] >= i_of[None, :]).astype(np.float32)
    ident = np.eye(128, dtype=np.float32)

    def pm(n):
        out = np.zeros((4, n, n), np.float64)
        t = np.arange(n)
        for gi, w in enumerate((2, 4, 8, 16)):
            lo = np.clip(t - w // 2, 0, n)
            hi = np.clip(t - w // 2 + w, 0, n)
            for to in range(n):
                out[gi, lo[to]:hi[to], to] = 1.0 / float(hi[to] - lo[to])
                out[gi, to, to] -= 1.0
        return out
    P256 = pm(256)
    bandp = np.zeros((128, 4, 4, 128), np.float32)
    for gi in range(4):
        bandp[:, gi, 0, :] = P256[gi, 0:128, 0:128]
        bandp[:, gi, 1, :] = P256[gi, 128:256, 0:128]
        bandp[:, gi, 2, :] = P256[gi, 128:256, 128:256]
        bandp[:, gi, 3, :] = P256[gi, 0:128, 128:256]
    P64 = pm(64)
    bands = np.zeros((128, 4, 128), np.float32)
    for gi in range(4):
        bands[0:64, gi, 0:64] = P64[gi]
        bands[64:128, gi, 64:128] = P64[gi]
    return {
        "c_pc": pcm.reshape(128, 8 * 240), "c_maskf": maskf, "c_maskb": maskb, "c_ident": ident,
        "c_bandp": bandp.reshape(128, -1), "c_bands": bands.reshape(128, -1),
    }


def kernel(x_prompt, x_sample, state_s5, c, c_ctx, w_ada, b_ada, w_in, b_in,
           s5_lam_re, s5_lam_im, s5_log_dt, s5_b_re, s5_b_im, s5_c_re, s5_c_im, s5_d,
           w_glu, b_glu, w_proj_a, w_pool, pool_scale, w_proj_b, w_out, b_out,
           ln1_g, ln1_b, w_mlp1, b_mlp1, w_mlp2, b_mlp2, ln2_g, ln2_b):
    f = lambda a: np.ascontiguousarray(np.asarray(a, dtype=np.float32))
    if "nc" not in _CACHE:
        _CACHE["nc"] = build_program()
    nc = _CACHE["nc"]
    xp = f(x_prompt)
    xs = f(x_sample)
    st = f(state_s5)
    cc = f(c)
    shared = {
        "w_ada": f(w_ada)[0], "b_ada": f(b_ada)[0], "w_in": f(w_in)[0], "b_in": f(b_in)[0],
        "lam_re": f(s5_lam_re)[0], "lam_im": f(s5_lam_im)[0], "log_dt": f(s5_log_dt)[0],
        "b_re": f(s5_b_re)[0], "b_im": f(s5_b_im)[0], "c_re": f(s5_c_re)[0], "c_im": f(s5_c_im)[0],
        "s5_d": f(s5_d)[0], "w_glu": f(w_glu)[0], "b_glu": f(b_glu)[0], "w_pa": f(w_proj_a)[0],
        "w_pool": f(w_pool)[0], "pool_scale": f(pool_scale)[0], "w_pb": f(w_proj_b)[0],
        "w_out": f(w_out)[0], "b_out": f(b_out)[0], "ln1_g": f(ln1_g), "ln1_b": f(ln1_b),
        "w1": f(w_mlp1)[0], "b1": f(b_mlp1)[0], "w2": f(w_mlp2)[0], "b2": f(b_mlp2)[0],
        "ln2_g": f(ln2_g), "ln2_b": f(ln2_b),
    }
    shared.update(_consts())
    in_maps = []
    for i in range(NCORES):
        m = dict(shared)
        m["xin"] = np.concatenate([xp[4 * i:4 * i + 4].reshape(1024, D), xs[i]], axis=0)
        m["st0"] = np.ascontiguousarray(st[i, 0])
        m["cvec"] = np.stack([f(c_ctx), cc[i]], axis=0)
        in_maps.append(m)
    res = run_bass_kernel_spmd(nc, in_maps, core_ids=list(range(NCORES)))
    yp = np.zeros((32, 256, D), np.float32)
    ys = np.zeros((8, 1024, D), np.float32)
    ns = np.zeros((32, 1, 2, 2, 64, 64), np.float32)
    for i in range(NCORES):
        r = res.results[i]
        yo = np.asarray(r["yout"])
        yp[4 * i:4 * i + 4] = yo[0:1024].reshape(4, 256, D)
        ys[i] = yo[1024:2048]
        ns[4 * i:4 * i + 4, 0] = np.asarray(r["nsout"])
    return yp, ys, ns
```

```python
import math
from contextlib import ExitStack
import numpy as np
import concourse.bass as bass
import concourse.mybir as mybir
from concourse.bass_utils import run_bass_kernel_spmd

F32 = mybir.dt.float32
BF16 = mybir.dt.bfloat16
AF = mybir.ActivationFunctionType
ALU = mybir.AluOpType
NCORES = 8
D = 1024
ALPHA = 2.0 ** 0.25
EPS = 1e-6
PI = math.pi


class Tok:
    __slots__ = ("sem", "val", "eng", "key")

    def __init__(self, sem, val, eng, key):
        self.sem, self.val, self.eng, self.key = sem, val, eng, key


class Buf:
    __slots__ = ("w", "r")

    def __init__(self):
        self.w = {}
        self.r = {}


class Eng:
    def __init__(self, name, h, sem):
        self.name, self.h, self.sem = name, h, sem
        self.n = 0
        self.waited = {}


class KB:
    def __init__(self, nc, es):
        self.nc = nc
        self.es = es
        self.E = {}
        for name, h in (("pe", nc.tensor), ("act", nc.scalar), ("dve", nc.vector),
                        ("pool", nc.gpsimd), ("sp", nc.sync)):
            self.E[name] = Eng(name, h, es.enter_context(nc.semaphore("sem_" + name)))
        self.rings = {}
        for q in ("sp", "pool"):
            self.rings[q] = [[es.enter_context(nc.semaphore("dq_%s_%d" % (q, i))), 0, None]
                             for i in range(12)]
        self.ring_pos = {"sp": 0, "pool": 0}
        self.bufs = {}
        self.nbank = 0

    def B(self, *key):
        b = self.bufs.get(key)
        if b is None:
            b = self.bufs[key] = Buf()
        return b

    def _wait(self, e, t):
        if t is None:
            return
        if t.eng is e and e.name in ("pe", "sp"):
            return
        if e.waited.get(t.key, 0) >= t.val:
            return
        e.h.wait_ge(t.sem, t.val)
        e.waited[t.key] = t.val

    def _deps(self, e, r, w, extra):
        for b in r:
            for t in b.w.values():
                self._wait(e, t)
        for b in w:
            for t in list(b.w.values()) + list(b.r.values()):
                self._wait(e, t)
        for t in extra:
            self._wait(e, t)

    def _post(self, tok, r, w):
        for b in r:
            b.r[tok.key] = tok
        for b in w:
            b.w[tok.key] = tok
            b.r = {}

    def op(self, en, fn, r=(), w=(), inc=True, extra=()):
        e = self.E[en]
        self._deps(e, r, w, extra)
        inst = fn(e.h)
        if inc:
            e.n += 1
            inst.then_inc(e.sem, 1)
            tok = Tok(e.sem, e.n, e, en)
        else:
            tok = Tok(e.sem, e.n + 1, e, en)
        self._post(tok, r, w)
        return tok

    def dma(self, q, out, in_, r=(), w=(), extra=(), **kw):
        e = self.E[q]
        self._deps(e, r, w, extra)
        ring = self.rings[q]
        i = self.ring_pos[q]
        self.ring_pos[q] = (i + 1) % len(ring)
        slot = ring[i]
        self._wait(e, slot[2])
        inst = e.h.dma_start(out=out, in_=in_, **kw)
        slot[1] += 16
        inst.then_inc(slot[0], 16)
        tok = Tok(slot[0], slot[1], None, "d%s%d" % (q, i))
        slot[2] = tok
        self._post(tok, r, w)
        return tok

    def barrier(self, final=False, dma=True):
        toks = []
        for e in self.E.values():
            if e.n > 0:
                toks.append(Tok(e.sem, e.n, e, e.name))
        for q, ring in self.rings.items():
            if not dma:
                continue
            if q == "pool" and not final:
                continue
            for slot in ring:
                if slot[2] is not None:
                    toks.append(slot[2])
        for e in self.E.values():
            for t in toks:
                if t.eng is e:
                    continue
                self._wait(e, t)

    def bank(self):
        i = self.nbank % 8
        self.nbank += 1
        return self.ps[i], self.B("ps", i)


class Stop(Exception):
    pass


LIMIT = None
NOREV = False
DUMPS = []


def build_program():
    nc = bass.Bass("TRN2", target_bir_lowering=False)
    es = ExitStack()
    try:
        _build(nc, es)
    except Stop:
        return nc
    es.close()
    return nc


def _build(nc, es):

    def din(name, shape):
        return nc.dram_tensor(name, list(shape), F32, kind="ExternalInput").ap()

    xin = din("xin", [2048, D])
    st0 = din("st0", [2, 2, 64, 64])
    cvec = din("cvec", [2, D])
    w_ada = din("w_ada", [D, 6 * D])
    b_ada = din("b_ada", [6 * D])
    w_in = din("w_in", [D, 3584])
    b_in = din("b_in", [3584])
    lam_re = din("lam_re", [2, 64, 64])
    lam_im = din("lam_im", [2, 64, 64])
    log_dt = din("log_dt", [2, 64])
    b_re = din("b_re", [2, 64, 64, 16])
    b_im = din("b_im", [2, 64, 64, 16])
    c_re = din("c_re", [2, 64, 16, 64])
    c_im = din("c_im", [2, 64, 16, 64])
    s5_d = din("s5_d", [2, D])
    w_glu = din("w_glu", [D, D])
    b_glu = din("b_glu", [D])
    w_pa = din("w_pa", [D, D])
    w_pool = din("w_pool", [4, 128, 128])
    pool_scale = din("pool_scale", [512])
    w_pb = din("w_pb", [512, D])
    w_out = din("w_out", [D, D])
    b_out = din("b_out", [D])
    ln1_g = din("ln1_g", [1, D])
    ln1_b = din("ln1_b", [1, D])
    w1 = din("w1", [D, 4 * D])
    b1 = din("b1", [4 * D])
    w2 = din("w2", [4 * D, D])
    b2 = din("b2", [D])
    ln2_g = din("ln2_g", [1, D])
    ln2_b = din("ln2_b", [1, D])
    c_pc = din("c_pc", [128, 8 * 240])
    c_maskf = din("c_maskf", [128, 128])
    c_maskb = din("c_maskb", [128, 128])
    c_ident = din("c_ident", [128, 128])
    c_bandp = din("c_bandp", [128, 4 * 4 * 128])
    c_bands = din("c_bands", [128, 4 * 128])

    yout = nc.dram_tensor("yout", [2048, D], F32, kind="ExternalOutput").ap()
    nsout = nc.dram_tensor("nsout", [4, 2, 2, 64, 64], F32, kind="ExternalOutput").ap()
    x1s = nc.dram_tensor("x1s", [2048, D], F32, kind="Internal").ap()
    wzt_d = nc.dram_tensor("wzt_d", [128, 2, 64, 128], BF16, kind="Internal").ap()
    wyb_d = nc.dram_tensor("wyb_d", [128, 2, 64, 128], BF16, kind="Internal").ap()
    w0b_d = nc.dram_tensor("w0b_d", [128, 64, 128], BF16, kind="Internal").ap()

    kb = KB(nc, es)
    op, dma, B = kb.op, kb.dma, kb.B

    def cut(n):
        if LIMIT == n:
            kb.barrier()
            raise Stop()

    def dump(name, ap, bname):
        if LIMIT is None:
            return
        shp = list(ap.shape)
        dt_ = nc.dram_tensor("dbg_" + name, shp, ap.dtype, kind="ExternalOutput").ap()
        dma("sp", dt_, ap, r=[(B(*x) if isinstance(x, tuple) else B(x)) for x in bname], w=[B("dbg_" + name)])
        DUMPS.append("dbg_" + name)

    used_names = {}

    def sb(name, shape, dt, scope=es):
        k = used_names.get(name, 0)
        used_names[name] = k + 1
        if k:
            name = "%s_%d" % (name, k)
        return scope.enter_context(nc.sbuf_tensor(name, list(shape), dt))

    kb.ps = [es.enter_context(nc.psum_tensor("psb%d" % i, [128, 512], F32)) for i in range(8)]

    NW = 5
    wsl = [sb("wsl%d" % i, [128, 4096], BF16) for i in range(NW)]
    wstate = {"i": 0}
    pc = sb("pc", [128, 8, 240], BF16)
    ident = sb("ident", [128, 128], F32)
    identb = sb("identb", [128, 128], BF16)
    maskf = sb("maskf", [128, 128], F32)
    maskb = sb("maskb", [128, 128], F32)
    bandP = sb("bandP", [128, 4, 4, 128], BF16)
    bandS = sb("bandS", [128, 4, 128], BF16)
    bub = sb("bub", [128, 512], F32)
    lnp = sb("lnp", [128, 4, D], F32)
    binT = sb("binT", [128, 28], F32)
    badaT = sb("badaT", [128, 48], F32)
    bgluT = sb("bgluT", [128, 8], F32)
    boutT = sb("boutT", [128, 8], F32)
    b1T = sb("b1T", [128, 32], F32)
    b2T = sb("b2T", [128, 8], F32)
    pscT = sb("pscT", [128, 4], F32)
    modT = sb("modT", [128, 48, 2], F32)
    mder = sb("mder", [128, 4, 8, 2], F32)
    A4 = sb("A4", [128, 4, 64], F32)
    B4 = sb("B4", [128, 4, 64], F32)
    s0t = sb("s0t", [128, 2, 64], F32)
    neghalf = sb("neghalf", [128, 1], F32)
    DEPTH = 4
    xsl = [sb("xsl%d" % i, [128, D], F32) for i in range(DEPTH)]
    NLN = 8
    lnst = [sb("lnst", [128, 2, 6], F32) for _ in range(NLN)]
    lnmv = [sb("lnmv", [128, 2], F32) for _ in range(NLN)]
    lnve = [sb("lnve", [128, 1], F32) for _ in range(NLN)]
    lnrs = [sb("lnrs", [128, 1], F32) for _ in range(NLN)]
    lnnb = [sb("lnnb", [128, 1], F32) for _ in range(NLN)]
    lnstate = {"i": 0}
    class WP:
        def __init__(self):
            self.free = list(range(NW))
            self.queue = []
            self.issued = {}
            self.hist = []

        def plan(self, specs):
            self.queue.extend(specs)
            self.pump()

        def pump(self):
            while self.free and self.queue:
                key = self.queue[0][0]
                merge = len(key) > 1 and key[1] in ("pb", "pa", "ga", "gb")
                cand = [i for i in self.free if i < NW or merge]
                if not cand:
                    break
                key, src, r0, nrows, c0, ncols = self.queue.pop(0)
                i = cand[0]
                self.free.remove(i)
                kc = nrows // 128
                view = wsl[i][:, 0:kc * ncols].rearrange("p (k c) -> p k c", k=kc)
                wb = B("wsl", i)
                if len(self.hist) >= 2:
                    kb._wait(kb.E["pool"], self.hist[-2])
                tok_ = dma("pool", view, src[r0:r0 + nrows, c0:c0 + ncols].rearrange("(k p) c -> p k c", p=128), w=[wb])
                self.hist.append(tok_)
                self.issued[key] = (view, wb, i)

        def get(self, key):
            self.pump()
            assert key in self.issued, ("weight tile not issued (no free slot)", key, self.free, [q[0] for q in self.queue[:3]])
            v_, b_, _ = self.issued[key]
            return v_, b_

        def release(self, key):
            _, _, i = self.issued.pop(key)
            self.free.append(i)
            self.pump()

    wp = WP()

    def ln_part1(src, srcB):
        i = lnstate["i"] % NLN
        lnstate["i"] += 1
        st, mv, ve = lnst[i], lnmv[i], lnve[i]
        stB, mvB, veB = B("lnst", i), B("lnmv", i), B("lnve", i)
        op("dve", lambda h: h.bn_stats(out=st[:, 0, :], in_=src[:, 0:512]), r=[srcB], w=[stB])
        op("dve", lambda h: h.bn_stats(out=st[:, 1, :], in_=src[:, 512:1024]), r=[srcB], w=[stB])
        op("dve", lambda h: h.bn_aggr(out=mv[:], in_=st[:].rearrange("p a b -> p (a b)")), r=[stB], w=[mvB])
        op("act", lambda h: h.activation(out=ve[:], in_=mv[:, 1:2], func=AF.Sqrt, bias=epsT[:, 0:1], scale=1.0), r=[mvB, B("epsT")], w=[veB])
        return i

    def ln_part2(i, dst, src, srcB, dstB):
        mv, ve, rs, nb = lnmv[i], lnve[i], lnrs[i], lnnb[i]
        mvB, veB, rsB, nbB = B("lnmv", i), B("lnve", i), B("lnrs", i), B("lnnb", i)
        op("dve", lambda h: h.reciprocal(out=rs[:], in_=ve[:]), r=[veB], w=[rsB])
        op("dve", lambda h: h.scalar_tensor_tensor(out=nb[:], in0=mv[:, 0:1], scalar=-1.0, in1=rs[:], op0=ALU.mult, op1=ALU.mult),
           r=[mvB, rsB], w=[nbB])
        op("act", lambda h: h.activation(out=dst, in_=src, func=AF.Identity, scale=rs[:, 0:1], bias=nb[:, 0:1]),
           r=[srcB, rsB, nbB], w=[dstB])

    def run_pipelined(gens, depth, filler=None):
        pending = list(gens)
        active = []
        while pending or active:
            while pending and len(active) < depth:
                active.append(pending.pop(0))
            for g in list(active):
                try:
                    next(g)
                except StopIteration:
                    active.remove(g)
            if filler is not None:
                filler()

    def to_feature_major(src, srcB, dstT, dstB_fn, tb, scale_ap, bias_ap, v):
        for hf in range(2):
            bk, bkB = kb.bank()
            for j in range(4):
                ct = hf * 4 + j
                op("pe", lambda h, ct=ct, j=j: h.transpose(bk[:, j * 128:(j + 1) * 128], src[:, ct * 128:(ct + 1) * 128], ident[:]),
                   r=[srcB, B("ident")], w=[bkB], inc=(j == 3))
            for j in range(4):
                ct = hf * 4 + j
                op("act", lambda h, ct=ct, j=j: h.activation(out=dstT[:, ct, tb * 128:(tb + 1) * 128], in_=bk[:, j * 128:(j + 1) * 128],
                                                             func=AF.Identity, scale=scale_ap[:, ct, v:v + 1], bias=bias_ap[:, ct, v:v + 1]),
                   r=[bkB, B("mod"), B("mod2")], w=[dstB_fn(tb // 4)])

    def ld(dst, src, bname, q="sp", **kw):
        return dma(q, dst, src, w=[B(bname)], **kw)

    ld(ident[:], c_ident, "ident")
    ld(maskf[:], c_maskf, "maskf")
    ld(maskb[:], c_maskb, "maskb")
    op("dve", lambda h: h.memset(neghalf[:], -0.5), w=[B("neghalf")])
    epsT = sb("epsT", [128, 1], F32)
    op("dve", lambda h: h.memset(epsT[:], EPS), w=[B("epsT")])

    NXS = 0
    xsc = ExitStack()
    for _ in range(NXS):
        wsl.append(sb("wslx", [128, 4096], BF16, xsc))
    wp.free = list(range(NW + NXS))

    def half_plan(hf_):
        P = []
        P += [((hf_, "ua", wt), w_in, 0, 1024, 512 * wt, 512) for wt in range(2)]
        P += [((hf_, "ub"), w_in, 0, 1024, 1024, 512), ((hf_, "wpool"), w_pool.rearrange("g i o -> (g i) o"), 0, 512, 0, 128)]
        P += [((hf_, "glu", wt), w_glu, 0, 1024, 512 * wt, 512) for wt in range(2)]
        for wt in range(2):
            P += [((hf_, "pb", wt), w_pb, 0, 512, 512 * wt, 512), ((hf_, "pa", wt), w_pa, 0, 1024, 512 * wt, 512),
                  ((hf_, "ga", wt), w_in, 0, 1024, 1536 + 512 * wt, 512), ((hf_, "gb", wt), w_in, 0, 1024, 2560 + 512 * wt, 512)]
        P += [((hf_, "out", wt), w_out, 0, 1024, 512 * wt, 512) for wt in range(2)]
        P += [((hf_, "w1", wt), w1, 0, 1024, 512 * wt, 512) for wt in range(8)]
        P += [((hf_, "w2", ps_, m_), w2, 0, 4096, 128 * m_, 128) for ps_ in range(2) for m_ in range(8)]
        return P

    with ExitStack() as sc:
        def t64(name):
            return sb(name, [128, 64], F32, sc)
        lre, lim, ldt = t64("lre"), t64("lim"), t64("ldt")
        Bre = sb("Bre", [128, 64, 16], F32, sc)
        Bim = sb("Bim", [128, 64, 16], F32, sc)
        Cre = sb("Cre", [128, 64, 16], F32, sc)
        Cim = sb("Cim", [128, 64, 16], F32, sc)
        dsum = t64("dsum")
        Ln = sb("Ln", [64, 2, 2, 64], F32, sc)
        Sn = sb("Sn", [64, 2, 2, 64], F32, sc)
        Cn = sb("Cn", [128, 2, 8, 2, 64], F32, sc)
        Dn = sb("Dn", [64, 2, 16], F32, sc)
        Dn8 = sb("Dn8", [64, 8, 16], F32, sc)
        dma("sp", Ln[:, 0], lam_re.rearrange("d g p -> g d p"), w=[B("Ln")])
        dma("sp", Ln[:, 1], lam_im.rearrange("d g p -> g d p"), w=[B("Ln")])
        for d in range(2):
            sl = slice(64 * d, 64 * d + 64)
            dma("sp", ldt[sl, :], log_dt[d:d + 1, :].partition_broadcast(64).rearrange("p o f -> p (o f)"), w=[B("ldt")])
        dma("sp", Dn[:], s5_d.rearrange("d (g h) -> g d h", h=16), w=[B("Dn")], allow_slow_non_contiguous=True)
        for r_ in range(2):
            dma("sp", Sn[:, r_], st0[:, r_].rearrange("d g p -> g d p"), w=[B("Sn")])
        for d in range(2):
            sl = slice(64 * d, 64 * d + 64)
            dma("sp", Bre[sl], b_re[d].rearrange("g p h -> p g h"), w=[B("Bre")], allow_slow_non_contiguous=True)
            dma("sp", Bim[sl], b_im[d].rearrange("g p h -> p g h"), w=[B("Bim")], allow_slow_non_contiguous=True)
        for ri_, src_ in enumerate((c_re, c_im)):
            for d in range(2):
                tokC = dma("sp", Cn[:, ri_, :, d, :], src_[d].rearrange("(gh gl) h p -> (gl h) gh p", gl=8), w=[B("Cn")])

        def tr_small(dst, src_ap, nrow, srcB, dstB):
            bk, bkB = kb.bank()
            op("pe", lambda h: h.transpose(bk[:, 0:nrow], src_ap, ident[0:nrow, 0:nrow]), r=[B(srcB), B("ident")], w=[bkB])
            op("act", lambda h: h.activation(out=dst, in_=bk[:, 0:nrow], func=AF.Identity), r=[bkB], w=[B(dstB)])

        tr_small(lre[:], Ln[:, 0].rearrange("g d p -> g (d p)"), 64, "Ln", "lre")
        tr_small(lim[:], Ln[:, 1].rearrange("g d p -> g (d p)"), 64, "Ln", "lim")
        tr_small(s0t[:, 0, :], Sn[:, 0].rearrange("g d p -> g (d p)"), 64, "Sn", "s0t")
        tr_small(s0t[:, 1, :], Sn[:, 1].rearrange("g d p -> g (d p)"), 64, "Sn", "s0t")
        op("dve", lambda h: h.tensor_tensor(out=Dn[:, 0, :], in0=Dn[:, 0, :], in1=Dn[:, 1, :], op=ALU.add), r=[B("Dn")], w=[B("Dn")])
        op("dve", lambda h: h.tensor_copy(out=Dn8[:], in_=Dn[:, 0, :].unsqueeze(1).broadcast_to([64, 8, 16])), r=[B("Dn")], w=[B("Dn8")])
        tr_small(dsum[:], Dn8[:].rearrange("g a b -> g (a b)"), 64, "Dn8", "dsum")
        for ri_, dstC in enumerate((Cre, Cim)):
            for gh4 in range(2):
                bk, bkB = kb.bank()
                for j in range(4):
                    gh = gh4 * 4 + j
                    op("pe", lambda h, j=j, gh=gh, ri_=ri_: h.transpose(bk[:, j * 128:(j + 1) * 128], Cn[:, ri_, gh].rearrange("p d x -> p (d x)"), ident[:]),
                       r=[B("Cn"), B("ident")], w=[bkB], inc=(j == 3))
                op("act", lambda h, gh4=gh4, dstC=dstC: h.activation(out=dstC[:, 32 * gh4:32 * gh4 + 32, :].rearrange("p a b -> p (a b)"), in_=bk[:], func=AF.Identity),
                   r=[bkB], w=[B("Cre"), B("Cim")])

        ld(pc[:].rearrange("p a b -> p (a b)"), c_pc, "pc", q="pool")
        ld(identb[:], c_ident, "identb", q="pool")
        ld(bandP[:].rearrange("p a b c -> p (a b c)"), c_bandp, "band", q="pool")
        ld(bandS[:].rearrange("p a b -> p (a b)"), c_bands, "band", q="pool")
        ld(bub[:], b_in[1024:1536].partition_broadcast(128), "bub")
        for i, src in enumerate((ln1_g, ln1_b, ln2_g, ln2_b)):
            ld(lnp[:, i, :], src.partition_broadcast(128).rearrange("p o f -> p (o f)"), "lnp")
        ld(binT[:], b_in.rearrange("(j p) -> p j", p=128), "binT", allow_slow_non_contiguous=True)
        ld(badaT[:], b_ada.rearrange("(j p) -> p j", p=128), "badaT", allow_slow_non_contiguous=True)
        ld(bgluT[:], b_glu.rearrange("(j p) -> p j", p=128), "bgluT", allow_slow_non_contiguous=True)
        ld(boutT[:], b_out.rearrange("(j p) -> p j", p=128), "boutT", allow_slow_non_contiguous=True)
        ld(b1T[:], b1.rearrange("(j p) -> p j", p=128), "b1T", allow_slow_non_contiguous=True)
        ld(b2T[:], b2.rearrange("(j p) -> p j", p=128), "b2T", allow_slow_non_contiguous=True)
        ld(pscT[:], pool_scale.rearrange("(j p) -> p j", p=128), "pscT", allow_slow_non_contiguous=True)
        for tb_ in range(DEPTH):
            dma("sp", xsl[tb_][:], xin[tb_ * 128:(tb_ + 1) * 128, :], w=[B("xs", tb_)])
        kb._wait(kb.E["pool"], tokC)
        wp.plan([(("ada", wt), w_ada, 0, 1024, 512 * wt, 512) for wt in range(5)])
        cnt = {"i": 0}

        def tt(out, a, b_, o, r, w, eng=None):
            en = eng or ("dve" if cnt["i"] % 2 == 0 else "pool")
            cnt["i"] += 1
            return op(en, lambda h: h.tensor_tensor(out=out, in0=a, in1=b_, op=o), r=[B(x) for x in r], w=[B(x) for x in w])

        def tsc(out, a, s1, s2, o0, o1, r, w):
            return op("dve", lambda h: h.tensor_scalar(out=out, in0=a, scalar1=s1, scalar2=s2, op0=o0, op1=o1) if o1 is not None else
                      h.tensor_scalar(out=out, in0=a, scalar1=s1, scalar2=None, op0=o0), r=[B(x) for x in r], w=[B(x) for x in w])

        dtt, a_, th, ea = t64("dtt"), t64("a_"), t64("th"), t64("ea")
        op("act", lambda h: h.activation(out=dtt[:], in_=ldt[:], func=AF.Exp), r=[B("ldt")], w=[B("dtt")])
        tt(a_[:], lre[:], dtt[:], ALU.mult, ["lre", "dtt"], ["a_"], "dve")
        tt(th[:], lim[:], dtt[:], ALU.mult, ["lim", "dtt"], ["th"], "dve")
        op("act", lambda h: h.activation(out=ea[:], in_=a_[:], func=AF.Exp), r=[B("a_")], w=[B("ea")])

        def sin_of(name, shift):
            x, k, r_ = t64(name + "x"), t64(name + "k"), t64(name + "r")
            res = t64(name)
            tsc(x[:], th[:], shift, None, ALU.add, None, ["th"], [name + "x"])
            tsc(k[:], x[:], PI, None, ALU.is_gt, None, [name + "x"], [name + "k"])
            for m in range(1, 6):
                op("dve", lambda h, m=m: h.scalar_tensor_tensor(out=k[:], in0=x[:], scalar=(2 * m + 1) * PI, in1=k[:], op0=ALU.is_gt, op1=ALU.add),
                   r=[B(name + "x"), B(name + "k")], w=[B(name + "k")])
            c1 = float(np.float32(2.0 * PI))
            c2 = 2.0 * PI - c1
            op("dve", lambda h: h.scalar_tensor_tensor(out=r_[:], in0=k[:], scalar=-c1, in1=x[:], op0=ALU.mult, op1=ALU.add),
               r=[B(name + "x"), B(name + "k")], w=[B(name + "r")])
            op("dve", lambda h: h.scalar_tensor_tensor(out=r_[:], in0=k[:], scalar=-c2, in1=r_[:], op0=ALU.mult, op1=ALU.add),
               r=[B(name + "r"), B(name + "k")], w=[B(name + "r")])
            tsc(r_[:], r_[:], -3.1415925, 3.1415925, ALU.max, ALU.min, [name + "r"], [name + "r"])
            op("act", lambda h: h.activation(out=res[:], in_=r_[:], func=AF.Sin), r=[B(name + "r")], w=[B(name)])
            return res

        sn = sin_of("sn", 0.0)
        cs = sin_of("cs", PI / 2)
        PWr = sb("PWr", [128, 9, 64], F32, sc)
        PWi = sb("PWi", [128, 9, 64], F32, sc)
        op("dve", lambda h: h.memset(PWr[:, 0, :], 1.0), w=[B("PW")])
        op("dve", lambda h: h.memset(PWi[:, 0, :], 0.0), w=[B("PW")])
        tt(PWr[:, 1, :], ea[:], cs[:], ALU.mult, ["ea", "cs"], ["PW"], "dve")
        tt(PWi[:, 1, :], ea[:], sn[:], ALU.mult, ["ea", "sn"], ["PW"], "dve")
        t1, t2 = t64("t1"), t64("t2")

        def cmul(outr, outi, ar, ai, br, bi, rn, wn):
            tt(t1[:], ar, br, ALU.mult, rn, ["t1"], "dve")
            tt(t2[:], ai, bi, ALU.mult, rn, ["t2"], "dve")
            tt(outr, t1[:], t2[:], ALU.subtract, ["t1", "t2"], wn, "dve")
            tt(t1[:], ar, bi, ALU.mult, rn, ["t1"], "dve")
            tt(t2[:], ai, br, ALU.mult, rn, ["t2"], "dve")
            tt(outi, t1[:], t2[:], ALU.add, ["t1", "t2"], wn, "dve")

        for k in range(1, 8):
            cmul(PWr[:, k + 1, :], PWi[:, k + 1, :], PWr[:, k, :], PWi[:, k, :], PWr[:, 1, :], PWi[:, 1, :], ["PW"], ["PW"])
        i8r, i8i, m2 = t64("i8r"), t64("i8i"), t64("m2")
        tt(t1[:], PWr[:, 8, :], PWr[:, 8, :], ALU.mult, ["PW"], ["t1"], "dve")
        tt(t2[:], PWi[:, 8, :], PWi[:, 8, :], ALU.mult, ["PW"], ["t2"], "dve")
        tt(m2[:], t1[:], t2[:], ALU.add, ["t1", "t2"], ["m2"], "dve")
        op("dve", lambda h: h.reciprocal(out=m2[:], in_=m2[:]), r=[B("m2")], w=[B("m2")])
        tt(i8r[:], PWr[:, 8, :], m2[:], ALU.mult, ["PW", "m2"], ["i8"], "dve")
        op("dve", lambda h: h.scalar_tensor_tensor(out=i8i[:], in0=PWi[:, 8, :], scalar=-1.0, in1=m2[:], op0=ALU.mult, op1=ALU.mult),
           r=[B("PW"), B("m2")], w=[B("i8")])
        cr, ci, nr = t64("cr"), t64("ci"), t64("nr")
        tsc(nr[:], PWr[:, 1, :], -1.0, None, ALU.add, None, ["PW"], ["nr"])
        tt(t1[:], lre[:], lre[:], ALU.mult, ["lre"], ["t1"], "dve")
        tt(t2[:], lim[:], lim[:], ALU.mult, ["lim"], ["t2"], "dve")
        tt(m2[:], t1[:], t2[:], ALU.add, ["t1", "t2"], ["m2"], "dve")
        op("dve", lambda h: h.reciprocal(out=m2[:], in_=m2[:]), r=[B("m2")], w=[B("m2")])
        tt(t1[:], nr[:], lre[:], ALU.mult, ["nr", "lre"], ["t1"], "dve")
        tt(t2[:], PWi[:, 1, :], lim[:], ALU.mult, ["PW", "lim"], ["t2"], "dve")
        tt(cr[:], t1[:], t2[:], ALU.add, ["t1", "t2"], ["cr"], "dve")
        tt(cr[:], cr[:], m2[:], ALU.mult, ["cr", "m2"], ["cr"], "dve")
        tt(t1[:], PWi[:, 1, :], lre[:], ALU.mult, ["PW", "lre"], ["t1"], "dve")
        tt(t2[:], nr[:], lim[:], ALU.mult, ["nr", "lim"], ["t2"], "dve")
        tt(ci[:], t1[:], t2[:], ALU.subtract, ["t1", "t2"], ["ci"], "dve")
        tt(ci[:], ci[:], m2[:], ALU.mult, ["ci", "m2"], ["ci"], "dve")
        op("dve", lambda h: h.tensor_copy(out=A4[:], in_=PWr[:, 8, :].unsqueeze(1).broadcast_to([128, 4, 64])), r=[B("PW")], w=[B("A4")])
        op("dve", lambda h: h.tensor_copy(out=B4[:], in_=PWi[:, 8, :].unsqueeze(1).broadcast_to([128, 4, 64])), r=[B("PW")], w=[B("A4")])
        pZr = sb("pZr", [128, 64, 8], F32, sc)
        pZi = sb("pZi", [128, 64, 8], F32, sc)
        pYr = sb("pYr", [128, 64, 8], F32, sc)
        pYi = sb("pYi", [128, 64, 8], F32, sc)
        pYn = sb("pYn", [128, 64, 8], F32, sc)
        lo, hi = slice(0, 64), slice(64, 128)
        for i in range(8):
            for (dst, src) in ((pZr, PWr), (pZi, PWi)):
                op("dve", lambda h, dst=dst, src=src, i=i: h.tensor_copy(out=dst[lo, :, i], in_=src[lo, 7 - i, :]), r=[B("PW")], w=[B("pZ")])
                op("dve", lambda h, dst=dst, src=src, i=i: h.tensor_copy(out=dst[hi, :, i], in_=src[hi, i, :]), r=[B("PW")], w=[B("pZ")])
            for (dst, src) in ((pYr, PWr), (pYi, PWi)):
                op("dve", lambda h, dst=dst, src=src, i=i: h.tensor_copy(out=dst[lo, :, i], in_=src[lo, i + 1, :]), r=[B("PW")], w=[B("pY")])
                op("dve", lambda h, dst=dst, src=src, i=i: h.tensor_copy(out=dst[hi, :, i], in_=src[hi, 8 - i, :]), r=[B("PW")], w=[B("pY")])
        tsc(pYn[:], pYi[:], -1.0, None, ALU.mult, None, ["pY"], ["pYn"])
        Bbr = sb("Bbr", [128, 64, 16], F32, sc)
        Bbi = sb("Bbi", [128, 64, 16], F32, sc)
        tb1 = sb("tb1", [128, 64, 16], F32, sc)
        tb2 = sb("tb2", [128, 64, 16], F32, sc)
        crb = cr[:].unsqueeze(2).broadcast_to([128, 64, 16])
        cib = ci[:].unsqueeze(2).broadcast_to([128, 64, 16])
        tt(tb1[:], Bre[:], crb, ALU.mult, ["Bre", "cr"], ["tb1"], "dve")
        tt(tb2[:], Bim[:], cib, ALU.mult, ["Bim", "ci"], ["tb2"], "dve")
        tt(Bbr[:], tb1[:], tb2[:], ALU.subtract, ["tb1", "tb2"], ["Bb"], "dve")
        tt(tb1[:], Bre[:], cib, ALU.mult, ["Bre", "ci"], ["tb1"], "dve")
        tt(tb2[:], Bim[:], crb, ALU.mult, ["Bim", "cr"], ["tb2"], "dve")
        tt(Bbi[:], tb1[:], tb2[:], ALU.add, ["tb1", "tb2"], ["Bb"], "dve")

        GB = 8
        NE = GB * 128
        WZr = sb("WZr", [128, GB, 8, 16], F32, sc)
        WZi = sb("WZi", [128, GB, 8, 16], F32, sc)
        WYr = sb("WYr", [128, GB, 8, 16], F32, sc)
        WYn = sb("WYn", [128, GB, 8, 16], F32, sc)
        X1 = sb("X1", [128, GB, 8, 16], F32, sc)
        X2 = sb("X2", [128, GB, 8, 16], F32, sc)
        WZTb = sb("WZTb", [128, 2, GB, 128], BF16, sc)
        WYb = sb("WYb", [128, 2, GB, 128], BF16, sc)
        W0f = sb("W0f", [128, GB, 128], F32, sc)
        W0b = sb("W0b", [128, GB, 128], BF16, sc)
        shp = [128, GB, 8, 16]
        for blk in range(64 // GB):
            gs = slice(GB * blk, GB * blk + GB)
            Bbrb = Bbr[:, gs, :].unsqueeze(2).broadcast_to(shp)
            Bbib = Bbi[:, gs, :].unsqueeze(2).broadcast_to(shp)
            Creb = Cre[:, gs, :].unsqueeze(2).broadcast_to(shp)
            Cimb = Cim[:, gs, :].unsqueeze(2).broadcast_to(shp)
            pzr = pZr[:, gs, :].unsqueeze(3).broadcast_to(shp)
            pzi = pZi[:, gs, :].unsqueeze(3).broadcast_to(shp)
            pyr = pYr[:, gs, :].unsqueeze(3).broadcast_to(shp)
            pyi = pYi[:, gs, :].unsqueeze(3).broadcast_to(shp)
            pyn = pYn[:, gs, :].unsqueeze(3).broadcast_to(shp)
            i8rb = i8r[:, gs].unsqueeze(2).unsqueeze(3).broadcast_to(shp)
            i8ib = i8i[:, gs].unsqueeze(2).unsqueeze(3).broadcast_to(shp)
            Y1 = tb1[:].rearrange("p (g i) h -> p g i h", g=GB)
            Z8r = W0f[:].rearrange("p g (a b) -> p g a b", a=8)
            tt(WZr[:], Bbrb, pzr, ALU.mult, ["Bb", "pZ"], ["WZr"], "dve")
            tt(X1[:], Bbib, pzi, ALU.mult, ["Bb", "pZ"], ["X1"], "dve")
            tt(WZi[:], Bbrb, pzi, ALU.mult, ["Bb", "pZ"], ["WZi"], "dve")
            tt(X2[:], Bbib, pzr, ALU.mult, ["Bb", "pZ"], ["X2"], "dve")
            tt(WZr[:], WZr[:], X1[:], ALU.subtract, ["WZr", "X1"], ["WZr"], "dve")
            tt(WZi[:], WZi[:], X2[:], ALU.add, ["WZi", "X2"], ["WZi"], "dve")
            for ri, src, sn_ in ((0, WZr, "WZr"), (1, WZi, "WZi")):
                for g4 in range(GB // 4):
                    bk, bkB = kb.bank()
                    for gg in range(4):
                        g = g4 * 4 + gg
                        op("pe", lambda h, g=g, gg=gg, src=src: h.transpose(bk[:, gg * 128:(gg + 1) * 128], src[:, g].rearrange("p a b -> p (a b)"), ident[:]),
                           r=[B(sn_), B("ident")], w=[bkB], inc=(gg == 3))
                    op("act", lambda h, ri=ri, g4=g4: h.activation(out=WZTb[:, ri, g4 * 4:g4 * 4 + 4, :].rearrange("p a b -> p (a b)"), in_=bk[:], func=AF.Identity),
                       r=[bkB], w=[B("WZTb")])
            dma("sp", wzt_d[:, :, gs, :], WZTb[:], r=[B("WZTb")], w=[B("wzt_d")])
            tt(Z8r, WZr[:], i8rb, ALU.mult, ["WZr", "i8"], ["W0f"], "dve")
            tt(X1[:], WZi[:], i8ib, ALU.mult, ["WZi", "i8"], ["X1"], "dve")
            tt(Y1, WZr[:], i8ib, ALU.mult, ["WZr", "i8"], ["tb1"], "dve")
            tt(X2[:], WZi[:], i8rb, ALU.mult, ["WZi", "i8"], ["X2"], "dve")
            tt(Z8r, Z8r, X1[:], ALU.subtract, ["W0f", "X1"], ["W0f"], "dve")
            tt(Y1, Y1, X2[:], ALU.add, ["tb1", "X2"], ["tb1"], "dve")
            tt(WYr[:], Creb, pyr, ALU.mult, ["Cre", "pY"], ["WYr"], "dve")
            tt(X1[:], Cimb, pyi, ALU.mult, ["Cim", "pY"], ["X1"], "dve")
            tt(WYn[:], Creb, pyn, ALU.mult, ["Cre", "pYn"], ["WYn"], "dve")
            tt(X2[:], Cimb, pyr, ALU.mult, ["Cim", "pY"], ["X2"], "dve")
            tt(WYr[:], WYr[:], X1[:], ALU.subtract, ["WYr", "X1"], ["WYr"], "dve")
            tt(WYn[:], WYn[:], X2[:], ALU.subtract, ["WYn", "X2"], ["WYn"], "dve")
            op("act", lambda h: h.activation(out=WYb[:, 0].rearrange("p g b -> p (g b)"), in_=WYr[:].rearrange("p g a b -> p (g a b)"), func=AF.Identity),
               r=[B("WYr")], w=[B("WYb")])
            op("act", lambda h: h.activation(out=WYb[:, 1].rearrange("p g b -> p (g b)"), in_=WYn[:].rearrange("p g a b -> p (g a b)"), func=AF.Identity),
               r=[B("WYn")], w=[B("WYb")])
            dma("sp", wyb_d[:, :, gs, :], WYb[:], r=[B("WYb")], w=[B("wyb_d")])
            for g4 in range(GB // 4):
                bkf, bkfB = kb.bank()
                bkb, bkbB = kb.bank()
                for gg in range(4):
                    g = g4 * 4 + gg
                    for d, (bk_, bB_) in enumerate(((bkf, bkfB), (bkb, bkbB))):
                        ps_ = slice(64 * d, 64 * d + 64)
                        op("pe", lambda h, g=g, gg=gg, bk_=bk_, ps_=ps_: h.matmul(bk_[:, gg * 128:(gg + 1) * 128], lhsT=W0f[ps_, g, :],
                                                                                  rhs=WYr[ps_, g].rearrange("p a b -> p (a b)"), start=True, stop=False),
                           r=[B("W0f"), B("WYr")], w=[bB_], inc=False)
                        op("pe", lambda h, g=g, gg=gg, bk_=bk_, ps_=ps_: h.matmul(bk_[:, gg * 128:(gg + 1) * 128], lhsT=tb1[ps_, GB * g:GB * g + 8, :].rearrange("p a b -> p (a b)"),
                                                                                  rhs=WYn[ps_, g].rearrange("p a b -> p (a b)"), start=False, stop=True),
                           r=[B("tb1"), B("WYn")], w=[bB_], inc=True)
                g4s = slice(4 * g4, 4 * g4 + 4)
                mfb = maskf[:].unsqueeze(1).broadcast_to([128, 4, 128])
                mbb = maskb[:].unsqueeze(1).broadcast_to([128, 4, 128])
                op("dve", lambda h, g4s=g4s, bkf=bkf: h.tensor_tensor(out=X2[:, g4s].rearrange("p g a b -> p g (a b)"), in0=bkf[:].rearrange("p (g c) -> p g c", g=4), in1=mfb, op=ALU.mult),
                   r=[bkfB, B("maskf")], w=[B("X2")])
                op("dve", lambda h, g4s=g4s, bkb=bkb: h.tensor_tensor(out=X1[:, g4s].rearrange("p g a b -> p g (a b)"), in0=bkb[:].rearrange("p (g c) -> p g c", g=4), in1=mbb, op=ALU.mult),
                   r=[bkbB, B("maskb")], w=[B("X1")])
            op("dve", lambda h: h.tensor_tensor(out=X2[:], in0=X2[:], in1=X1[:], op=ALU.add), r=[B("X2"), B("X1")], w=[B("X2")])
            op("dve", lambda h, gs=gs: h.tensor_tensor(out=X1[:].rearrange("p g a b -> p g (a b)"), in0=ident[:].unsqueeze(1).broadcast_to([128, GB, 128]),
                                                       in1=dsum[:, gs].unsqueeze(2).broadcast_to([128, GB, 128]), op=ALU.mult),
               r=[B("ident"), B("dsum"), B("X1")], w=[B("X1")])
            op("dve", lambda h: h.tensor_tensor(out=W0b[:], in0=X2[:].rearrange("p g a b -> p g (a b)"), in1=X1[:].rearrange("p g a b -> p g (a b)"), op=ALU.add),
               r=[B("X2"), B("X1")], w=[B("W0b")])
            dma("sp", w0b_d[:, gs, :], W0b[:], r=[B("W0b")], w=[B("w0b_d")])
            if blk == 0:
                dump("W0b", W0b[:], ["W0b"])
                dump("WYb", WYb[:], ["WYb"])
                dump("WZTb", WZTb[:], ["WZTb"])
        dump("PWr", PWr[:], ["PW"])
        dump("PWi", PWi[:], ["PW"])
        dump("cr", cr[:], ["cr"])
        dump("ci", ci[:], ["ci"])
        dump("dsum", dsum[:], ["dsum"])
        kb.barrier()
        cut(1)

    cTb = sb("cTb", [128, 8, 2], BF16)
    with ExitStack() as sc:
        cT = sb("cT", [128, 8, 2], F32, sc)
        for v in range(2):
            dma("sp", cT[:, :, v], cvec[v, :].rearrange("(k p) -> p k", p=128), w=[B("cT")], allow_slow_non_contiguous=True)
        op("act", lambda h: h.activation(out=cTb[:], in_=cT[:], func=AF.Silu), r=[B("cT")], w=[B("cTb")])
        kb.barrier()

    def mod_part(tiles, j0, nj, modB):
        bk, bkB = kb.bank()
        for wt in tiles:
            wv, wb = wp.get(("ada", wt))
            for mm in range(4):
                j = wt * 4 + mm
                c0 = 2 * (j - j0)
                for k in range(8):
                    op("pe", lambda h, k=k, mm=mm, c0=c0: h.matmul(bk[:, c0:c0 + 2], lhsT=wv[:, k, mm * 128:(mm + 1) * 128], rhs=cTb[:, k, :],
                                                                  start=(k == 0), stop=(k == 7)),
                       r=[wb, B("cTb")], w=[bkB], inc=(k == 7))
            wp.release(("ada", wt))
        return bk, bkB

    def mod_evac(bk, bkB, j0, nj, modB):
        op("dve", lambda h: h.tensor_tensor(out=modT[:, j0:j0 + nj, :], in0=bk[:, 0:2 * nj].rearrange("p (j v) -> p j v", v=2),
                                            in1=badaT[:, j0:j0 + nj].unsqueeze(2).broadcast_to([128, nj, 2]), op=ALU.add),
           r=[bkB, B("badaT")], w=[modB])

    bk_, bkB_ = mod_part(range(4), 0, 16, B("mod"))
    mod_evac(bk_, bkB_, 0, 16, B("mod"))
    op("dve", lambda h: h.tensor_scalar(out=mder[:, 0], in0=modT[:, 8:16, :], scalar1=1.0, scalar2=None, op0=ALU.add), r=[B("mod")], w=[B("mod")])

    def mod_part2_evac(bk, bkB):
        mod_evac(bk, bkB, 16, 32, B("mod2"))
        op("dve", lambda h: h.tensor_scalar(out=mder[:, 1], in0=modT[:, 32:40, :], scalar1=1.0, scalar2=None, op0=ALU.add), r=[B("mod2")], w=[B("mod2")])
        op("dve", lambda h: h.tensor_tensor(out=mder[:, 2], in0=modT[:, 16:24, :], in1=boutT[:].unsqueeze(2).broadcast_to([128, 8, 2]), op=ALU.mult),
           r=[B("mod2"), B("boutT")], w=[B("mod2")])
        op("dve", lambda h: h.tensor_tensor(out=mder[:, 3], in0=modT[:, 40:48, :], in1=b2T[:].unsqueeze(2).broadcast_to([128, 8, 2]), op=ALU.mult),
           r=[B("mod2"), B("b2T")], w=[B("mod2")])
    cut(0)
    wp.free = [i for i in wp.free if i < NW]
    del wsl[NW:]
    xsc.close()
    hp0 = half_plan(0)
    wp.plan(hp0[:2] + [(("ada", wt), w_ada, 0, 1024, 512 * wt, 512) for wt in range(5, 12)] + hp0[2:])
    sh1 = modT[:, 0:8, :]
    sc1p = mder[:, 0]
    gt1 = modT[:, 16:24, :]
    sh2 = modT[:, 24:32, :]
    sc2p = mder[:, 1]
    gt2 = modT[:, 40:48, :]
    gt1b = mder[:, 2]
    gt2b = mder[:, 3]

    hT = sb("hT", [128, 8, 1024], BF16)
    vT = sb("vT", [128, 8, 1024], BF16)
    h2T = vT
    for hf in range(2):
        v = hf
        row0 = 1024 * hf
        nseq = 4 if hf == 0 else 1
        CS = 128 // nseq
        hTB = lambda th: B("hT", th)
        def chainA(tb):
            par = tb % DEPTH
            xs, xB = xsl[par], B("xs", par)
            if not (hf == 0 and tb < DEPTH):
                dma("sp", xs[:], xin[row0 + tb * 128:row0 + (tb + 1) * 128, :], w=[xB])
            yield
            i = ln_part1(xs, xB)
            yield
            ln_part2(i, xs[:], xs[:], xB, xB)
            yield
            to_feature_major(xs, xB, hT, hTB, tb, sc1p, sh1, v)
        run_pipelined([chainA(tb) for tb in range(8)], DEPTH)
        if hf == 0:
            dump("hT", hT[:], [("hT", 0), ("hT", 1)])
            cut(2)

        with ExitStack() as s5sc:
            U = sb("U", [128, 64, 128], BF16, s5sc)
            ZS = sb("ZS", [128, 2, 128, 64], BF16, s5sc)
            s5w = [sb("s5w%d" % i, [128, 3, 8, 128], BF16, s5sc) for i in range(2)]
            with ExitStack() as uasc:
                uaT = sb("uaT", [128, 8, 1024], BF16, uasc)
                for wt in range(2):
                    wv, wb = wp.get((hf, "ua", wt))
                    for mm in range(4):
                        m = wt * 4 + mm
                        for th in range(2):
                            bk, bkB = kb.bank()
                            for k in range(8):
                                op("pe", lambda h, k=k, mm=mm, th=th: h.matmul(bk[:], lhsT=wv[:, k, mm * 128:(mm + 1) * 128], rhs=hT[:, k, th * 512:(th + 1) * 512],
                                                                              start=(k == 0), stop=(k == 7)),
                                   r=[wb, hTB(th)], w=[bkB], inc=(k == 7))
                            op("act", lambda h, m=m, th=th: h.activation(out=uaT[:, m, th * 512:(th + 1) * 512], in_=bk[:], func=AF.Identity, bias=binT[:, m:m + 1], scale=1.0),
                               r=[bkB, B("binT")], w=[B("uaT", m)])
                    wp.release((hf, "ua", wt))
                xrc = [sb("xrc%d" % i, [128, 1024], BF16, uasc) for i in range(2)]
                for ct in range(8):
                    bk, bkB = kb.bank()
                    bkh = bk[:].bitcast(BF16)
                    for i in range(8):
                        op("pe", lambda h, i=i, ct=ct: h.transpose(bkh[:, i * 128:(i + 1) * 128], uaT[:, ct, i::8], identb[:]),
                           r=[B("uaT", ct), B("identb")], w=[bkB], inc=(i == 7))
                    xr = xrc[ct % 2]
                    xrB = B("xrc", ct % 2)
                    op("dve", lambda h: h.tensor_copy(out=xr[:].rearrange("p (g i o) -> p g i o", g=8, i=8), in_=bkh.rearrange("p (i g o) -> p g i o", i=8, g=8)),
                       r=[bkB], w=[xrB])
                    bk2, bk2B = kb.bank()
                    bk2h = bk2[:].bitcast(BF16)
                    for gl in range(8):
                        op("pe", lambda h, gl=gl: h.transpose(bk2h[:, gl * 128:(gl + 1) * 128], xr[:, gl * 128:(gl + 1) * 128], identb[:]),
                           r=[xrB, B("identb")], w=[bk2B], inc=(gl == 7))
                    op("act", lambda h, ct=ct: h.activation(out=U[:, 8 * ct:8 * ct + 8, :].rearrange("p a b -> p (a b)"), in_=bk2h, func=AF.Identity),
                       r=[bk2B], w=[B("U")])
                if hf == 0:
                    dump("uaT", uaT[:], [("uaT", m_) for m_ in range(8)])
                    dump("U", U[:], ["U"])
                kb.barrier()
                if hf == 0:
                    cut(3)

            def rev(ap3):
                if NOREV:
                    return ap3
                if nseq == 1:
                    return ap3[:, ::-1]
                return ap3.rearrange("p (q c) -> p q c", q=nseq)[:, :, ::-1]

            if hf == 0:
                nsT = sb("nsT", [64, 8, 128], F32, s5sc)
                finS = [sb("finS%d" % ri, [128, 256], F32, s5sc) for ri in range(2)]
            for gb in range(8):
                sw = s5w[gb % 2]
                swB = B("s5w", gb % 2)
                dma("sp", sw[:, 0:2], wzt_d[:, :, 8 * gb:8 * gb + 8, :], r=[B("wzt_d")], w=[swB])
                for g4_ in range(2):
                    for ri in range(2):
                        bkF, bkFB = kb.bank()
                        for gg in range(4):
                            gl = 4 * g4_ + gg
                            g = 8 * gb + gl
                            op("pe", lambda h, ri=ri, gl=gl, g=g, gg=gg: h.matmul(bkF[:, gg::4], lhsT=sw[:, ri, gl, :], rhs=U[:, g, :], start=True, stop=True),
                               r=[swB, B("U")], w=[bkFB], inc=(gg == 3))
                        g0 = 8 * gb + 4 * g4_
                        en = "act" if ri == 0 else "dve"
                        if en == "act":
                            op("act", lambda h, g0=g0, ri=ri: h.activation(out=ZS[:, ri, :, g0:g0 + 4], in_=bkF[:].rearrange("p (n g) -> p n g", g=4), func=AF.Identity),
                               r=[bkFB], w=[B("ZSz"), B("ZSs")])
                        else:
                            op("dve", lambda h, g0=g0, ri=ri: h.tensor_copy(out=ZS[:, ri, :, g0:g0 + 4], in_=bkF[:].rearrange("p (n g) -> p n g", g=4)),
                               r=[bkFB], w=[B("ZSz"), B("ZSs")])

            if hf == 0:
                dump("Z", ZS[:], ["ZSz"])
                cut(4)
            with ExitStack() as rsc:
                W = 64 * nseq
                if hf == 0:
                    while kb.nbank % 8 < 4:
                        kb.nbank += 1
                    mbk, mbkB = mod_part(range(4, 12), 16, 32, B("mod2"))
                pst = [kb.ps[i] for i in range(4)]
                St = [[pst[a][:, ri * 256:ri * 256 + W] for ri in range(2)] for a in range(2)]
                rt = [pst[2][:, 0:W], None, pst[2][:, 256:256 + W], None, pst[3][:, 0:W], pst[3][:, 256:256 + W]]
                rts = [sb("rts%d" % i, [128, W], F32, rsc) for i in range(2)]
                rt[1], rt[3] = rts[0][:], rts[1][:]
                Aw = A4[:].rearrange("p q g -> p (q g)") if nseq == 4 else A4[:, 0, :]
                Bw = B4[:].rearrange("p q g -> p (q g)") if nseq == 4 else B4[:, 0, :]
                psB = [B("ps", i) for i in range(4)]
                if hf == 0:
                    op("dve", lambda h: h.memset(pst[0][:], 0.0), w=[psB[0], B("St", 0)])
                else:
                    op("dve", lambda h: h.memset(pst[0][:], 0.0), w=[psB[0], B("St", 0)])
                    op("dve", lambda h: h.tensor_copy(out=St[0][0], in_=s0t[:, 0, :]), r=[B("s0t")], w=[B("St", 0)])
                    op("dve", lambda h: h.tensor_copy(out=St[0][1], in_=s0t[:, 1, :]), r=[B("s0t")], w=[B("St", 0)])
                op("dve", lambda h: h.memset(pst[1][:], 0.0), w=[psB[1], B("St", 1)])
                op("dve", lambda h: h.memset(pst[2][:], 0.0), w=[psB[2], B("tA")])
                op("dve", lambda h: h.memset(pst[3][:], 0.0), w=[psB[3], B("tN", 0), B("tN", 1)])

                def zs_slot(ri, c):
                    if nseq == 4:
                        return ZS[:, ri].rearrange("p (q c) g -> p q c g", q=4)[:, :, c, :]
                    return ZS[:, ri, c, :]

                def st_view(t):
                    if nseq == 4:
                        return t.rearrange("p (q g) -> p q g", q=4)
                    return t

                tA = pst[2][:].rearrange("p (r x) -> p r x", r=2)[:, :, 0:W]
                tBs = sb("tBs", [128, 2, W], F32, rsc)
                tN = pst[3][:].rearrange("p (r x) -> p r x", r=2)[:, :, 0:W]
                Abc = Aw.unsqueeze(1).broadcast_to([128, 2, W])
                Bbc = Bw.unsqueeze(1).broadcast_to([128, 2, W])

                def st2(a_):
                    return pst[a_][:].rearrange("p (r x) -> p r x", r=2)[:, :, 0:W]

                def zs2(c, P_):
                    if nseq == 4:
                        return ZS[P_].rearrange("p r (q c) g -> p r q c g", q=4)[:, :, :, c, :]
                    return ZS[P_, :, c, :]

                def v2(ap, P_):
                    if nseq == 4:
                        return ap[P_].rearrange("p r (q g) -> p r q g", q=4)
                    return ap[P_]

                for c in range(CS):
                    cur, nxt = st2(c % 2), st2((c + 1) % 2)
                    cB, nB = B("St", c % 2), B("St", (c + 1) % 2)
                    op("dve", lambda h: h.tensor_tensor(out=tA, in0=cur, in1=Abc, op=ALU.mult), r=[cB, B("A4")], w=[B("tA")])
                    op("dve", lambda h: h.tensor_tensor(out=tBs[:], in0=cur, in1=Bbc, op=ALU.mult), r=[cB, B("A4")], w=[B("tBs")])
                    op("dve", lambda h: h.tensor_tensor(out=tN[:, 0, :], in0=tA[:, 0, :], in1=tBs[:, 1, :], op=ALU.subtract), r=[B("tA"), B("tBs")], w=[B("tN", 0)])
                    op("dve", lambda h: h.tensor_tensor(out=tN[:, 1, :], in0=tA[:, 1, :], in1=tBs[:, 0, :], op=ALU.add), r=[B("tA"), B("tBs")], w=[B("tN", 1)])
                    for d_ in range(2):
                        P_ = slice(64 * d_, 64 * d_ + 64)
                        cc = c if d_ == 0 else CS - 1 - c
                        zt = op("dve", lambda h, P_=P_, cc=cc: h.tensor_tensor(out=v2(nxt, P_), in0=v2(tN, P_), in1=zs2(cc, P_), op=ALU.add),
                                r=[B("tN", 0), B("tN", 1), B("ZSz")], w=[nB])
                        op("act", lambda h, P_=P_, cc=cc: h.activation(out=zs2(cc, P_), in_=v2(cur, P_), func=AF.Identity), r=[cB], w=[B("ZSs")], extra=[zt])
                if hf == 0:
                    mod_part2_evac(mbk, mbkB)
                fin = St[CS % 2]
                fB = B("St", CS % 2)
                fB1 = B("St", CS % 2)
                if hf == 0:
                    while kb.nbank % 8 < 4:
                        kb.nbank += 1
                    for ri in range(2):
                        op("act", lambda h, ri=ri: h.activation(out=finS[ri][:], in_=fin[ri], func=AF.Identity), r=[fB, fB1], w=[B("finS", ri)])
                        fv = finS[ri][:].rearrange("p (q g) -> p q g", q=4)
                        bk, bkB = kb.bank()
                        for q in range(4):
                            op("pe", lambda h, q=q, fv=fv: h.transpose(bk[0:64, q * 128:(q + 1) * 128], fv[:, q, :], ident[:]),
                               r=[B("finS", ri), B("ident")], w=[bkB], inc=(q == 3))
                        op("act", lambda h, ri=ri: h.activation(out=nsT[:, ri * 4:ri * 4 + 4, :].rearrange("p a b -> p (a b)"), in_=bk[0:64, :], func=AF.Identity),
                           r=[bkB], w=[B("nsT")])
                    for ri in range(2):
                        for q in range(4):
                            dma("sp", nsout[q, :, ri, :, :].rearrange("d g p -> g d p"), nsT[:, ri * 4 + q, :].rearrange("p (d x) -> p d x", d=2),
                                r=[B("nsT")], w=[B("nsout")])
                if hf == 0:
                    dump("S", ZS[:], ["ZSs"])
                kb.barrier(dma=False)
                if hf == 0:
                    cut(5)

            with ExitStack() as ysc:
                Yb = [sb("Yb%d" % i, [128, 8, 128], BF16, ysc) for i in range(2)]
                xrl = [sb("xr%d" % i, [128, 1024], BF16, ysc) for i in range(2)]
                for gb in range(8):
                    sw = s5w[gb % 2]
                    swB = B("s5w", gb % 2)
                    dma("sp", sw[:, 0], w0b_d[:, 8 * gb:8 * gb + 8, :], r=[B("w0b_d")], w=[swB])
                    dma("sp", sw[:, 1:3], wyb_d[:, :, 8 * gb:8 * gb + 8, :], r=[B("wyb_d")], w=[swB])
                    yb = Yb[gb % 2]
                    ybB = B("Yb", gb % 2)
                    for g4 in range(2):
                        bk, bkB = kb.bank()
                        for gg in range(4):
                            gl = 4 * g4 + gg
                            g = 8 * gb + gl
                            cs_ = slice(gg * 128, (gg + 1) * 128)
                            op("pe", lambda h, gl=gl, g=g, cs_=cs_: h.matmul(bk[:, cs_], lhsT=sw[:, 0, gl, :], rhs=U[:, g, :], start=True, stop=False),
                               r=[swB, B("U")], w=[bkB], inc=False)
                            op("pe", lambda h, gl=gl, g=g, cs_=cs_: h.matmul(bk[:, cs_], lhsT=sw[:, 1, gl, :], rhs=ZS[:, 0, :, g], start=False, stop=False),
                               r=[swB, B("ZSs")], w=[bkB], inc=False)
                            op("pe", lambda h, gl=gl, g=g, cs_=cs_: h.matmul(bk[:, cs_], lhsT=sw[:, 2, gl, :], rhs=ZS[:, 1, :, g], start=False, stop=True),
                               r=[swB, B("ZSs")], w=[bkB], inc=True)
                        op("dve", lambda h, g4=g4: h.tensor_copy(out=yb[:, 4 * g4:4 * g4 + 4, :].rearrange("p a b -> p (a b)"), in_=bk[:]),
                           r=[bkB], w=[ybB])
                    ct = gb
                    bk, bkB = kb.bank()
                    bkh = bk[:].bitcast(BF16)
                    for gl in range(8):
                        op("pe", lambda h, gl=gl: h.transpose(bkh[:, gl * 128:(gl + 1) * 128], yb[:, gl, :], identb[:]),
                           r=[ybB, B("identb")], w=[bkB], inc=(gl == 7))
                    xr = xrl[gb % 2]
                    xrB = B("xr", gb % 2)
                    op("dve", lambda h: h.tensor_copy(out=xr[:].rearrange("p (j g o) -> p j g o", j=8, g=8), in_=bkh.rearrange("p (g j o) -> p j g o", g=8, j=8)),
                       r=[bkB], w=[xrB])
                    bk2, bk2B = kb.bank()
                    bk2h = bk2[:].bitcast(BF16)
                    for j in range(8):
                        op("pe", lambda h, j=j: h.transpose(bk2h[:, j * 128:(j + 1) * 128], xr[:, j * 128:(j + 1) * 128], identb[:]),
                           r=[xrB, B("identb")], w=[bk2B], inc=(j == 7))
                    op("act", lambda h, ct=ct: h.activation(out=vT[:, ct, :].rearrange("p (n j) -> p n j", j=8), in_=bk2h.rearrange("p (j n) -> p n j", j=8), func=AF.Gelu_apprx_tanh),
                       r=[bk2B], w=[B("vT", 0), B("vT", 1)])
                if hf == 0:
                    dump("vT", vT[:], [("vT", 0), ("vT", 1)])
                kb.barrier()
                if hf == 0:
                    cut(6)

        with ExitStack() as m1:
            vgT = sb("vgT", [128, 8, 1024], BF16, m1)
            mgT = sb("mgT", [128, 8, 1024], BF16, m1)
            hsc = ExitStack()
            pbT = sb("pbT", [128, 4, 1024], BF16, hsc)
            sig = [sb("sig%d" % i, [128, 512], F32, hsc) for i in range(2)]
            tmpf = [sb("tmpf%d" % i, [128, 512], F32, hsc) for i in range(2)]
            wsl.append(sb("wslx", [128, 4096], BF16, hsc))
            wp.free.append(NW)
            wp.pump()
            with ExitStack() as psc:
                ubt = [sb("ubt", [128, 512], BF16, psc) for _ in range(8)]
                pql = [sb("pq", [128, 512], BF16, psc) for _ in range(2)]
                wv, wb = wp.get((hf, "ub"))
                wpv, wpb = wp.get((hf, "wpool"))
                for tb in range(8):
                    bk, bkB = kb.bank()
                    for k in range(8):
                        op("pe", lambda h, k=k, tb=tb: h.matmul(bk[:], lhsT=hT[:, k, tb * 128:(tb + 1) * 128], rhs=wv[:, k, :], start=(k == 0), stop=(k == 7)),
                           r=[wb, hTB(tb // 4)], w=[bkB], inc=(k == 7))
                    op("dve", lambda h, tb=tb: h.tensor_tensor(out=ubt[tb][:], in0=bk[:], in1=bub[:], op=ALU.add), r=[bkB, B("bub")], w=[B("ubt", tb)])
                wp.release((hf, "ub"))
                pit = 0
                for gi in range(4):
                    gs_ = slice(gi * 128, (gi + 1) * 128)
                    for th in range(2):
                        par = pit % 2
                        pit += 1
                        pq, pqB = pql[par], B("pq", par)
                        bk, bkB = kb.bank()
                        for t4 in range(4):
                            tb = th * 4 + t4
                            cs_ = slice(t4 * 128, (t4 + 1) * 128)
                            if hf == 1:
                                op("pe", lambda h, tb=tb, cs_=cs_: h.matmul(bk[:, cs_], lhsT=ubt[tb][:, gs_], rhs=bandS[:, gi, :], start=True, stop=True),
                                   r=[B("ubt", tb), B("band")], w=[bkB], inc=(t4 == 3))
                            else:
                                k0, k1, ot = (0, 1, tb + 1) if tb % 2 == 0 else (2, 3, tb - 1)
                                op("pe", lambda h, tb=tb, cs_=cs_, k0=k0: h.matmul(bk[:, cs_], lhsT=ubt[tb][:, gs_], rhs=bandP[:, gi, k0, :], start=True, stop=False),
                                   r=[B("ubt", tb), B("band")], w=[bkB], inc=False)
                                op("pe", lambda h, ot=ot, cs_=cs_, k1=k1: h.matmul(bk[:, cs_], lhsT=ubt[ot][:, gs_], rhs=bandP[:, gi, k1, :], start=False, stop=True),
                                   r=[B("ubt", ot), B("band")], w=[bkB], inc=True)
                        op("dve", lambda h: h.tensor_copy(out=pq[:], in_=bk[:]), r=[bkB], w=[pqB])
                        bk2, bk2B = kb.bank()
                        op("pe", lambda h, gi=gi: h.matmul(bk2[:], lhsT=wpv[:, gi, :], rhs=pq[:], start=True, stop=True), r=[wpb, pqB], w=[bk2B])
                        op("act", lambda h, gi=gi, th=th: h.activation(out=pbT[:, gi, th * 512:(th + 1) * 512], in_=bk2[:], func=AF.Identity, scale=pscT[:, gi:gi + 1]),
                           r=[bk2B, B("pscT")], w=[B("pbT", th)])
                wp.release((hf, "wpool"))
                kb.barrier()
            wsl.append(sb("wslx", [128, 4096], BF16, hsc))
            wp.free.append(NW + 1)
            wp.pump()
            it = 0
            for wt in range(2):
                wv, wb = wp.get((hf, "glu", wt))
                for mm in range(4):
                    m = wt * 4 + mm
                    for th in range(2):
                        bk, bkB = kb.bank()
                        for k in range(8):
                            op("pe", lambda h, k=k, mm=mm, th=th: h.matmul(bk[:], lhsT=wv[:, k, mm * 128:(mm + 1) * 128], rhs=vT[:, k, th * 512:(th + 1) * 512],
                                                                          start=(k == 0), stop=(k == 7)),
                               r=[wb, B("vT", th)], w=[bkB], inc=(k == 7))
                        sg, sgB = sig[it % 2], B("sig", it % 2)
                        it += 1
                        op("act", lambda h, m=m: h.activation(out=sg[:], in_=bk[:], func=AF.Sigmoid, bias=bgluT[:, m:m + 1], scale=1.0),
                           r=[bkB, B("bgluT")], w=[sgB])
                        op("dve", lambda h, m=m, th=th: h.tensor_tensor(out=vgT[:, m, th * 512:(th + 1) * 512], in0=vT[:, m, th * 512:(th + 1) * 512], in1=sg[:], op=ALU.mult),
                           r=[sgB, B("vT", th)], w=[B("vgT", th)])
                wp.release((hf, "glu", wt))
            if hf == 0:
                dump("vgT", vgT[:], [("vgT", 0), ("vgT", 1)])
                cut(7)
            if hf == 0:
                dump("pbT", pbT[:], [("pbT", 0), ("pbT", 1)])
                cut(8)
            for wt in range(2):
                wpbv, wpbB = wp.get((hf, "pb", wt))
                wpa_v, wpaB = wp.get((hf, "pa", wt))
                wga_v, wgaB = wp.get((hf, "ga", wt))
                wgb_v, wgbB = wp.get((hf, "gb", wt))
                for mm in range(4):
                    m = wt * 4 + mm
                    ms = slice(mm * 128, (mm + 1) * 128)
                    for th in range(2):
                        ts_ = slice(th * 512, (th + 1) * 512)
                        bya, byaB = kb.bank()
                        bga, bgaB = kb.bank()
                        byb, bybB = kb.bank()
                        bgb, bgbB = kb.bank()
                        for k in range(8):
                            op("pe", lambda h, k=k: h.matmul(bya[:], lhsT=wpa_v[:, k, ms], rhs=vgT[:, k, ts_], start=(k == 0), stop=(k == 7)),
                               r=[wpaB, B("vgT", th)], w=[byaB], inc=(k == 7))
                        for k in range(8):
                            op("pe", lambda h, k=k: h.matmul(bga[:], lhsT=wga_v[:, k, ms], rhs=hT[:, k, ts_], start=(k == 0), stop=(k == 7)),
                               r=[wgaB, hTB(th)], w=[bgaB], inc=(k == 7))
                        for k in range(4):
                            op("pe", lambda h, k=k: h.matmul(byb[:], lhsT=wpbv[:, k, ms], rhs=pbT[:, k, ts_], start=(k == 0), stop=(k == 3)),
                               r=[wpbB, B("pbT", th)], w=[bybB], inc=(k == 3))
                        for k in range(8):
                            op("pe", lambda h, k=k: h.matmul(bgb[:], lhsT=wgb_v[:, k, ms], rhs=hT[:, k, ts_], start=(k == 0), stop=(k == 7)),
                               r=[wgbB, hTB(th)], w=[bgbB], inc=(k == 7))
                        op("act", lambda h: h.activation(out=sig[0][:], in_=bga[:], func=AF.Sigmoid, bias=binT[:, 12 + m:13 + m], scale=1.0),
                           r=[bgaB, B("binT")], w=[B("sig", 0)])
                        op("act", lambda h: h.activation(out=sig[1][:], in_=bgb[:], func=AF.Sigmoid, bias=binT[:, 20 + m:21 + m], scale=1.0),
                           r=[bgbB, B("binT")], w=[B("sig", 1)])
                        op("dve", lambda h: h.tensor_tensor(out=tmpf[0][:], in0=bya[:], in1=sig[0][:], op=ALU.mult), r=[byaB, B("sig", 0)], w=[B("tmpf", 0)])
                        op("dve", lambda h: h.tensor_tensor(out=tmpf[1][:], in0=byb[:], in1=sig[1][:], op=ALU.mult), r=[bybB, B("sig", 1)], w=[B("tmpf", 1)])
                        op("dve", lambda h: h.tensor_tensor(out=mgT[:, m, ts_], in0=tmpf[0][:], in1=tmpf[1][:], op=ALU.add),
                           r=[B("tmpf", 0), B("tmpf", 1)], w=[B("mgT", th)])
                for nm_ in ("pb", "pa", "ga", "gb"):
                    wp.release((hf, nm_, wt))
            if hf == 0:
                dump("mgT", mgT[:], [("mgT", 0), ("mgT", 1)])
                cut(9)
            kb.barrier()
            assert NW in wp.free and NW + 1 in wp.free
            wp.free.remove(NW)
            wp.free.remove(NW + 1)
            del wsl[NW:]
            hsc.close()
            with ExitStack() as jsc:
                tmT = [sb("tmT", [128, 8, 512], F32, jsc),
                       vgT[:].rearrange("p a b -> p (a b)").bitcast(F32).rearrange("p (a b) -> p a b", a=8)]
                rtl = [sb("rtile", [128, D], F32, jsc) for _ in range(DEPTH)]
                x1l = [sb("x1t", [128, D], F32, jsc) for _ in range(DEPTH)]
                wo = [wp.get((hf, "out", wt)) for wt in range(2)]
                for th in range(2):
                    for m in range(8):
                        wv, wb = wo[m // 4]
                        mm = m % 4
                        bk, bkB = kb.bank()
                        for k in range(8):
                            op("pe", lambda h, k=k, mm=mm, wv=wv: h.matmul(bk[:], lhsT=wv[:, k, mm * 128:(mm + 1) * 128], rhs=mgT[:, k, th * 512:(th + 1) * 512],
                                                                          start=(k == 0), stop=(k == 7)),
                               r=[wb, B("mgT", th)], w=[bkB], inc=(k == 7))
                        op("act", lambda h, m=m, th=th: h.activation(out=tmT[th][:, m, :], in_=bk[:], func=AF.Identity, scale=gt1[:, m, v:v + 1], bias=gt1b[:, m, v:v + 1]),
                           r=[bkB, B("mod2")], w=[B("tmT", th)])
                wp.release((hf, "out", 0))
                wp.release((hf, "out", 1))
                def chainJ(tb):
                    th, t4 = divmod(tb, 4)
                    par = tb % DEPTH
                    xs, xB = xsl[par], B("xs", par)
                    rt_, rB_ = rtl[par], B("rtile", par)
                    x1_, x1B_ = x1l[par], B("x1t", par)
                    dma("sp", xs[:], xin[row0 + tb * 128:row0 + (tb + 1) * 128, :], w=[xB])
                    for h2 in range(2):
                        bk, bkB = kb.bank()
                        for j in range(4):
                            ct = h2 * 4 + j
                            op("pe", lambda h, ct=ct, j=j: h.transpose(bk[:, j * 128:(j + 1) * 128], tmT[th][:, ct, t4 * 128:(t4 + 1) * 128], ident[:]),
                               r=[B("tmT", th), B("ident")], w=[bkB], inc=(j == 3))
                        op("dve", lambda h, h2=h2, bk=bk: h.scalar_tensor_tensor(out=rt_[:, h2 * 512:(h2 + 1) * 512], in0=xs[:, h2 * 512:(h2 + 1) * 512], scalar=ALPHA,
                                                                                 in1=bk[:], op0=ALU.mult, op1=ALU.add),
                           r=[bkB, xB], w=[rB_])
                    yield
                    i = ln_part1(rt_, rB_)
                    yield
                    ln_part2(i, x1_[:], rt_[:], rB_, x1B_)
                    yield
                    op("dve", lambda h: h.tensor_tensor(out=x1_[:], in0=x1_[:], in1=lnp[:, 0, :], op=ALU.mult), r=[x1B_, B("lnp")], w=[x1B_])
                    op("dve", lambda h: h.tensor_tensor(out=x1_[:], in0=x1_[:], in1=lnp[:, 1, :], op=ALU.add), r=[x1B_, B("lnp")], w=[x1B_])
                    dma("sp", x1s[row0 + tb * 128:row0 + (tb + 1) * 128, :], x1_[:], r=[x1B_], w=[B("x1s", hf, tb)])
                    i = ln_part1(x1_, x1B_)
                    yield
                    ln_part2(i, rt_[:], x1_[:], x1B_, rB_)
                    yield
                    to_feature_major(rt_, rB_, h2T, lambda th_: B("vT", th_), tb, sc2p, sh2, v)
                run_pipelined([chainJ(tb) for tb in range(8)], DEPTH)
                kb.barrier()
            if hf == 0:
                dump("h2T", h2T[:], [("vT", 0), ("vT", 1)])
            kb.barrier()
            if hf == 0:
                cut(10)

        with ExitStack() as m2:
            hid = sb("hid", [128, 32, 1024], BF16, m2)
            rl = [sb("rl%d" % i, [128, 512], F32, m2) for i in range(2)]
            fT = hT[:].rearrange("p a b -> p (a b)").bitcast(F32).rearrange("p (a b) -> p a b", a=8)
            rtl2 = [sb("rtile2", [128, D], F32, m2) for _ in range(DEPTH)]
            it = 0
            for wt in range(8):
                wv, wb = wp.get((hf, "w1", wt))
                for mm in range(4):
                    m = wt * 4 + mm
                    for th in range(2):
                        bk, bkB = kb.bank()
                        for k in range(8):
                            op("pe", lambda h, k=k, mm=mm, th=th: h.matmul(bk[:], lhsT=wv[:, k, mm * 128:(mm + 1) * 128], rhs=h2T[:, k, th * 512:(th + 1) * 512],
                                                                          start=(k == 0), stop=(k == 7)),
                               r=[wb, B("vT", th)], w=[bkB], inc=(k == 7))
                        r_, rB = rl[it % 2], B("rl", it % 2)
                        it += 1
                        op("act", lambda h, m=m: h.activation(out=r_[:], in_=bk[:], func=AF.Relu, bias=b1T[:, m:m + 1], scale=1.0),
                           r=[bkB, B("b1T")], w=[rB])
                        op("dve", lambda h, m=m, th=th: h.tensor_tensor(out=hid[:, m, th * 512:(th + 1) * 512], in0=r_[:], in1=r_[:], op=ALU.mult),
                           r=[rB], w=[B("hid", th)])
                wp.release((hf, "w1", wt))
            fTl = [fT, h2T[:].rearrange("p a b -> p (a b)").bitcast(F32).rearrange("p (a b) -> p a b", a=8)]
            def mlp2_group(th, m):
                wv, wb = wp.get((hf, "w2", th, m))
                bk, bkB = kb.bank()
                for k in range(32):
                    op("pe", lambda h, k=k: h.matmul(bk[:], lhsT=wv[:, k, :], rhs=hid[:, k, th * 512:(th + 1) * 512], start=(k == 0), stop=(k == 31)),
                       r=[wb, B("hid", th)], w=[bkB], inc=(k == 31))
                op("act", lambda h: h.activation(out=fTl[th][:, m, :], in_=bk[:], func=AF.Identity, scale=gt2[:, m, v:v + 1], bias=gt2b[:, m, v:v + 1]),
                   r=[bkB, B("mod2")], w=[B("fT", th), B("vT", 0), B("vT", 1), B("hT", 0), B("hT", 1)])
                wp.release((hf, "w2", th, m))
                if hf == 0 and th == 1 and m == 4:
                    wp.plan(half_plan(1))
            for m in range(8):
                mlp2_group(0, m)
            def chainF(tb):
                th, t4 = divmod(tb, 4)
                par = tb % DEPTH
                xs, xB = xsl[par], B("xs", par)
                rt_, rB_ = rtl2[par], B("rtile2", par)
                dma("sp", xs[:], x1s[row0 + tb * 128:row0 + (tb + 1) * 128, :], r=[B("x1s", hf, tb)], w=[xB])
                for h2 in range(2):
                    bk, bkB = kb.bank()
                    for j in range(4):
                        ct = h2 * 4 + j
                        op("pe", lambda h, ct=ct, j=j: h.transpose(bk[:, j * 128:(j + 1) * 128], fTl[th][:, ct, t4 * 128:(t4 + 1) * 128], ident[:]),
                           r=[B("fT", th), B("ident")], w=[bkB], inc=(j == 3))
                    op("dve", lambda h, h2=h2, bk=bk: h.scalar_tensor_tensor(out=rt_[:, h2 * 512:(h2 + 1) * 512], in0=xs[:, h2 * 512:(h2 + 1) * 512], scalar=ALPHA,
                                                                             in1=bk[:], op0=ALU.mult, op1=ALU.add),
                       r=[bkB, xB], w=[rB_])
                yield
                i = ln_part1(rt_, rB_)
                yield
                ln_part2(i, rt_[:], rt_[:], rB_, rB_)
                yield
                op("dve", lambda h: h.tensor_tensor(out=rt_[:], in0=rt_[:], in1=lnp[:, 2, :], op=ALU.mult), r=[rB_, B("lnp")], w=[rB_])
                op("dve", lambda h: h.tensor_tensor(out=rt_[:], in0=rt_[:], in1=lnp[:, 3, :], op=ALU.add), r=[rB_, B("lnp")], w=[rB_])
                dma("sp", yout[row0 + tb * 128:row0 + (tb + 1) * 128, :], rt_[:], r=[rB_], w=[B("yout")])
            todo = list(range(8))

            def fill():
                for _ in range(2):
                    if todo:
                        mlp2_group(1, todo.pop(0))
            run_pipelined([chainF(tb) for tb in range(4)], DEPTH, filler=fill)
            while todo:
                mlp2_group(1, todo.pop(0))
            run_pipelined([chainF(tb) for tb in range(4, 8)], DEPTH)
            kb.barrier()
            if hf == 0:
                cut(11)

    kb.barrier(final=True)


_CACHE = {}


def _consts():
    pcm = np.zeros((128, 8, 240), np.float32)
    for a in range(8):
        for h in range(16):
            pcm[16 * a + h, a, 112 + h] = 1.0
    i_of = np.arange(128) // 16
    maskf = (i_of[:, None] <= i_of[None, :]).astype(np.float32)
    maskb = (i_of[:, None] >= i_of[None, :]).astype(np.float32)
    ident = np.eye(128, dtype=np.float32)

    def pm(n):
        out = np.zeros((4, n, n), np.float64)
        t = np.arange(n)
        for gi, w in enumerate((2, 4, 8, 16)):
            lo = np.clip(t - w // 2, 0, n)
            hi = np.clip(t - w // 2 + w, 0, n)
            for to in range(n):
                out[gi, lo[to]:hi[to], to] = 1.0 / float(hi[to] - lo[to])
                out[gi, to, to] -= 1.0
        return out
    P256 = pm(256)
    bandp = np.zeros((128, 4, 4, 128), np.float32)
    for gi in range(4):
        bandp[:, gi, 0, :] = P256[gi, 0:128, 0:128]
        bandp[:, gi, 1, :] = P256[gi, 128:256, 0:128]
        bandp[:, gi, 2, :] = P256[gi, 128:256, 128:256]
        bandp[:, gi, 3, :] = P256[gi, 0:128, 128:256]
    P64 = pm(64)
    bands = np.zeros((128, 4, 128), np.float32)
    for gi in range(4):
        bands[0:64, gi, 0:64] = P64[gi]
        bands[64:128, gi, 64:128] = P64[gi]
    return {
        "c_pc": pcm.reshape(128, 8 * 240), "c_maskf": maskf, "c_maskb": maskb, "c_ident": ident,
        "c_bandp": bandp.reshape(128, -1), "c_bands": bands.reshape(128, -1),
    }


def kernel(x_prompt, x_sample, state_s5, c, c_ctx, w_ada, b_ada, w_in, b_in,
           s5_lam_re, s5_lam_im, s5_log_dt, s5_b_re, s5_b_im, s5_c_re, s5_c_im, s5_d,
           w_glu, b_glu, w_proj_a, w_pool, pool_scale, w_proj_b, w_out, b_out,
           ln1_g, ln1_b, w_mlp1, b_mlp1, w_mlp2, b_mlp2, ln2_g, ln2_b):
    f = lambda a: np.ascontiguousarray(np.asarray(a, dtype=np.float32))
    if "nc" not in _CACHE:
        _CACHE["nc"] = build_program()
    nc = _CACHE["nc"]
    xp = f(x_prompt)
    xs = f(x_sample)
    st = f(state_s5)
    cc = f(c)
    shared = {
        "w_ada": f(w_ada)[0], "b_ada": f(b_ada)[0], "w_in": f(w_in)[0], "b_in": f(b_in)[0],
        "lam_re": f(s5_lam_re)[0], "lam_im": f(s5_lam_im)[0], "log_dt": f(s5_log_dt)[0],
        "b_re": f(s5_b_re)[0], "b_im": f(s5_b_im)[0], "c_re": f(s5_c_re)[0], "c_im": f(s5_c_im)[0],
        "s5_d": f(s5_d)[0], "w_glu": f(w_glu)[0], "b_glu": f(b_glu)[0], "w_pa": f(w_proj_a)[0],
        "w_pool": f(w_pool)[0], "pool_scale": f(pool_scale)[0], "w_pb": f(w_proj_b)[0],
        "w_out": f(w_out)[0], "b_out": f(b_out)[0], "ln1_g": f(ln1_g), "ln1_b": f(ln1_b),
        "w1": f(w_mlp1)[0], "b1": f(b_mlp1)[0], "w2": f(w_mlp2)[0], "b2": f(b_mlp2)[0],
        "ln2_g": f(ln2_g), "ln2_b": f(ln2_b),
    }
    shared.update(_consts())
    in_maps = []
    for i in range(NCORES):
        m = dict(shared)
        m["xin"] = np.concatenate([xp[4 * i:4 * i + 4].reshape(1024, D), xs[i]], axis=0)
        m["st0"] = np.ascontiguousarray(st[i, 0])
        m["cvec"] = np.stack([f(c_ctx), cc[i]], axis=0)
        in_maps.append(m)
    res = run_bass_kernel_spmd(nc, in_maps, core_ids=list(range(NCORES)))
    yp = np.zeros((32, 256, D), np.float32)
    ys = np.zeros((8, 1024, D), np.float32)
    ns = np.zeros((32, 1, 2, 2, 64, 64), np.float32)
    for i in range(NCORES):
        r = res.results[i]
        yo = np.asarray(r["yout"])
        yp[4 * i:4 * i + 4] = yo[0:1024].reshape(4, 256, D)
        ys[i] = yo[1024:2048]
        ns[4 * i:4 * i + 4, 0] = np.asarray(r["nsout"])
    return yp, ys, ns
```

```python
import math
from contextlib import ExitStack
import numpy as np
import concourse.bass as bass
import concourse.mybir as mybir
from concourse.bass_utils import run_bass_kernel_spmd

F32 = mybir.dt.float32
BF16 = mybir.dt.bfloat16
AF = mybir.ActivationFunctionType
ALU = mybir.AluOpType
NCORES = 8
D = 1024
ALPHA = 2.0 ** 0.25
EPS = 1e-6
PI = math.pi


class Tok:
    __slots__ = ("sem", "val", "eng", "key")

    def __init__(self, sem, val, eng, key):
        self.sem, self.val, self.eng, self.key = sem, val, eng, key


class Buf:
    __slots__ = ("w", "r")

    def __init__(self):
        self.w = {}
        self.r = {}


class Eng:
    def __init__(self, name, h, sem):
        self.name, self.h, self.sem = name, h, sem
        self.n = 0
        self.waited = {}


class KB:
    def __init__(self, nc, es):
        self.nc = nc
        self.es = es
        self.E = {}
        for name, h in (("pe", nc.tensor), ("act", nc.scalar), ("dve", nc.vector),
                        ("pool", nc.gpsimd), ("sp", nc.sync)):
            self.E[name] = Eng(name, h, es.enter_context(nc.semaphore("sem_" + name)))
        self.rings = {}
        for q in ("sp", "pool"):
            self.rings[q] = [[es.enter_context(nc.semaphore("dq_%s_%d" % (q, i))), 0, None]
                             for i in range(12)]
        self.ring_pos = {"sp": 0, "pool": 0}
        self.bufs = {}
        self.nbank = 0

    def B(self, *key):
        b = self.bufs.get(key)
        if b is None:
            b = self.bufs[key] = Buf()
        return b

    def _wait(self, e, t):
        if t is None:
            return
        if t.eng is e and e.name in ("pe", "sp"):
            return
        if e.waited.get(t.key, 0) >= t.val:
            return
        e.h.wait_ge(t.sem, t.val)
        e.waited[t.key] = t.val

    def _deps(self, e, r, w, extra):
        for b in r:
            for t in b.w.values():
                self._wait(e, t)
        for b in w:
            for t in list(b.w.values()) + list(b.r.values()):
                self._wait(e, t)
        for t in extra:
            self._wait(e, t)

    def _post(self, tok, r, w):
        for b in r:
            b.r[tok.key] = tok
        for b in w:
            b.w[tok.key] = tok
            b.r = {}

    def op(self, en, fn, r=(), w=(), inc=True, extra=()):
        e = self.E[en]
        self._deps(e, r, w, extra)
        inst = fn(e.h)
        if inc:
            e.n += 1
            inst.then_inc(e.sem, 1)
            tok = Tok(e.sem, e.n, e, en)
        else:
            tok = Tok(e.sem, e.n + 1, e, en)
        self._post(tok, r, w)
        return tok

    def dma(self, q, out, in_, r=(), w=(), extra=(), **kw):
        e = self.E[q]
        self._deps(e, r, w, extra)
        ring = self.rings[q]
        i = self.ring_pos[q]
        self.ring_pos[q] = (i + 1) % len(ring)
        slot = ring[i]
        self._wait(e, slot[2])
        inst = e.h.dma_start(out=out, in_=in_, **kw)
        slot[1] += 16
        inst.then_inc(slot[0], 16)
        tok = Tok(slot[0], slot[1], None, "d%s%d" % (q, i))
        slot[2] = tok
        self._post(tok, r, w)
        return tok

    def barrier(self, final=False, dma=True):
        toks = []
        for e in self.E.values():
            if e.n > 0:
                toks.append(Tok(e.sem, e.n, e, e.name))
        for q, ring in self.rings.items():
            if not dma:
                continue
            if q == "pool" and not final:
                continue
            for slot in ring:
                if slot[2] is not None:
                    toks.append(slot[2])
        for e in self.E.values():
            for t in toks:
                if t.eng is e:
                    continue
                self._wait(e, t)

    def bank(self):
        i = self.nbank % 8
        self.nbank += 1
        return self.ps[i], self.B("ps", i)


class Stop(Exception):
    pass


LIMIT = None
NOREV = False
DUMPS = []


def build_program():
    nc = bass.Bass("TRN2", target_bir_lowering=False)
    es = ExitStack()
    try:
        _build(nc, es)
    except Stop:
        return nc
    es.close()
    return nc


def _build(nc, es):

    def din(name, shape):
        return nc.dram_tensor(name, list(shape), F32, kind="ExternalInput").ap()

    xin = din("xin", [2048, D])
    st0 = din("st0", [2, 2, 64, 64])
    cvec = din("cvec", [2, D])
    w_ada = din("w_ada", [D, 6 * D])
    b_ada = din("b_ada", [6 * D])
    w_in = din("w_in", [D, 3584])
    b_in = din("b_in", [3584])
    lam_re = din("lam_re", [2, 64, 64])
    lam_im = din("lam_im", [2, 64, 64])
    log_dt = din("log_dt", [2, 64])
    b_re = din("b_re", [2, 64, 64, 16])
    b_im = din("b_im", [2, 64, 64, 16])
    c_re = din("c_re", [2, 64, 16, 64])
    c_im = din("c_im", [2, 64, 16, 64])
    s5_d = din("s5_d", [2, D])
    w_glu = din("w_glu", [D, D])
    b_glu = din("b_glu", [D])
    w_pa = din("w_pa", [D, D])
    w_pool = din("w_pool", [4, 128, 128])
    pool_scale = din("pool_scale", [512])
    w_pb = din("w_pb", [512, D])
    w_out = din("w_out", [D, D])
    b_out = din("b_out", [D])
    ln1_g = din("ln1_g", [1, D])
    ln1_b = din("ln1_b", [1, D])
    w1 = din("w1", [D, 4 * D])
    b1 = din("b1", [4 * D])
    w2 = din("w2", [4 * D, D])
    b2 = din("b2", [D])
    ln2_g = din("ln2_g", [1, D])
    ln2_b = din("ln2_b", [1, D])
    c_pc = din("c_pc", [128, 8 * 240])
    c_maskf = din("c_maskf", [128, 128])
    c_maskb = din("c_maskb", [128, 128])
    c_ident = din("c_ident", [128, 128])
    c_bandp = din("c_bandp", [128, 4 * 4 * 128])
    c_bands = din("c_bands", [128, 4 * 128])

    yout = nc.dram_tensor("yout", [2048, D], F32, kind="ExternalOutput").ap()
    nsout = nc.dram_tensor("nsout", [4, 2, 2, 64, 64], F32, kind="ExternalOutput").ap()
    x1s = nc.dram_tensor("x1s", [2048, D], F32, kind="Internal").ap()
    wzt_d = nc.dram_tensor("wzt_d", [128, 2, 64, 128], BF16, kind="Internal").ap()
    wyb_d = nc.dram_tensor("wyb_d", [128, 2, 64, 128], BF16, kind="Internal").ap()
    w0b_d = nc.dram_tensor("w0b_d", [128, 64, 128], BF16, kind="Internal").ap()

    kb = KB(nc, es)
    op, dma, B = kb.op, kb.dma, kb.B

    def cut(n):
        if LIMIT == n:
            kb.barrier()
            raise Stop()

    def dump(name, ap, bname):
        if LIMIT is None:
            return
        shp = list(ap.shape)
        dt_ = nc.dram_tensor("dbg_" + name, shp, ap.dtype, kind="ExternalOutput").ap()
        dma("sp", dt_, ap, r=[(B(*x) if isinstance(x, tuple) else B(x)) for x in bname], w=[B("dbg_" + name)])
        DUMPS.append("dbg_" + name)

    used_names = {}

    def sb(name, shape, dt, scope=es):
        k = used_names.get(name, 0)
        used_names[name] = k + 1
        if k:
            name = "%s_%d" % (name, k)
        return scope.enter_context(nc.sbuf_tensor(name, list(shape), dt))

    kb.ps = [es.enter_context(nc.psum_tensor("psb%d" % i, [128, 512], F32)) for i in range(8)]

    NW = 5
    wsl = [sb("wsl%d" % i, [128, 4096], BF16) for i in range(NW)]
    wstate = {"i": 0}
    pc = sb("pc", [128, 8, 240], BF16)
    ident = sb("ident", [128, 128], F32)
    identb = sb("identb", [128, 128], BF16)
    maskf = sb("maskf", [128, 128], F32)
    maskb = sb("maskb", [128, 128], F32)
    bandP = sb("bandP", [128, 4, 4, 128], BF16)
    bandS = sb("bandS", [128, 4, 128], BF16)
    bub = sb("bub", [128, 512], F32)
    lnp = sb("lnp", [128, 4, D], F32)
    binT = sb("binT", [128, 28], F32)
    badaT = sb("badaT", [128, 48], F32)
    bgluT = sb("bgluT", [128, 8], F32)
    boutT = sb("boutT", [128, 8], F32)
    b1T = sb("b1T", [128, 32], F32)
    b2T = sb("b2T", [128, 8], F32)
    pscT = sb("pscT", [128, 4], F32)
    modT = sb("modT", [128, 48, 2], F32)
    mder = sb("mder", [128, 4, 8, 2], F32)
    A4 = sb("A4", [128, 4, 64], F32)
    B4 = sb("B4", [128, 4, 64], F32)
    s0t = sb("s0t", [128, 2, 64], F32)
    neghalf = sb("neghalf", [128, 1], F32)
    DEPTH = 4
    xsl = [sb("xsl%d" % i, [128, D], F32) for i in range(DEPTH)]
    NLN = 8
    lnst = [sb("lnst", [128, 2, 6], F32) for _ in range(NLN)]
    lnmv = [sb("lnmv", [128, 2], F32) for _ in range(NLN)]
    lnve = [sb("lnve", [128, 1], F32) for _ in range(NLN)]
    lnrs = [sb("lnrs", [128, 1], F32) for _ in range(NLN)]
    lnnb = [sb("lnnb", [128, 1], F32) for _ in range(NLN)]
    lnstate = {"i": 0}
    class WP:
        def __init__(self):
            self.free = list(range(NW))
            self.queue = []
            self.issued = {}
            self.hist = []

        def plan(self, specs):
            self.queue.extend(specs)
            self.pump()

        def pump(self):
            while self.free and self.queue:
                key = self.queue[0][0]
                merge = len(key) > 1 and key[1] in ("pb", "pa", "ga", "gb")
                cand = [i for i in self.free if i < NW or merge]
                if not cand:
                    break
                key, src, r0, nrows, c0, ncols = self.queue.pop(0)
                i = cand[0]
                self.free.remove(i)
                kc = nrows // 128
                view = wsl[i][:, 0:kc * ncols].rearrange("p (k c) -> p k c", k=kc)
                wb = B("wsl", i)
                if len(self.hist) >= 2:
                    kb._wait(kb.E["pool"], self.hist[-2])
                tok_ = dma("pool", view, src[r0:r0 + nrows, c0:c0 + ncols].rearrange("(k p) c -> p k c", p=128), w=[wb])
                self.hist.append(tok_)
                self.issued[key] = (view, wb, i)

        def get(self, key):
            self.pump()
            assert key in self.issued, ("weight tile not issued (no free slot)", key, self.free, [q[0] for q in self.queue[:3]])
            v_, b_, _ = self.issued[key]
            return v_, b_

        def release(self, key):
            _, _, i = self.issued.pop(key)
            self.free.append(i)
            self.pump()

    wp = WP()

    def ln_part1(src, srcB):
        i = lnstate["i"] % NLN
        lnstate["i"] += 1
        st, mv, ve = lnst[i], lnmv[i], lnve[i]
        stB, mvB, veB = B("lnst", i), B("lnmv", i), B("lnve", i)
        op("dve", lambda h: h.bn_stats(out=st[:, 0, :], in_=src[:, 0:512]), r=[srcB], w=[stB])
        op("dve", lambda h: h.bn_stats(out=st[:, 1, :], in_=src[:, 512:1024]), r=[srcB], w=[stB])
        op("dve", lambda h: h.bn_aggr(out=mv[:], in_=st[:].rearrange("p a b -> p (a b)")), r=[stB], w=[mvB])
        op("act", lambda h: h.activation(out=ve[:], in_=mv[:, 1:2], func=AF.Sqrt, bias=epsT[:, 0:1], scale=1.0), r=[mvB, B("epsT")], w=[veB])
        return i

    def ln_part2(i, dst, src, srcB, dstB):
        mv, ve, rs, nb = lnmv[i], lnve[i], lnrs[i], lnnb[i]
        mvB, veB, rsB, nbB = B("lnmv", i), B("lnve", i), B("lnrs", i), B("lnnb", i)
        op("dve", lambda h: h.reciprocal(out=rs[:], in_=ve[:]), r=[veB], w=[rsB])
        op("dve", lambda h: h.scalar_tensor_tensor(out=nb[:], in0=mv[:, 0:1], scalar=-1.0, in1=rs[:], op0=ALU.mult, op1=ALU.mult),
           r=[mvB, rsB], w=[nbB])
        op("act", lambda h: h.activation(out=dst, in_=src, func=AF.Identity, scale=rs[:, 0:1], bias=nb[:, 0:1]),
           r=[srcB, rsB, nbB], w=[dstB])

    def run_pipelined(gens, depth, filler=None):
        pending = list(gens)
        active = []
        while pending or active:
            while pending and len(active) < depth:
                active.append(pending.pop(0))
            for g in list(active):
                try:
                    next(g)
                except StopIteration:
                    active.remove(g)
            if filler is not None:
                filler()

    def to_feature_major(src, srcB, dstT, dstB_fn, tb, scale_ap, bias_ap, v):
        for hf in range(2):
            bk, bkB = kb.bank()
            for j in range(4):
                ct = hf * 4 + j
                op("pe", lambda h, ct=ct, j=j: h.transpose(bk[:, j * 128:(j + 1) * 128], src[:, ct * 128:(ct + 1) * 128], ident[:]),
                   r=[srcB, B("ident")], w=[bkB], inc=(j == 3))
            for j in range(4):
                ct = hf * 4 + j
                op("act", lambda h, ct=ct, j=j: h.activation(out=dstT[:, ct, tb * 128:(tb + 1) * 128], in_=bk[:, j * 128:(j + 1) * 128],
                                                             func=AF.Identity, scale=scale_ap[:, ct, v:v + 1], bias=bias_ap[:, ct, v:v + 1]),
                   r=[bkB, B("mod"), B("mod2")], w=[dstB_fn(tb // 4)])

    def ld(dst, src, bname, q="sp", **kw):
        return dma(q, dst, src, w=[B(bname)], **kw)

    ld(ident[:], c_ident, "ident")
    ld(maskf[:], c_maskf, "maskf")
    ld(maskb[:], c_maskb, "maskb")
    op("dve", lambda h: h.memset(neghalf[:], -0.5), w=[B("neghalf")])
    epsT = sb("epsT", [128, 1], F32)
    op("dve", lambda h: h.memset(epsT[:], EPS), w=[B("epsT")])

    NXS = 0
    xsc = ExitStack()
    for _ in range(NXS):
        wsl.append(sb("wslx", [128, 4096], BF16, xsc))
    wp.free = list(range(NW + NXS))

    def half_plan(hf_):
        P = []
        P += [((hf_, "ua", wt), w_in, 0, 1024, 512 * wt, 512) for wt in range(2)]
        P += [((hf_, "ub"), w_in, 0, 1024, 1024, 512), ((hf_, "wpool"), w_pool.rearrange("g i o -> (g i) o"), 0, 512, 0, 128)]
        P += [((hf_, "glu", wt), w_glu, 0, 1024, 512 * wt, 512) for wt in range(2)]
        for wt in range(2):
            P += [((hf_, "pb", wt), w_pb, 0, 512, 512 * wt, 512), ((hf_, "pa", wt), w_pa, 0, 1024, 512 * wt, 512),
                  ((hf_, "ga", wt), w_in, 0, 1024, 1536 + 512 * wt, 512), ((hf_, "gb", wt), w_in, 0, 1024, 2560 + 512 * wt, 512)]
        P += [((hf_, "out", wt), w_out, 0, 1024, 512 * wt, 512) for wt in range(2)]
        P += [((hf_, "w1", wt), w1, 0, 1024, 512 * wt, 512) for wt in range(8)]
        P += [((hf_, "w2", ps_, m_), w2, 0, 4096, 128 * m_, 128) for ps_ in range(2) for m_ in range(8)]
        return P

    with ExitStack() as sc:
        def t64(name):
            return sb(name, [128, 64], F32, sc)
        lre, lim, ldt = t64("lre"), t64("lim"), t64("ldt")
        Bre = sb("Bre", [128, 64, 16], F32, sc)
        Bim = sb("Bim", [128, 64, 16], F32, sc)
        Cre = sb("Cre", [128, 64, 16], F32, sc)
        Cim = sb("Cim", [128, 64, 16], F32, sc)
        dsum = t64("dsum")
        Ln = sb("Ln", [64, 2, 2, 64], F32, sc)
        Sn = sb("Sn", [64, 2, 2, 64], F32, sc)
        Cn = sb("Cn", [128, 2, 8, 2, 64], F32, sc)
        Dn = sb("Dn", [64, 2, 16], F32, sc)
        Dn8 = sb("Dn8", [64, 8, 16], F32, sc)
        dma("sp", Ln[:, 0], lam_re.rearrange("d g p -> g d p"), w=[B("Ln")])
        dma("sp", Ln[:, 1], lam_im.rearrange("d g p -> g d p"), w=[B("Ln")])
        for d in range(2):
            sl = slice(64 * d, 64 * d + 64)
            dma("sp", ldt[sl, :], log_dt[d:d + 1, :].partition_broadcast(64).rearrange("p o f -> p (o f)"), w=[B("ldt")])
        dma("sp", Dn[:], s5_d.rearrange("d (g h) -> g d h", h=16), w=[B("Dn")], allow_slow_non_contiguous=True)
        for r_ in range(2):
            dma("sp", Sn[:, r_], st0[:, r_].rearrange("d g p -> g d p"), w=[B("Sn")])
        for d in range(2):
            sl = slice(64 * d, 64 * d + 64)
            dma("sp", Bre[sl], b_re[d].rearrange("g p h -> p g h"), w=[B("Bre")], allow_slow_non_contiguous=True)
            dma("sp", Bim[sl], b_im[d].rearrange("g p h -> p g h"), w=[B("Bim")], allow_slow_non_contiguous=True)
        for ri_, src_ in enumerate((c_re, c_im)):
            for d in range(2):
                tokC = dma("sp", Cn[:, ri_, :, d, :], src_[d].rearrange("(gh gl) h p -> (gl h) gh p", gl=8), w=[B("Cn")])

        def tr_small(dst, src_ap, nrow, srcB, dstB):
            bk, bkB = kb.bank()
            op("pe", lambda h: h.transpose(bk[:, 0:nrow], src_ap, ident[0:nrow, 0:nrow]), r=[B(srcB), B("ident")], w=[bkB])
            op("act", lambda h: h.activation(out=dst, in_=bk[:, 0:nrow], func=AF.Identity), r=[bkB], w=[B(dstB)])

        tr_small(lre[:], Ln[:, 0].rearrange("g d p -> g (d p)"), 64, "Ln", "lre")
        tr_small(lim[:], Ln[:, 1].rearrange("g d p -> g (d p)"), 64, "Ln", "lim")
        tr_small(s0t[:, 0, :], Sn[:, 0].rearrange("g d p -> g (d p)"), 64, "Sn", "s0t")
        tr_small(s0t[:, 1, :], Sn[:, 1].rearrange("g d p -> g (d p)"), 64, "Sn", "s0t")
        op("dve", lambda h: h.tensor_tensor(out=Dn[:, 0, :], in0=Dn[:, 0, :], in1=Dn[:, 1, :], op=ALU.add), r=[B("Dn")], w=[B("Dn")])
        op("dve", lambda h: h.tensor_copy(out=Dn8[:], in_=Dn[:, 0, :].unsqueeze(1).broadcast_to([64, 8, 16])), r=[B("Dn")], w=[B("Dn8")])
        tr_small(dsum[:], Dn8[:].rearrange("g a b -> g (a b)"), 64, "Dn8", "dsum")
        for ri_, dstC in enumerate((Cre, Cim)):
            for gh4 in range(2):
                bk, bkB = kb.bank()
                for j in range(4):
                    gh = gh4 * 4 + j
                    op("pe", lambda h, j=j, gh=gh, ri_=ri_: h.transpose(bk[:, j * 128:(j + 1) * 128], Cn[:, ri_, gh].rearrange("p d x -> p (d x)"), ident[:]),
                       r=[B("Cn"), B("ident")], w=[bkB], inc=(j == 3))
                op("act", lambda h, gh4=gh4, dstC=dstC: h.activation(out=dstC[:, 32 * gh4:32 * gh4 + 32, :].rearrange("p a b -> p (a b)"), in_=bk[:], func=AF.Identity),
                   r=[bkB], w=[B("Cre"), B("Cim")])

        ld(pc[:].rearrange("p a b -> p (a b)"), c_pc, "pc", q="pool")
        ld(identb[:], c_ident, "identb", q="pool")
        ld(bandP[:].rearrange("p a b c -> p (a b c)"), c_bandp, "band", q="pool")
        ld(bandS[:].rearrange("p a b -> p (a b)"), c_bands, "band", q="pool")
        ld(bub[:], b_in[1024:1536].partition_broadcast(128), "bub")
        for i, src in enumerate((ln1_g, ln1_b, ln2_g, ln2_b)):
            ld(lnp[:, i, :], src.partition_broadcast(128).rearrange("p o f -> p (o f)"), "lnp")
        ld(binT[:], b_in.rearrange("(j p) -> p j", p=128), "binT", allow_slow_non_contiguous=True)
        ld(badaT[:], b_ada.rearrange("(j p) -> p j", p=128), "badaT", allow_slow_non_contiguous=True)
        ld(bgluT[:], b_glu.rearrange("(j p) -> p j", p=128), "bgluT", allow_slow_non_contiguous=True)
        ld(boutT[:], b_out.rearrange("(j p) -> p j", p=128), "boutT", allow_slow_non_contiguous=True)
        ld(b1T[:], b1.rearrange("(j p) -> p j", p=128), "b1T", allow_slow_non_contiguous=True)
        ld(b2T[:], b2.rearrange("(j p) -> p j", p=128), "b2T", allow_slow_non_contiguous=True)
        ld(pscT[:], pool_scale.rearrange("(j p) -> p j", p=128), "pscT", allow_slow_non_contiguous=True)
        for tb_ in range(DEPTH):
            dma("sp", xsl[tb_][:], xin[tb_ * 128:(tb_ + 1) * 128, :], w=[B("xs", tb_)])
        kb._wait(kb.E["pool"], tokC)
        wp.plan([(("ada", wt), w_ada, 0, 1024, 512 * wt, 512) for wt in range(5)])
        cnt = {"i": 0}

        def tt(out, a, b_, o, r, w, eng=None):
            en = eng or ("dve" if cnt["i"] % 2 == 0 else "pool")
            cnt["i"] += 1
            return op(en, lambda h: h.tensor_tensor(out=out, in0=a, in1=b_, op=o), r=[B(x) for x in r], w=[B(x) for x in w])

        def tsc(out, a, s1, s2, o0, o1, r, w):
            return op("dve", lambda h: h.tensor_scalar(out=out, in0=a, scalar1=s1, scalar2=s2, op0=o0, op1=o1) if o1 is not None else
                      h.tensor_scalar(out=out, in0=a, scalar1=s1, scalar2=None, op0=o0), r=[B(x) for x in r], w=[B(x) for x in w])

        dtt, a_, th, ea = t64("dtt"), t64("a_"), t64("th"), t64("ea")
        op("act", lambda h: h.activation(out=dtt[:], in_=ldt[:], func=AF.Exp), r=[B("ldt")], w=[B("dtt")])
        tt(a_[:], lre[:], dtt[:], ALU.mult, ["lre", "dtt"], ["a_"], "dve")
        tt(th[:], lim[:], dtt[:], ALU.mult, ["lim", "dtt"], ["th"], "dve")
        op("act", lambda h: h.activation(out=ea[:], in_=a_[:], func=AF.Exp), r=[B("a_")], w=[B("ea")])

        def sin_of(name, shift):
            x, k, r_ = t64(name + "x"), t64(name + "k"), t64(name + "r")
            res = t64(name)
            tsc(x[:], th[:], shift, None, ALU.add, None, ["th"], [name + "x"])
            tsc(k[:], x[:], PI, None, ALU.is_gt, None, [name + "x"], [name + "k"])
            for m in range(1, 6):
                op("dve", lambda h, m=m: h.scalar_tensor_tensor(out=k[:], in0=x[:], scalar=(2 * m + 1) * PI, in1=k[:], op0=ALU.is_gt, op1=ALU.add),
                   r=[B(name + "x"), B(name + "k")], w=[B(name + "k")])
            c1 = float(np.float32(2.0 * PI))
            c2 = 2.0 * PI - c1
            op("dve", lambda h: h.scalar_tensor_tensor(out=r_[:], in0=k[:], scalar=-c1, in1=x[:], op0=ALU.mult, op1=ALU.add),
               r=[B(name + "x"), B(name + "k")], w=[B(name + "r")])
            op("dve", lambda h: h.scalar_tensor_tensor(out=r_[:], in0=k[:], scalar=-c2, in1=r_[:], op0=ALU.mult, op1=ALU.add),
               r=[B(name + "r"), B(name + "k")], w=[B(name + "r")])
            tsc(r_[:], r_[:], -3.1415925, 3.1415925, ALU.max, ALU.min, [name + "r"], [name + "r"])
            op("act", lambda h: h.activation(out=res[:], in_=r_[:], func=AF.Sin), r=[B(name + "r")], w=[B(name)])
            return res

        sn = sin_of("sn", 0.0)
        cs = sin_of("cs", PI / 2)
        PWr = sb("PWr", [128, 9, 64], F32, sc)
        PWi = sb("PWi", [128, 9, 64], F32, sc)
        op("dve", lambda h: h.memset(PWr[:, 0, :], 1.0), w=[B("PW")])
        op("dve", lambda h: h.memset(PWi[:, 0, :], 0.0), w=[B("PW")])
        tt(PWr[:, 1, :], ea[:], cs[:], ALU.mult, ["ea", "cs"], ["PW"], "dve")
        tt(PWi[:, 1, :], ea[:], sn[:], ALU.mult, ["ea", "sn"], ["PW"], "dve")
        t1, t2 = t64("t1"), t64("t2")

        def cmul(outr, outi, ar, ai, br, bi, rn, wn):
            tt(t1[:], ar, br, ALU.mult, rn, ["t1"], "dve")
            tt(t2[:], ai, bi, ALU.mult, rn, ["t2"], "dve")
            tt(outr, t1[:], t2[:], ALU.subtract, ["t1", "t2"], wn, "dve")
            tt(t1[:], ar, bi, ALU.mult, rn, ["t1"], "dve")
            tt(t2[:], ai, br, ALU.mult, rn, ["t2"], "dve")
            tt(outi, t1[:], t2[:], ALU.add, ["t1", "t2"], wn, "dve")

        for k in range(1, 8):
            cmul(PWr[:, k + 1, :], PWi[:, k + 1, :], PWr[:, k, :], PWi[:, k, :], PWr[:, 1, :], PWi[:, 1, :], ["PW"], ["PW"])
        i8r, i8i, m2 = t64("i8r"), t64("i8i"), t64("m2")
        tt(t1[:], PWr[:, 8, :], PWr[:, 8, :], ALU.mult, ["PW"], ["t1"], "dve")
        tt(t2[:], PWi[:, 8, :], PWi[:, 8, :], ALU.mult, ["PW"], ["t2"], "dve")
        tt(m2[:], t1[:], t2[:], ALU.add, ["t1", "t2"], ["m2"], "dve")
        op("dve", lambda h: h.reciprocal(out=m2[:], in_=m2[:]), r=[B("m2")], w=[B("m2")])
        tt(i8r[:], PWr[:, 8, :], m2[:], ALU.mult, ["PW", "m2"], ["i8"], "dve")
        op("dve", lambda h: h.scalar_tensor_tensor(out=i8i[:], in0=PWi[:, 8, :], scalar=-1.0, in1=m2[:], op0=ALU.mult, op1=ALU.mult),
           r=[B("PW"), B("m2")], w=[B("i8")])
        cr, ci, nr = t64("cr"), t64("ci"), t64("nr")
        tsc(nr[:], PWr[:, 1, :], -1.0, None, ALU.add, None, ["PW"], ["nr"])
        tt(t1[:], lre[:], lre[:], ALU.mult, ["lre"], ["t1"], "dve")
        tt(t2[:], lim[:], lim[:], ALU.mult, ["lim"], ["t2"], "dve")
        tt(m2[:], t1[:], t2[:], ALU.add, ["t1", "t2"], ["m2"], "dve")
        op("dve", lambda h: h.reciprocal(out=m2[:], in_=m2[:]), r=[B("m2")], w=[B("m2")])
        tt(t1[:], nr[:], lre[:], ALU.mult, ["nr", "lre"], ["t1"], "dve")
        tt(t2[:], PWi[:, 1, :], lim[:], ALU.mult, ["PW", "lim"], ["t2"], "dve")
        tt(cr[:], t1[:], t2[:], ALU.add, ["t1", "t2"], ["cr"], "dve")
        tt(cr[:], cr[:], m2[:], ALU.mult, ["cr", "m2"], ["cr"], "dve")
        tt(t1[:], PWi[:, 1, :], lre[:], ALU.mult, ["PW", "lre"], ["t1"], "dve")
        tt(t2[:], nr[:], lim[:], ALU.mult, ["nr", "lim"], ["t2"], "dve")
        tt(ci[:], t1[:], t2[:], ALU.subtract, ["t1", "t2"], ["ci"], "dve")
        tt(ci[:], ci[:], m2[:], ALU.mult, ["ci", "m2"], ["ci"], "dve")
        op("dve", lambda h: h.tensor_copy(out=A4[:], in_=PWr[:, 8, :].unsqueeze(1).broadcast_to([128, 4, 64])), r=[B("PW")], w=[B("A4")])
        op("dve", lambda h: h.tensor_copy(out=B4[:], in_=PWi[:, 8, :].unsqueeze(1).broadcast_to([128, 4, 64])), r=[B("PW")], w=[B("A4")])
        pZr = sb("pZr", [128, 64, 8], F32, sc)
        pZi = sb("pZi", [128, 64, 8], F32, sc)
        pYr = sb("pYr", [128, 64, 8], F32, sc)
        pYi = sb("pYi", [128, 64, 8], F32, sc)
        pYn = sb("pYn", [128, 64, 8], F32, sc)
        lo, hi = slice(0, 64), slice(64, 128)
        for i in range(8):
            for (dst, src) in ((pZr, PWr), (pZi, PWi)):
                op("dve", lambda h, dst=dst, src=src, i=i: h.tensor_copy(out=dst[lo, :, i], in_=src[lo, 7 - i, :]), r=[B("PW")], w=[B("pZ")])
                op("dve", lambda h, dst=dst, src=src, i=i: h.tensor_copy(out=dst[hi, :, i], in_=src[hi, i, :]), r=[B("PW")], w=[B("pZ")])
            for (dst, src) in ((pYr, PWr), (pYi, PWi)):
                op("dve", lambda h, dst=dst, src=src, i=i: h.tensor_copy(out=dst[lo, :, i], in_=src[lo, i + 1, :]), r=[B("PW")], w=[B("pY")])
                op("dve", lambda h, dst=dst, src=src, i=i: h.tensor_copy(out=dst[hi, :, i], in_=src[hi, 8 - i, :]), r=[B("PW")], w=[B("pY")])
        tsc(pYn[:], pYi[:], -1.0, None, ALU.mult, None, ["pY"], ["pYn"])
        Bbr = sb("Bbr", [128, 64, 16], F32, sc)
        Bbi = sb("Bbi", [128, 64, 16], F32, sc)
        tb1 = sb("tb1", [128, 64, 16], F32, sc)
        tb2 = sb("tb2", [128, 64, 16], F32, sc)
        crb = cr[:].unsqueeze(2).broadcast_to([128, 64, 16])
        cib = ci[:].unsqueeze(2).broadcast_to([128, 64, 16])
        tt(tb1[:], Bre[:], crb, ALU.mult, ["Bre", "cr"], ["tb1"], "dve")
        tt(tb2[:], Bim[:], cib, ALU.mult, ["Bim", "ci"], ["tb2"], "dve")
        tt(Bbr[:], tb1[:], tb2[:], ALU.subtract, ["tb1", "tb2"], ["Bb"], "dve")
        tt(tb1[:], Bre[:], cib, ALU.mult, ["Bre", "ci"], ["tb1"], "dve")
        tt(tb2[:], Bim[:], crb, ALU.mult, ["Bim", "cr"], ["tb2"], "dve")
        tt(Bbi[:], tb1[:], tb2[:], ALU.add, ["tb1", "tb2"], ["Bb"], "dve")

        GB = 8
        NE = GB * 128
        WZr = sb("WZr", [128, GB, 8, 16], F32, sc)
        WZi = sb("WZi", [128, GB, 8, 16], F32, sc)
        WYr = sb("WYr", [128, GB, 8, 16], F32, sc)
        WYn = sb("WYn", [128, GB, 8, 16], F32, sc)
        X1 = sb("X1", [128, GB, 8, 16], F32, sc)
        X2 = sb("X2", [128, GB, 8, 16], F32, sc)
        WZTb = sb("WZTb", [128, 2, GB, 128], BF16, sc)
        WYb = sb("WYb", [128, 2, GB, 128], BF16, sc)
        W0f = sb("W0f", [128, GB, 128], F32, sc)
        W0b = sb("W0b", [128, GB, 128], BF16, sc)
        shp = [128, GB, 8, 16]
        for blk in range(64 // GB):
            gs = slice(GB * blk, GB * blk + GB)
            Bbrb = Bbr[:, gs, :].unsqueeze(2).broadcast_to(shp)
            Bbib = Bbi[:, gs, :].unsqueeze(2).broadcast_to(shp)
            Creb = Cre[:, gs, :].unsqueeze(2).broadcast_to(shp)
            Cimb = Cim[:, gs, :].unsqueeze(2).broadcast_to(shp)
            pzr = pZr[:, gs, :].unsqueeze(3).broadcast_to(shp)
            pzi = pZi[:, gs, :].unsqueeze(3).broadcast_to(shp)
            pyr = pYr[:, gs, :].unsqueeze(3).broadcast_to(shp)
            pyi = pYi[:, gs, :].unsqueeze(3).broadcast_to(shp)
            pyn = pYn[:, gs, :].unsqueeze(3).broadcast_to(shp)
            i8rb = i8r[:, gs].unsqueeze(2).unsqueeze(3).broadcast_to(shp)
            i8ib = i8i[:, gs].unsqueeze(2).unsqueeze(3).broadcast_to(shp)
            Y1 = tb1[:].rearrange("p (g i) h -> p g i h", g=GB)
            Z8r = W0f[:].rearrange("p g (a b) -> p g a b", a=8)
            tt(WZr[:], Bbrb, pzr, ALU.mult, ["Bb", "pZ"], ["WZr"], "dve")
            tt(X1[:], Bbib, pzi, ALU.mult, ["Bb", "pZ"], ["X1"], "dve")
            tt(WZi[:], Bbrb, pzi, ALU.mult, ["Bb", "pZ"], ["WZi"], "dve")
            tt(X2[:], Bbib, pzr, ALU.mult, ["Bb", "pZ"], ["X2"], "dve")
            tt(WZr[:], WZr[:], X1[:], ALU.subtract, ["WZr", "X1"], ["WZr"], "dve")
            tt(WZi[:], WZi[:], X2[:], ALU.add, ["WZi", "X2"], ["WZi"], "dve")
            for ri, src, sn_ in ((0, WZr, "WZr"), (1, WZi, "WZi")):
                for g4 in range(GB // 4):
                    bk, bkB = kb.bank()
                    for gg in range(4):
                        g = g4 * 4 + gg
                        op("pe", lambda h, g=g, gg=gg, src=src: h.transpose(bk[:, gg * 128:(gg + 1) * 128], src[:, g].rearrange("p a b -> p (a b)"), ident[:]),
                           r=[B(sn_), B("ident")], w=[bkB], inc=(gg == 3))
                    op("act", lambda h, ri=ri, g4=g4: h.activation(out=WZTb[:, ri, g4 * 4:g4 * 4 + 4, :].rearrange("p a b -> p (a b)"), in_=bk[:], func=AF.Identity),
                       r=[bkB], w=[B("WZTb")])
            dma("sp", wzt_d[:, :, gs, :], WZTb[:], r=[B("WZTb")], w=[B("wzt_d")])
            tt(Z8r, WZr[:], i8rb, ALU.mult, ["WZr", "i8"], ["W0f"], "dve")
            tt(X1[:], WZi[:], i8ib, ALU.mult, ["WZi", "i8"], ["X1"], "dve")
            tt(Y1, WZr[:], i8ib, ALU.mult, ["WZr", "i8"], ["tb1"], "dve")
            tt(X2[:], WZi[:], i8rb, ALU.mult, ["WZi", "i8"], ["X2"], "dve")
            tt(Z8r, Z8r, X1[:], ALU.subtract, ["W0f", "X1"], ["W0f"], "dve")
            tt(Y1, Y1, X2[:], ALU.add, ["tb1", "X2"], ["tb1"], "dve")
            tt(WYr[:], Creb, pyr, ALU.mult, ["Cre", "pY"], ["WYr"], "dve")
            tt(X1[:], Cimb, pyi, ALU.mult, ["Cim", "pY"], ["X1"], "dve")
            tt(WYn[:], Creb, pyn, ALU.mult, ["Cre", "pYn"], ["WYn"], "dve")
            tt(X2[:], Cimb, pyr, ALU.mult, ["Cim", "pY"], ["X2"], "dve")
            tt(WYr[:], WYr[:], X1[:], ALU.subtract, ["WYr", "X1"], ["WYr"], "dve")
            tt(WYn[:], WYn[:], X2[:], ALU.subtract, ["WYn", "X2"], ["WYn"], "dve")
            op("act", lambda h: h.activation(out=WYb[:, 0].rearrange("p g b -> p (g b)"), in_=WYr[:].rearrange("p g a b -> p (g a b)"), func=AF.Identity),
               r=[B("WYr")], w=[B("WYb")])
            op("act", lambda h: h.activation(out=WYb[:, 1].rearrange("p g b -> p (g b)"), in_=WYn[:].rearrange("p g a b -> p (g a b)"), func=AF.Identity),
               r=[B("WYn")], w=[B("WYb")])
            dma("sp", wyb_d[:, :, gs, :], WYb[:], r=[B("WYb")], w=[B("wyb_d")])
            for g4 in range(GB // 4):
                bkf, bkfB = kb.bank()
                bkb, bkbB = kb.bank()
                for gg in range(4):
                    g = g4 * 4 + gg
                    for d, (bk_, bB_) in enumerate(((bkf, bkfB), (bkb, bkbB))):
                        ps_ = slice(64 * d, 64 * d + 64)
                        op("pe", lambda h, g=g, gg=gg, bk_=bk_, ps_=ps_: h.matmul(bk_[:, gg * 128:(gg + 1) * 128], lhsT=W0f[ps_, g, :],
                                                                                  rhs=WYr[ps_, g].rearrange("p a b -> p (a b)"), start=True, stop=False),
                           r=[B("W0f"), B("WYr")], w=[bB_], inc=False)
                        op("pe", lambda h, g=g, gg=gg, bk_=bk_, ps_=ps_: h.matmul(bk_[:, gg * 128:(gg + 1) * 128], lhsT=tb1[ps_, GB * g:GB * g + 8, :].rearrange("p a b -> p (a b)"),
                                                                                  rhs=WYn[ps_, g].rearrange("p a b -> p (a b)"), start=False, stop=True),
                           r=[B("tb1"), B("WYn")], w=[bB_], inc=True)
                g4s = slice(4 * g4, 4 * g4 + 4)
                mfb = maskf[:].unsqueeze(1).broadcast_to([128, 4, 128])
                mbb = maskb[:].unsqueeze(1).broadcast_to([128, 4, 128])
                op("dve", lambda h, g4s=g4s, bkf=bkf: h.tensor_tensor(out=X2[:, g4s].rearrange("p g a b -> p g (a b)"), in0=bkf[:].rearrange("p (g c) -> p g c", g=4), in1=mfb, op=ALU.mult),
                   r=[bkfB, B("maskf")], w=[B("X2")])
                op("dve", lambda h, g4s=g4s, bkb=bkb: h.tensor_tensor(out=X1[:, g4s].rearrange("p g a b -> p g (a b)"), in0=bkb[:].rearrange("p (g c) -> p g c", g=4), in1=mbb, op=ALU.mult),
                   r=[bkbB, B("maskb")], w=[B("X1")])
            op("dve", lambda h: h.tensor_tensor(out=X2[:], in0=X2[:], in1=X1[:], op=ALU.add), r=[B("X2"), B("X1")], w=[B("X2")])
            op("dve", lambda h, gs=gs: h.tensor_tensor(out=X1[:].rearrange("p g a b -> p g (a b)"), in0=ident[:].unsqueeze(1).broadcast_to([128, GB, 128]),
                                                       in1=dsum[:, gs].unsqueeze(2).broadcast_to([128, GB, 128]), op=ALU.mult),
               r=[B("ident"), B("dsum"), B("X1")], w=[B("X1")])
            op("dve", lambda h: h.tensor_tensor(out=W0b[:], in0=X2[:].rearrange("p g a b -> p g (a b)"), in1=X1[:].rearrange("p g a b -> p g (a b)"), op=ALU.add),
               r=[B("X2"), B("X1")], w=[B("W0b")])
            dma("sp", w0b_d[:, gs, :], W0b[:], r=[B("W0b")], w=[B("w0b_d")])
            if blk == 0:
                dump("W0b", W0b[:], ["W0b"])
                dump("WYb", WYb[:], ["WYb"])
                dump("WZTb", WZTb[:], ["WZTb"])
        dump("PWr", PWr[:], ["PW"])
        dump("PWi", PWi[:], ["PW"])
        dump("cr", cr[:], ["cr"])
        dump("ci", ci[:], ["ci"])
        dump("dsum", dsum[:], ["dsum"])
        kb.barrier()
        cut(1)

    cTb = sb("cTb", [128, 8, 2], BF16)
    with ExitStack() as sc:
        cT = sb("cT", [128, 8, 2], F32, sc)
        for v in range(2):
            dma("sp", cT[:, :, v], cvec[v, :].rearrange("(k p) -> p k", p=128), w=[B("cT")], allow_slow_non_contiguous=True)
        op("act", lambda h: h.activation(out=cTb[:], in_=cT[:], func=AF.Silu), r=[B("cT")], w=[B("cTb")])
        kb.barrier()

    def mod_part(tiles, j0, nj, modB):
        bk, bkB = kb.bank()
        for wt in tiles:
            wv, wb = wp.get(("ada", wt))
            for mm in range(4):
                j = wt * 4 + mm
                c0 = 2 * (j - j0)
                for k in range(8):
                    op("pe", lambda h, k=k, mm=mm, c0=c0: h.matmul(bk[:, c0:c0 + 2], lhsT=wv[:, k, mm * 128:(mm + 1) * 128], rhs=cTb[:, k, :],
                                                                  start=(k == 0), stop=(k == 7)),
                       r=[wb, B("cTb")], w=[bkB], inc=(k == 7))
            wp.release(("ada", wt))
        return bk, bkB

    def mod_evac(bk, bkB, j0, nj, modB):
        op("dve", lambda h: h.tensor_tensor(out=modT[:, j0:j0 + nj, :], in0=bk[:, 0:2 * nj].rearrange("p (j v) -> p j v", v=2),
                                            in1=badaT[:, j0:j0 + nj].unsqueeze(2).broadcast_to([128, nj, 2]), op=ALU.add),
           r=[bkB, B("badaT")], w=[modB])

    bk_, bkB_ = mod_part(range(4), 0, 16, B("mod"))
    mod_evac(bk_, bkB_, 0, 16, B("mod"))
    op("dve", lambda h: h.tensor_scalar(out=mder[:, 0], in0=modT[:, 8:16, :], scalar1=1.0, scalar2=None, op0=ALU.add), r=[B("mod")], w=[B("mod")])

    def mod_part2_evac(bk, bkB):
        mod_evac(bk, bkB, 16, 32, B("mod2"))
        op("dve", lambda h: h.tensor_scalar(out=mder[:, 1], in0=modT[:, 32:40, :], scalar1=1.0, scalar2=None, op0=ALU.add), r=[B("mod2")], w=[B("mod2")])
        op("dve", lambda h: h.tensor_tensor(out=mder[:, 2], in0=modT[:, 16:24, :], in1=boutT[:].unsqueeze(2).broadcast_to([128, 8, 2]), op=ALU.mult),
           r=[B("mod2"), B("boutT")], w=[B("mod2")])
        op("dve", lambda h: h.tensor_tensor(out=mder[:, 3], in0=modT[:, 40:48, :], in1=b2T[:].unsqueeze(2).broadcast_to([128, 8, 2]), op=ALU.mult),
           r=[B("mod2"), B("b2T")], w=[B("mod2")])
    cut(0)
    wp.free = [i for i in wp.free if i < NW]
    del wsl[NW:]
    xsc.close()
    hp0 = half_plan(0)
    wp.plan(hp0[:2] + [(("ada", wt), w_ada, 0, 1024, 512 * wt, 512) for wt in range(5, 12)] + hp0[2:])
    sh1 = modT[:, 0:8, :]
    sc1p = mder[:, 0]
    gt1 = modT[:, 16:24, :]
    sh2 = modT[:, 24:32, :]
    sc2p = mder[:, 1]
    gt2 = modT[:, 40:48, :]
    gt1b = mder[:, 2]
    gt2b = mder[:, 3]

    hT = sb("hT", [128, 8, 1024], BF16)
    vT = sb("vT", [128, 8, 1024], BF16)
    h2T = vT
    for hf in range(2):
        v = hf
        row0 = 1024 * hf
        nseq = 4 if hf == 0 else 1
        CS = 128 // nseq
        hTB = lambda th: B("hT", th)
        with ExitStack() as asc:
            xsA = list(xsl) + [sb("xsA", [128, D], F32, asc) for _ in range(8 - DEPTH)]

            def chainA(tb):
                xs, xB = xsA[tb], B("xs", tb)
                if not (hf == 0 and tb < DEPTH):
                    dma("sp", xs[:], xin[row0 + tb * 128:row0 + (tb + 1) * 128, :], w=[xB])
                yield
                i = ln_part1(xs, xB)
                yield
                ln_part2(i, xs[:], xs[:], xB, xB)
                yield
                to_feature_major(xs, xB, hT, hTB, tb, sc1p, sh1, v)
            run_pipelined([chainA(tb) for tb in range(8)], 8)
            kb.barrier()
        if hf == 0:
            dump("hT", hT[:], [("hT", 0), ("hT", 1)])
            cut(2)

        with ExitStack() as s5sc:
            U = sb("U", [128, 64, 128], BF16, s5sc)
            ZS = sb("ZS", [128, 2, 128, 64], BF16, s5sc)
            s5w = [sb("s5w%d" % i, [128, 3, 8, 128], BF16, s5sc) for i in range(2)]
            with ExitStack() as uasc:
                uaT = sb("uaT", [128, 8, 1024], BF16, uasc)
                for wt in range(2):
                    wv, wb = wp.get((hf, "ua", wt))
                    for mm in range(4):
                        m = wt * 4 + mm
                        for th in range(2):
                            bk, bkB = kb.bank()
                            for k in range(8):
                                op("pe", lambda h, k=k, mm=mm, th=th: h.matmul(bk[:], lhsT=wv[:, k, mm * 128:(mm + 1) * 128], rhs=hT[:, k, th * 512:(th + 1) * 512],
                                                                              start=(k == 0), stop=(k == 7)),
                                   r=[wb, hTB(th)], w=[bkB], inc=(k == 7))
                            op("act", lambda h, m=m, th=th: h.activation(out=uaT[:, m, th * 512:(th + 1) * 512], in_=bk[:], func=AF.Identity, bias=binT[:, m:m + 1], scale=1.0),
                               r=[bkB, B("binT")], w=[B("uaT", m)])
                    wp.release((hf, "ua", wt))
                xrc = [sb("xrc%d" % i, [128, 1024], BF16, uasc) for i in range(2)]
                for ct in range(8):
                    bk, bkB = kb.bank()
                    bkh = bk[:].bitcast(BF16)
                    for i in range(8):
                        op("pe", lambda h, i=i, ct=ct: h.transpose(bkh[:, i * 128:(i + 1) * 128], uaT[:, ct, i::8], identb[:]),
                           r=[B("uaT", ct), B("identb")], w=[bkB], inc=(i == 7))
                    xr = xrc[ct % 2]
                    xrB = B("xrc", ct % 2)
                    op("dve", lambda h: h.tensor_copy(out=xr[:].rearrange("p (g i o) -> p g i o", g=8, i=8), in_=bkh.rearrange("p (i g o) -> p g i o", i=8, g=8)),
                       r=[bkB], w=[xrB])
                    bk2, bk2B = kb.bank()
                    bk2h = bk2[:].bitcast(BF16)
                    for gl in range(8):
                        op("pe", lambda h, gl=gl: h.transpose(bk2h[:, gl * 128:(gl + 1) * 128], xr[:, gl * 128:(gl + 1) * 128], identb[:]),
                           r=[xrB, B("identb")], w=[bk2B], inc=(gl == 7))
                    op("act", lambda h, ct=ct: h.activation(out=U[:, 8 * ct:8 * ct + 8, :].rearrange("p a b -> p (a b)"), in_=bk2h, func=AF.Identity),
                       r=[bk2B], w=[B("U")])
                if hf == 0:
                    dump("uaT", uaT[:], [("uaT", m_) for m_ in range(8)])
                    dump("U", U[:], ["U"])
                kb.barrier()
                if hf == 0:
                    cut(3)

            def rev(ap3):
                if NOREV:
                    return ap3
                if nseq == 1:
                    return ap3[:, ::-1]
                return ap3.rearrange("p (q c) -> p q c", q=nseq)[:, :, ::-1]

            if hf == 0:
                nsT = sb("nsT", [64, 8, 128], F32, s5sc)
                finS = [sb("finS%d" % ri, [128, 256], F32, s5sc) for ri in range(2)]
            for gb in range(8):
                sw = s5w[gb % 2]
                swB = B("s5w", gb % 2)
                dma("sp", sw[:, 0:2], wzt_d[:, :, 8 * gb:8 * gb + 8, :], r=[B("wzt_d")], w=[swB])
                for g4_ in range(2):
                    for ri in range(2):
                        bkF, bkFB = kb.bank()
                        for gg in range(4):
                            gl = 4 * g4_ + gg
                            g = 8 * gb + gl
                            op("pe", lambda h, ri=ri, gl=gl, g=g, gg=gg: h.matmul(bkF[:, gg::4], lhsT=sw[:, ri, gl, :], rhs=U[:, g, :], start=True, stop=True),
                               r=[swB, B("U")], w=[bkFB], inc=(gg == 3))
                        g0 = 8 * gb + 4 * g4_
                        en = "act" if ri == 0 else "dve"
                        if en == "act":
                            op("act", lambda h, g0=g0, ri=ri: h.activation(out=ZS[:, ri, :, g0:g0 + 4], in_=bkF[:].rearrange("p (n g) -> p n g", g=4), func=AF.Identity),
                               r=[bkFB], w=[B("ZSz"), B("ZSs")])
                        else:
                            op("dve", lambda h, g0=g0, ri=ri: h.tensor_copy(out=ZS[:, ri, :, g0:g0 + 4], in_=bkF[:].rearrange("p (n g) -> p n g", g=4)),
                               r=[bkFB], w=[B("ZSz"), B("ZSs")])

            if hf == 0:
                dump("Z", ZS[:], ["ZSz"])
                cut(4)
            with ExitStack() as rsc:
                W = 64 * nseq
                if hf == 0:
                    while kb.nbank % 8 < 4:
                        kb.nbank += 1
                    mbk, mbkB = mod_part(range(4, 12), 16, 32, B("mod2"))
                pst = [kb.ps[i] for i in range(4)]
                St = [[pst[a][:, ri * 256:ri * 256 + W] for ri in range(2)] for a in range(2)]
                rt = [pst[2][:, 0:W], None, pst[2][:, 256:256 + W], None, pst[3][:, 0:W], pst[3][:, 256:256 + W]]
                rts = [sb("rts%d" % i, [128, W], F32, rsc) for i in range(2)]
                rt[1], rt[3] = rts[0][:], rts[1][:]
                Aw = A4[:].rearrange("p q g -> p (q g)") if nseq == 4 else A4[:, 0, :]
                Bw = B4[:].rearrange("p q g -> p (q g)") if nseq == 4 else B4[:, 0, :]
                psB = [B("ps", i) for i in range(4)]
                if hf == 0:
                    op("dve", lambda h: h.memset(pst[0][:], 0.0), w=[psB[0], B("St", 0)])
                else:
                    op("dve", lambda h: h.memset(pst[0][:], 0.0), w=[psB[0], B("St", 0)])
                    op("dve", lambda h: h.tensor_copy(out=St[0][0], in_=s0t[:, 0, :]), r=[B("s0t")], w=[B("St", 0)])
                    op("dve", lambda h: h.tensor_copy(out=St[0][1], in_=s0t[:, 1, :]), r=[B("s0t")], w=[B("St", 0)])
                op("dve", lambda h: h.memset(pst[1][:], 0.0), w=[psB[1], B("St", 1)])
                op("dve", lambda h: h.memset(pst[2][:], 0.0), w=[psB[2], B("tA")])
                op("dve", lambda h: h.memset(pst[3][:], 0.0), w=[psB[3], B("tN", 0), B("tN", 1)])

                def zs_slot(ri, c):
                    if nseq == 4:
                        return ZS[:, ri].rearrange("p (q c) g -> p q c g", q=4)[:, :, c, :]
                    return ZS[:, ri, c, :]

                def st_view(t):
                    if nseq == 4:
                        return t.rearrange("p (q g) -> p q g", q=4)
                    return t

                tA = pst[2][:].rearrange("p (r x) -> p r x", r=2)[:, :, 0:W]
                tBs = sb("tBs", [128, 2, W], F32, rsc)
                tN = pst[3][:].rearrange("p (r x) -> p r x", r=2)[:, :, 0:W]
                Abc = Aw.unsqueeze(1).broadcast_to([128, 2, W])
                Bbc = Bw.unsqueeze(1).broadcast_to([128, 2, W])

                def st2(a_):
                    return pst[a_][:].rearrange("p (r x) -> p r x", r=2)[:, :, 0:W]

                def zs2(c, P_):
                    if nseq == 4:
                        return ZS[P_].rearrange("p r (q c) g -> p r q c g", q=4)[:, :, :, c, :]
                    return ZS[P_, :, c, :]

                def v2(ap, P_):
                    if nseq == 4:
                        return ap[P_].rearrange("p r (q g) -> p r q g", q=4)
                    return ap[P_]

                for c in range(CS):
                    cur, nxt = st2(c % 2), st2((c + 1) % 2)
                    cB, nB = B("St", c % 2), B("St", (c + 1) % 2)
                    op("dve", lambda h: h.tensor_tensor(out=tA, in0=cur, in1=Abc, op=ALU.mult), r=[cB, B("A4")], w=[B("tA")])
                    op("dve", lambda h: h.tensor_tensor(out=tBs[:], in0=cur, in1=Bbc, op=ALU.mult), r=[cB, B("A4")], w=[B("tBs")])
                    op("dve", lambda h: h.tensor_tensor(out=tN[:, 0, :], in0=tA[:, 0, :], in1=tBs[:, 1, :], op=ALU.subtract), r=[B("tA"), B("tBs")], w=[B("tN", 0)])
                    op("dve", lambda h: h.tensor_tensor(out=tN[:, 1, :], in0=tA[:, 1, :], in1=tBs[:, 0, :], op=ALU.add), r=[B("tA"), B("tBs")], w=[B("tN", 1)])
                    for d_ in range(2):
                        P_ = slice(64 * d_, 64 * d_ + 64)
                        cc = c if d_ == 0 else CS - 1 - c
                        zt = op("dve", lambda h, P_=P_, cc=cc: h.tensor_tensor(out=v2(nxt, P_), in0=v2(tN, P_), in1=zs2(cc, P_), op=ALU.add),
                                r=[B("tN", 0), B("tN", 1), B("ZSz")], w=[nB])
                        op("act", lambda h, P_=P_, cc=cc: h.activation(out=zs2(cc, P_), in_=v2(cur, P_), func=AF.Identity), r=[cB], w=[B("ZSs")], extra=[zt])
                if hf == 0:
                    mod_part2_evac(mbk, mbkB)
                fin = St[CS % 2]
                fB = B("St", CS % 2)
                fB1 = B("St", CS % 2)
                if hf == 0:
                    while kb.nbank % 8 < 4:
                        kb.nbank += 1
                    for ri in range(2):
                        op("act", lambda h, ri=ri: h.activation(out=finS[ri][:], in_=fin[ri], func=AF.Identity), r=[fB, fB1], w=[B("finS", ri)])
                        fv = finS[ri][:].rearrange("p (q g) -> p q g", q=4)
                        bk, bkB = kb.bank()
                        for q in range(4):
                            op("pe", lambda h, q=q, fv=fv: h.transpose(bk[0:64, q * 128:(q + 1) * 128], fv[:, q, :], ident[:]),
                               r=[B("finS", ri), B("ident")], w=[bkB], inc=(q == 3))
                        op("act", lambda h, ri=ri: h.activation(out=nsT[:, ri * 4:ri * 4 + 4, :].rearrange("p a b -> p (a b)"), in_=bk[0:64, :], func=AF.Identity),
                           r=[bkB], w=[B("nsT")])
                    for ri in range(2):
                        for q in range(4):
                            dma("sp", nsout[q, :, ri, :, :].rearrange("d g p -> g d p"), nsT[:, ri * 4 + q, :].rearrange("p (d x) -> p d x", d=2),
                                r=[B("nsT")], w=[B("nsout")])
                if hf == 0:
                    dump("S", ZS[:], ["ZSs"])
                kb.barrier(dma=False)
                if hf == 0:
                    cut(5)

            with ExitStack() as ysc:
                Yb = [sb("Yb%d" % i, [128, 8, 128], BF16, ysc) for i in range(2)]
                xrl = [sb("xr%d" % i, [128, 1024], BF16, ysc) for i in range(2)]
                for gb in range(8):
                    sw = s5w[gb % 2]
                    swB = B("s5w", gb % 2)
                    dma("sp", sw[:, 0], w0b_d[:, 8 * gb:8 * gb + 8, :], r=[B("w0b_d")], w=[swB])
                    dma("sp", sw[:, 1:3], wyb_d[:, :, 8 * gb:8 * gb + 8, :], r=[B("wyb_d")], w=[swB])
                    yb = Yb[gb % 2]
                    ybB = B("Yb", gb % 2)
                    for g4 in range(2):
                        bk, bkB = kb.bank()
                        for gg in range(4):
                            gl = 4 * g4 + gg
                            g = 8 * gb + gl
                            cs_ = slice(gg * 128, (gg + 1) * 128)
                            op("pe", lambda h, gl=gl, g=g, cs_=cs_: h.matmul(bk[:, cs_], lhsT=sw[:, 0, gl, :], rhs=U[:, g, :], start=True, stop=False),
                               r=[swB, B("U")], w=[bkB], inc=False)
                            op("pe", lambda h, gl=gl, g=g, cs_=cs_: h.matmul(bk[:, cs_], lhsT=sw[:, 1, gl, :], rhs=ZS[:, 0, :, g], start=False, stop=False),
                               r=[swB, B("ZSs")], w=[bkB], inc=False)
                            op("pe", lambda h, gl=gl, g=g, cs_=cs_: h.matmul(bk[:, cs_], lhsT=sw[:, 2, gl, :], rhs=ZS[:, 1, :, g], start=False, stop=True),
                               r=[swB, B("ZSs")], w=[bkB], inc=True)
                        op("dve", lambda h, g4=g4: h.tensor_copy(out=yb[:, 4 * g4:4 * g4 + 4, :].rearrange("p a b -> p (a b)"), in_=bk[:]),
                           r=[bkB], w=[ybB])
                    ct = gb
                    bk, bkB = kb.bank()
                    bkh = bk[:].bitcast(BF16)
                    for gl in range(8):
                        op("pe", lambda h, gl=gl: h.transpose(bkh[:, gl * 128:(gl + 1) * 128], yb[:, gl, :], identb[:]),
                           r=[ybB, B("identb")], w=[bkB], inc=(gl == 7))
                    xr = xrl[gb % 2]
                    xrB = B("xr", gb % 2)
                    op("dve", lambda h: h.tensor_copy(out=xr[:].rearrange("p (j g o) -> p j g o", j=8, g=8), in_=bkh.rearrange("p (g j o) -> p j g o", g=8, j=8)),
                       r=[bkB], w=[xrB])
                    bk2, bk2B = kb.bank()
                    bk2h = bk2[:].bitcast(BF16)
                    for j in range(8):
                        op("pe", lambda h, j=j: h.transpose(bk2h[:, j * 128:(j + 1) * 128], xr[:, j * 128:(j + 1) * 128], identb[:]),
                           r=[xrB, B("identb")], w=[bk2B], inc=(j == 7))
                    op("act", lambda h, ct=ct: h.activation(out=vT[:, ct, :].rearrange("p (n j) -> p n j", j=8), in_=bk2h.rearrange("p (j n) -> p n j", j=8), func=AF.Gelu_apprx_tanh),
                       r=[bk2B], w=[B("vT", 0), B("vT", 1)])
                if hf == 0:
                    dump("vT", vT[:], [("vT", 0), ("vT", 1)])
                kb.barrier()
                if hf == 0:
                    cut(6)

        with ExitStack() as m1:
            vgT = sb("vgT", [128, 8, 1024], BF16, m1)
            mgT = sb("mgT", [128, 8, 1024], BF16, m1)
            hsc = ExitStack()
            pbT = sb("pbT", [128, 4, 1024], BF16, hsc)
            sig = [sb("sig%d" % i, [128, 512], F32, hsc) for i in range(2)]
            tmpf = [sb("tmpf%d" % i, [128, 512], F32, hsc) for i in range(2)]
            wsl.append(sb("wslx", [128, 4096], BF16, hsc))
            wp.free.append(NW)
            wp.pump()
            with ExitStack() as psc:
                ubt = [sb("ubt", [128, 512], BF16, psc) for _ in range(8)]
                pql = [sb("pq", [128, 512], BF16, psc) for _ in range(2)]
                wv, wb = wp.get((hf, "ub"))
                wpv, wpb = wp.get((hf, "wpool"))
                for tb in range(8):
                    bk, bkB = kb.bank()
                    for k in range(8):
                        op("pe", lambda h, k=k, tb=tb: h.matmul(bk[:], lhsT=hT[:, k, tb * 128:(tb + 1) * 128], rhs=wv[:, k, :], start=(k == 0), stop=(k == 7)),
                           r=[wb, hTB(tb // 4)], w=[bkB], inc=(k == 7))
                    op("dve", lambda h, tb=tb: h.tensor_tensor(out=ubt[tb][:], in0=bk[:], in1=bub[:], op=ALU.add), r=[bkB, B("bub")], w=[B("ubt", tb)])
                wp.release((hf, "ub"))
                pit = 0
                for gi in range(4):
                    gs_ = slice(gi * 128, (gi + 1) * 128)
                    for th in range(2):
                        par = pit % 2
                        pit += 1
                        pq, pqB = pql[par], B("pq", par)
                        bk, bkB = kb.bank()
                        for t4 in range(4):
                            tb = th * 4 + t4
                            cs_ = slice(t4 * 128, (t4 + 1) * 128)
                            if hf == 1:
                                op("pe", lambda h, tb=tb, cs_=cs_: h.matmul(bk[:, cs_], lhsT=ubt[tb][:, gs_], rhs=bandS[:, gi, :], start=True, stop=True),
                                   r=[B("ubt", tb), B("band")], w=[bkB], inc=(t4 == 3))
                            else:
                                k0, k1, ot = (0, 1, tb + 1) if tb % 2 == 0 else (2, 3, tb - 1)
                                op("pe", lambda h, tb=tb, cs_=cs_, k0=k0: h.matmul(bk[:, cs_], lhsT=ubt[tb][:, gs_], rhs=bandP[:, gi, k0, :], start=True, stop=False),
                                   r=[B("ubt", tb), B("band")], w=[bkB], inc=False)
                                op("pe", lambda h, ot=ot, cs_=cs_, k1=k1: h.matmul(bk[:, cs_], lhsT=ubt[ot][:, gs_], rhs=bandP[:, gi, k1, :], start=False, stop=True),
                                   r=[B("ubt", ot), B("band")], w=[bkB], inc=True)
                        op("dve", lambda h: h.tensor_copy(out=pq[:], in_=bk[:]), r=[bkB], w=[pqB])
                        bk2, bk2B = kb.bank()
                        op("pe", lambda h, gi=gi: h.matmul(bk2[:], lhsT=wpv[:, gi, :], rhs=pq[:], start=True, stop=True), r=[wpb, pqB], w=[bk2B])
                        op("act", lambda h, gi=gi, th=th: h.activation(out=pbT[:, gi, th * 512:(th + 1) * 512], in_=bk2[:], func=AF.Identity, scale=pscT[:, gi:gi + 1]),
                           r=[bk2B, B("pscT")], w=[B("pbT", th)])
                wp.release((hf, "wpool"))
                kb.barrier()
            it = 0
            for wt in range(2):
                wv, wb = wp.get((hf, "glu", wt))
                for mm in range(4):
                    m = wt * 4 + mm
                    for th in range(2):
                        bk, bkB = kb.bank()
                        for k in range(8):
                            op("pe", lambda h, k=k, mm=mm, th=th: h.matmul(bk[:], lhsT=wv[:, k, mm * 128:(mm + 1) * 128], rhs=vT[:, k, th * 512:(th + 1) * 512],
                                                                          start=(k == 0), stop=(k == 7)),
                               r=[wb, B("vT", th)], w=[bkB], inc=(k == 7))
                        sg, sgB = sig[it % 2], B("sig", it % 2)
                        it += 1
                        op("act", lambda h, m=m: h.activation(out=sg[:], in_=bk[:], func=AF.Sigmoid, bias=bgluT[:, m:m + 1], scale=1.0),
                           r=[bkB, B("bgluT")], w=[sgB])
                        op("dve", lambda h, m=m, th=th: h.tensor_tensor(out=vgT[:, m, th * 512:(th + 1) * 512], in0=vT[:, m, th * 512:(th + 1) * 512], in1=sg[:], op=ALU.mult),
                           r=[sgB, B("vT", th)], w=[B("vgT", th)])
                wp.release((hf, "glu", wt))
            if hf == 0:
                dump("vgT", vgT[:], [("vgT", 0), ("vgT", 1)])
                cut(7)
            if hf == 0:
                dump("pbT", pbT[:], [("pbT", 0), ("pbT", 1)])
                cut(8)
            for wt in range(2):
                wpbv, wpbB = wp.get((hf, "pb", wt))
                wpa_v, wpaB = wp.get((hf, "pa", wt))
                wga_v, wgaB = wp.get((hf, "ga", wt))
                wgb_v, wgbB = wp.get((hf, "gb", wt))
                for mm in range(4):
                    m = wt * 4 + mm
                    ms = slice(mm * 128, (mm + 1) * 128)
                    for th in range(2):
                        ts_ = slice(th * 512, (th + 1) * 512)
                        bya, byaB = kb.bank()
                        bga, bgaB = kb.bank()
                        byb, bybB = kb.bank()
                        bgb, bgbB = kb.bank()
                        for k in range(8):
                            op("pe", lambda h, k=k: h.matmul(bya[:], lhsT=wpa_v[:, k, ms], rhs=vgT[:, k, ts_], start=(k == 0), stop=(k == 7)),
                               r=[wpaB, B("vgT", th)], w=[byaB], inc=(k == 7))
                        for k in range(8):
                            op("pe", lambda h, k=k: h.matmul(bga[:], lhsT=wga_v[:, k, ms], rhs=hT[:, k, ts_], start=(k == 0), stop=(k == 7)),
                               r=[wgaB, hTB(th)], w=[bgaB], inc=(k == 7))
                        for k in range(4):
                            op("pe", lambda h, k=k: h.matmul(byb[:], lhsT=wpbv[:, k, ms], rhs=pbT[:, k, ts_], start=(k == 0), stop=(k == 3)),
                               r=[wpbB, B("pbT", th)], w=[bybB], inc=(k == 3))
                        for k in range(8):
                            op("pe", lambda h, k=k: h.matmul(bgb[:], lhsT=wgb_v[:, k, ms], rhs=hT[:, k, ts_], start=(k == 0), stop=(k == 7)),
                               r=[wgbB, hTB(th)], w=[bgbB], inc=(k == 7))
                        op("act", lambda h: h.activation(out=sig[0][:], in_=bga[:], func=AF.Sigmoid, bias=binT[:, 12 + m:13 + m], scale=1.0),
                           r=[bgaB, B("binT")], w=[B("sig", 0)])
                        op("act", lambda h: h.activation(out=sig[1][:], in_=bgb[:], func=AF.Sigmoid, bias=binT[:, 20 + m:21 + m], scale=1.0),
                           r=[bgbB, B("binT")], w=[B("sig", 1)])
                        op("dve", lambda h: h.tensor_tensor(out=tmpf[0][:], in0=bya[:], in1=sig[0][:], op=ALU.mult), r=[byaB, B("sig", 0)], w=[B("tmpf", 0)])
                        op("dve", lambda h: h.tensor_tensor(out=tmpf[1][:], in0=byb[:], in1=sig[1][:], op=ALU.mult), r=[bybB, B("sig", 1)], w=[B("tmpf", 1)])
                        op("dve", lambda h: h.tensor_tensor(out=mgT[:, m, ts_], in0=tmpf[0][:], in1=tmpf[1][:], op=ALU.add),
                           r=[B("tmpf", 0), B("tmpf", 1)], w=[B("mgT", th)])
                for nm_ in ("pb", "pa", "ga", "gb"):
                    wp.release((hf, nm_, wt))
            if hf == 0:
                dump("mgT", mgT[:], [("mgT", 0), ("mgT", 1)])
                cut(9)
            kb.barrier()
            assert NW in wp.free
            wp.free.remove(NW)
            del wsl[NW:]
            hsc.close()
            with ExitStack() as jsc:
                tmT = [sb("tmT", [128, 8, 512], F32, jsc),
                       vgT[:].rearrange("p a b -> p (a b)").bitcast(F32).rearrange("p (a b) -> p a b", a=8)]
                rtl = [sb("rtile", [128, D], F32, jsc) for _ in range(DEPTH)]
                x1l = [sb("x1t", [128, D], F32, jsc) for _ in range(DEPTH)]
                wo = [wp.get((hf, "out", wt)) for wt in range(2)]
                for th in range(2):
                    for m in range(8):
                        wv, wb = wo[m // 4]
                        mm = m % 4
                        bk, bkB = kb.bank()
                        for k in range(8):
                            op("pe", lambda h, k=k, mm=mm, wv=wv: h.matmul(bk[:], lhsT=wv[:, k, mm * 128:(mm + 1) * 128], rhs=mgT[:, k, th * 512:(th + 1) * 512],
                                                                          start=(k == 0), stop=(k == 7)),
                               r=[wb, B("mgT", th)], w=[bkB], inc=(k == 7))
                        op("act", lambda h, m=m, th=th: h.activation(out=tmT[th][:, m, :], in_=bk[:], func=AF.Identity, scale=gt1[:, m, v:v + 1], bias=gt1b[:, m, v:v + 1]),
                           r=[bkB, B("mod2")], w=[B("tmT", th)])
                wp.release((hf, "out", 0))
                wp.release((hf, "out", 1))
                def chainJ(tb):
                    th, t4 = divmod(tb, 4)
                    par = tb % DEPTH
                    xs, xB = xsl[par], B("xs", par)
                    rt_, rB_ = rtl[par], B("rtile", par)
                    x1_, x1B_ = x1l[par], B("x1t", par)
                    dma("sp", xs[:], xin[row0 + tb * 128:row0 + (tb + 1) * 128, :], w=[xB])
                    for h2 in range(2):
                        bk, bkB = kb.bank()
                        for j in range(4):
                            ct = h2 * 4 + j
                            op("pe", lambda h, ct=ct, j=j: h.transpose(bk[:, j * 128:(j + 1) * 128], tmT[th][:, ct, t4 * 128:(t4 + 1) * 128], ident[:]),
                               r=[B("tmT", th), B("ident")], w=[bkB], inc=(j == 3))
                        op("dve", lambda h, h2=h2, bk=bk: h.scalar_tensor_tensor(out=rt_[:, h2 * 512:(h2 + 1) * 512], in0=xs[:, h2 * 512:(h2 + 1) * 512], scalar=ALPHA,
                                                                                 in1=bk[:], op0=ALU.mult, op1=ALU.add),
                           r=[bkB, xB], w=[rB_])
                    yield
                    i = ln_part1(rt_, rB_)
                    yield
                    ln_part2(i, x1_[:], rt_[:], rB_, x1B_)
                    yield
                    op("dve", lambda h: h.tensor_tensor(out=x1_[:], in0=x1_[:], in1=lnp[:, 0, :], op=ALU.mult), r=[x1B_, B("lnp")], w=[x1B_])
                    op("dve", lambda h: h.tensor_tensor(out=x1_[:], in0=x1_[:], in1=lnp[:, 1, :], op=ALU.add), r=[x1B_, B("lnp")], w=[x1B_])
                    dma("sp", x1s[row0 + tb * 128:row0 + (tb + 1) * 128, :], x1_[:], r=[x1B_], w=[B("x1s", hf, tb)])
                    i = ln_part1(x1_, x1B_)
                    yield
                    ln_part2(i, rt_[:], x1_[:], x1B_, rB_)
                    yield
                    to_feature_major(rt_, rB_, h2T, lambda th_: B("vT", th_), tb, sc2p, sh2, v)
                run_pipelined([chainJ(tb) for tb in range(8)], DEPTH)
                kb.barrier()
            if hf == 0:
                dump("h2T", h2T[:], [("vT", 0), ("vT", 1)])
            kb.barrier()
            if hf == 0:
                cut(10)

        with ExitStack() as m2:
            hid = sb("hid", [128, 32, 1024], BF16, m2)
            rl = [sb("rl%d" % i, [128, 512], F32, m2) for i in range(2)]
            fT = hT[:].rearrange("p a b -> p (a b)").bitcast(F32).rearrange("p (a b) -> p a b", a=8)
            rtl2 = [sb("rtile2", [128, D], F32, m2) for _ in range(DEPTH)]
            it = 0
            for wt in range(8):
                wv, wb = wp.get((hf, "w1", wt))
                for mm in range(4):
                    m = wt * 4 + mm
                    for th in range(2):
                        bk, bkB = kb.bank()
                        for k in range(8):
                            op("pe", lambda h, k=k, mm=mm, th=th: h.matmul(bk[:], lhsT=wv[:, k, mm * 128:(mm + 1) * 128], rhs=h2T[:, k, th * 512:(th + 1) * 512],
                                                                          start=(k == 0), stop=(k == 7)),
                               r=[wb, B("vT", th)], w=[bkB], inc=(k == 7))
                        r_, rB = rl[it % 2], B("rl", it % 2)
                        it += 1
                        op("act", lambda h, m=m: h.activation(out=r_[:], in_=bk[:], func=AF.Relu, bias=b1T[:, m:m + 1], scale=1.0),
                           r=[bkB, B("b1T")], w=[rB])
                        op("dve", lambda h, m=m, th=th: h.tensor_tensor(out=hid[:, m, th * 512:(th + 1) * 512], in0=r_[:], in1=r_[:], op=ALU.mult),
                           r=[rB], w=[B("hid", th)])
                wp.release((hf, "w1", wt))
            fTl = [fT, h2T[:].rearrange("p a b -> p (a b)").bitcast(F32).rearrange("p (a b) -> p a b", a=8)]
            def mlp2_group(th, m):
                wv, wb = wp.get((hf, "w2", th, m))
                bk, bkB = kb.bank()
                for k in range(32):
                    op("pe", lambda h, k=k: h.matmul(bk[:], lhsT=wv[:, k, :], rhs=hid[:, k, th * 512:(th + 1) * 512], start=(k == 0), stop=(k == 31)),
                       r=[wb, B("hid", th)], w=[bkB], inc=(k == 31))
                op("act", lambda h: h.activation(out=fTl[th][:, m, :], in_=bk[:], func=AF.Identity, scale=gt2[:, m, v:v + 1], bias=gt2b[:, m, v:v + 1]),
                   r=[bkB, B("mod2")], w=[B("fT", th), B("vT", 0), B("vT", 1), B("hT", 0), B("hT", 1)])
                wp.release((hf, "w2", th, m))
                if hf == 0 and th == 1 and m == 4:
                    wp.plan(half_plan(1))
            for m in range(8):
                mlp2_group(0, m)
            def chainF(tb):
                th, t4 = divmod(tb, 4)
                par = tb % DEPTH
                xs, xB = xsl[par], B("xs", par)
                rt_, rB_ = rtl2[par], B("rtile2", par)
                dma("sp", xs[:], x1s[row0 + tb * 128:row0 + (tb + 1) * 128, :], r=[B("x1s", hf, tb)], w=[xB])
                for h2 in range(2):
                    bk, bkB = kb.bank()
                    for j in range(4):
                        ct = h2 * 4 + j
                        op("pe", lambda h, ct=ct, j=j: h.transpose(bk[:, j * 128:(j + 1) * 128], fTl[th][:, ct, t4 * 128:(t4 + 1) * 128], ident[:]),
                           r=[B("fT", th), B("ident")], w=[bkB], inc=(j == 3))
                    op("dve", lambda h, h2=h2, bk=bk: h.scalar_tensor_tensor(out=rt_[:, h2 * 512:(h2 + 1) * 512], in0=xs[:, h2 * 512:(h2 + 1) * 512], scalar=ALPHA,
                                                                             in1=bk[:], op0=ALU.mult, op1=ALU.add),
                       r=[bkB, xB], w=[rB_])
                yield
                i = ln_part1(rt_, rB_)
                yield
                ln_part2(i, rt_[:], rt_[:], rB_, rB_)
                yield
                op("dve", lambda h: h.tensor_tensor(out=rt_[:], in0=rt_[:], in1=lnp[:, 2, :], op=ALU.mult), r=[rB_, B("lnp")], w=[rB_])
                op("dve", lambda h: h.tensor_tensor(out=rt_[:], in0=rt_[:], in1=lnp[:, 3, :], op=ALU.add), r=[rB_, B("lnp")], w=[rB_])
                dma("sp", yout[row0 + tb * 128:row0 + (tb + 1) * 128, :], rt_[:], r=[rB_], w=[B("yout")])
            todo = list(range(8))

            def fill():
                for _ in range(2):
                    if todo:
                        mlp2_group(1, todo.pop(0))
            run_pipelined([chainF(tb) for tb in range(4)], DEPTH, filler=fill)
            while todo:
                mlp2_group(1, todo.pop(0))
            run_pipelined([chainF(tb) for tb in range(4, 8)], DEPTH)
            kb.barrier()
            if hf == 0:
                cut(11)

    kb.barrier(final=True)


_CACHE = {}


def _consts():
    pcm = np.zeros((128, 8, 240), np.float32)
    for a in range(8):
        for h in range(16):
            pcm[16 * a + h, a, 112 + h] = 1.0
    i_of = np.arange(128) // 16
    maskf = (i_of[:, None] <= i_of[None, :]).astype(np.float32)
    maskb = (i_of[:, None] >= i_of[None, :]).astype(np.float32)
    ident = np.eye(128, dtype=np.float32)

    def pm(n):
        out = np.zeros((4, n, n), np.float64)
        t = np.arange(n)
        for gi, w in enumerate((2, 4, 8, 16)):
            lo = np.clip(t - w // 2, 0, n)
            hi = np.clip(t - w // 2 + w, 0, n)
            for to in range(n):
                out[gi, lo[to]:hi[to], to] = 1.0 / float(hi[to] - lo[to])
                out[gi, to, to] -= 1.0
        return out
    P256 = pm(256)
    bandp = np.zeros((128, 4, 4, 128), np.float32)
    for gi in range(4):
        bandp[:, gi, 0, :] = P256[gi, 0:128, 0:128]
        bandp[:, gi, 1, :] = P256[gi, 128:256, 0:128]
        bandp[:, gi, 2, :] = P256[gi, 128:256, 128:256]
        bandp[:, gi, 3, :] = P256[gi, 0:128, 128:256]
    P64 = pm(64)
    bands = np.zeros((128, 4, 128), np.float32)
    for gi in range(4):
        bands[0:64, gi, 0:64] = P64[gi]
        bands[64:128, gi, 64:128] = P64[gi]
    return {
        "c_pc": pcm.reshape(128, 8 * 240), "c_maskf": maskf, "c_maskb": maskb, "c_ident": ident,
        "c_bandp": bandp.reshape(128, -1), "c_bands": bands.reshape(128, -1),
    }


def kernel(x_prompt, x_sample, state_s5, c, c_ctx, w_ada, b_ada, w_in, b_in,
           s5_lam_re, s5_lam_im, s5_log_dt, s5_b_re, s5_b_im, s5_c_re, s5_c_im, s5_d,
           w_glu, b_glu, w_proj_a, w_pool, pool_scale, w_proj_b, w_out, b_out,
           ln1_g, ln1_b, w_mlp1, b_mlp1, w_mlp2, b_mlp2, ln2_g, ln2_b):
    f = lambda a: np.ascontiguousarray(np.asarray(a, dtype=np.float32))
    if "nc" not in _CACHE:
        _CACHE["nc"] = build_program()
    nc = _CACHE["nc"]
    xp = f(x_prompt)
    xs = f(x_sample)
    st = f(state_s5)
    cc = f(c)
    shared = {
        "w_ada": f(w_ada)[0], "b_ada": f(b_ada)[0], "w_in": f(w_in)[0], "b_in": f(b_in)[0],
        "lam_re": f(s5_lam_re)[0], "lam_im": f(s5_lam_im)[0], "log_dt": f(s5_log_dt)[0],
        "b_re": f(s5_b_re)[0], "b_im": f(s5_b_im)[0], "c_re": f(s5_c_re)[0], "c_im": f(s5_c_im)[0],
        "s5_d": f(s5_d)[0], "w_glu": f(w_glu)[0], "b_glu": f(b_glu)[0], "w_pa": f(w_proj_a)[0],
        "w_pool": f(w_pool)[0], "pool_scale": f(pool_scale)[0], "w_pb": f(w_proj_b)[0],
        "w_out": f(w_out)[0], "b_out": f(b_out)[0], "ln1_g": f(ln1_g), "ln1_b": f(ln1_b),
        "w1": f(w_mlp1)[0], "b1": f(b_mlp1)[0], "w2": f(w_mlp2)[0], "b2": f(b_mlp2)[0],
        "ln2_g": f(ln2_g), "ln2_b": f(ln2_b),
    }
    shared.update(_consts())
    in_maps = []
    for i in range(NCORES):
        m = dict(shared)
        m["xin"] = np.concatenate([xp[4 * i:4 * i + 4].reshape(1024, D), xs[i]], axis=0)
        m["st0"] = np.ascontiguousarray(st[i, 0])
        m["cvec"] = np.stack([f(c_ctx), cc[i]], axis=0)
        in_maps.append(m)
    res = run_bass_kernel_spmd(nc, in_maps, core_ids=list(range(NCORES)))
    yp = np.zeros((32, 256, D), np.float32)
    ys = np.zeros((8, 1024, D), np.float32)
    ns = np.zeros((32, 1, 2, 2, 64, 64), np.float32)
    for i in range(NCORES):
        r = res.results[i]
        yo = np.asarray(r["yout"])
        yp[4 * i:4 * i + 4] = yo[0:1024].reshape(4, 256, D)
        ys[i] = yo[1024:2048]
        ns[4 * i:4 * i + 4, 0] = np.asarray(r["nsout"])
    return yp, ys, ns
```

```python
import math
from contextlib import ExitStack
import numpy as np
import concourse.bass as bass
import concourse.mybir as mybir
from concourse.bass_utils import run_bass_kernel_spmd

F32 = mybir.dt.float32
BF16 = mybir.dt.bfloat16
AF = mybir.ActivationFunctionType
ALU = mybir.AluOpType
NCORES = 8
D = 1024
ALPHA = 2.0 ** 0.25
EPS = 1e-6
PI = math.pi


class Tok:
    __slots__ = ("sem", "val", "eng", "key")

    def __init__(self, sem, val, eng, key):
        self.sem, self.val, self.eng, self.key = sem, val, eng, key


class Buf:
    __slots__ = ("w", "r")

    def __init__(self):
        self.w = {}
        self.r = {}


class Eng:
    def __init__(self, name, h, sem):
        self.name, self.h, self.sem = name, h, sem
        self.n = 0
        self.waited = {}


class KB:
    def __init__(self, nc, es):
        self.nc = nc
        self.es = es
        self.E = {}
        for name, h in (("pe", nc.tensor), ("act", nc.scalar), ("dve", nc.vector),
                        ("pool", nc.gpsimd), ("sp", nc.sync)):
            self.E[name] = Eng(name, h, es.enter_context(nc.semaphore("sem_" + name)))
        self.rings = {}
        for q in ("sp", "pool"):
            self.rings[q] = [[es.enter_context(nc.semaphore("dq_%s_%d" % (q, i))), 0, None]
                             for i in range(12)]
        self.ring_pos = {"sp": 0, "pool": 0}
        self.bufs = {}
        self.nbank = 0

    def B(self, *key):
        b = self.bufs.get(key)
        if b is None:
            b = self.bufs[key] = Buf()
        return b

    def _wait(self, e, t):
        if t is None:
            return
        if t.eng is e and e.name in ("pe", "sp"):
            return
        if e.waited.get(t.key, 0) >= t.val:
            return
        e.h.wait_ge(t.sem, t.val)
        e.waited[t.key] = t.val

    def _deps(self, e, r, w, extra):
        for b in r:
            for t in b.w.values():
                self._wait(e, t)
        for b in w:
            for t in list(b.w.values()) + list(b.r.values()):
                self._wait(e, t)
        for t in extra:
            self._wait(e, t)

    def _post(self, tok, r, w):
        for b in r:
            b.r[tok.key] = tok
        for b in w:
            b.w[tok.key] = tok
            b.r = {}

    def op(self, en, fn, r=(), w=(), inc=True, extra=()):
        e = self.E[en]
        self._deps(e, r, w, extra)
        inst = fn(e.h)
        if inc:
            e.n += 1
            inst.then_inc(e.sem, 1)
            tok = Tok(e.sem, e.n, e, en)
        else:
            tok = Tok(e.sem, e.n + 1, e, en)
        self._post(tok, r, w)
        return tok

    def dma(self, q, out, in_, r=(), w=(), extra=(), **kw):
        e = self.E[q]
        self._deps(e, r, w, extra)
        ring = self.rings[q]
        i = self.ring_pos[q]
        self.ring_pos[q] = (i + 1) % len(ring)
        slot = ring[i]
        self._wait(e, slot[2])
        inst = e.h.dma_start(out=out, in_=in_, **kw)
        slot[1] += 16
        inst.then_inc(slot[0], 16)
        tok = Tok(slot[0], slot[1], None, "d%s%d" % (q, i))
        slot[2] = tok
        self._post(tok, r, w)
        return tok

    def barrier(self, final=False, dma=True):
        toks = []
        for e in self.E.values():
            if e.n > 0:
                toks.append(Tok(e.sem, e.n, e, e.name))
        for q, ring in self.rings.items():
            if not dma:
                continue
            if q == "pool" and not final:
                continue
            for slot in ring:
                if slot[2] is not None:
                    toks.append(slot[2])
        for e in self.E.values():
            for t in toks:
                if t.eng is e:
                    continue
                self._wait(e, t)

    def bank(self):
        i = self.nbank % 8
        self.nbank += 1
        return self.ps[i], self.B("ps", i)


class Stop(Exception):
    pass


LIMIT = None
NOREV = False
DUMPS = []


def build_program():
    nc = bass.Bass("TRN2", target_bir_lowering=False)
    es = ExitStack()
    try:
        _build(nc, es)
    except Stop:
        return nc
    es.close()
    return nc


def _build(nc, es):

    def din(name, shape):
        return nc.dram_tensor(name, list(shape), F32, kind="ExternalInput").ap()

    xin = din("xin", [2048, D])
    st0 = din("st0", [2, 2, 64, 64])
    cvec = din("cvec", [2, D])
    w_ada = din("w_ada", [D, 6 * D])
    b_ada = din("b_ada", [6 * D])
    w_in = din("w_in", [D, 3584])
    b_in = din("b_in", [3584])
    lam_re = din("lam_re", [2, 64, 64])
    lam_im = din("lam_im", [2, 64, 64])
    log_dt = din("log_dt", [2, 64])
    b_re = din("b_re", [2, 64, 64, 16])
    b_im = din("b_im", [2, 64, 64, 16])
    c_re = din("c_re", [2, 64, 16, 64])
    c_im = din("c_im", [2, 64, 16, 64])
    s5_d = din("s5_d", [2, D])
    w_glu = din("w_glu", [D, D])
    b_glu = din("b_glu", [D])
    w_pa = din("w_pa", [D, D])
    w_pool = din("w_pool", [4, 128, 128])
    pool_scale = din("pool_scale", [512])
    w_pb = din("w_pb", [512, D])
    w_out = din("w_out", [D, D])
    b_out = din("b_out", [D])
    ln1_g = din("ln1_g", [1, D])
    ln1_b = din("ln1_b", [1, D])
    w1 = din("w1", [D, 4 * D])
    b1 = din("b1", [4 * D])
    w2 = din("w2", [4 * D, D])
    b2 = din("b2", [D])
    ln2_g = din("ln2_g", [1, D])
    ln2_b = din("ln2_b", [1, D])
    c_pc = din("c_pc", [128, 8 * 240])
    c_maskf = din("c_maskf", [128, 128])
    c_maskb = din("c_maskb", [128, 128])
    c_ident = din("c_ident", [128, 128])
    c_bandp = din("c_bandp", [128, 4 * 4 * 128])
    c_bands = din("c_bands", [128, 4 * 128])

    yout = nc.dram_tensor("yout", [2048, D], F32, kind="ExternalOutput").ap()
    nsout = nc.dram_tensor("nsout", [4, 2, 2, 64, 64], F32, kind="ExternalOutput").ap()
    x1s = nc.dram_tensor("x1s", [2048, D], F32, kind="Internal").ap()
    wzt_d = nc.dram_tensor("wzt_d", [128, 2, 64, 128], BF16, kind="Internal").ap()
    wyb_d = nc.dram_tensor("wyb_d", [128, 2, 64, 128], BF16, kind="Internal").ap()
    w0b_d = nc.dram_tensor("w0b_d", [128, 64, 128], BF16, kind="Internal").ap()

    kb = KB(nc, es)
    op, dma, B = kb.op, kb.dma, kb.B

    def cut(n):
        if LIMIT == n:
            kb.barrier()
            raise Stop()

    def dump(name, ap, bname):
        if LIMIT is None:
            return
        shp = list(ap.shape)
        dt_ = nc.dram_tensor("dbg_" + name, shp, ap.dtype, kind="ExternalOutput").ap()
        dma("sp", dt_, ap, r=[(B(*x) if isinstance(x, tuple) else B(x)) for x in bname], w=[B("dbg_" + name)])
        DUMPS.append("dbg_" + name)

    used_names = {}

    def sb(name, shape, dt, scope=es):
        k = used_names.get(name, 0)
        used_names[name] = k + 1
        if k:
            name = "%s_%d" % (name, k)
        return scope.enter_context(nc.sbuf_tensor(name, list(shape), dt))

    kb.ps = [es.enter_context(nc.psum_tensor("psb%d" % i, [128, 512], F32)) for i in range(8)]

    NW = 5
    wsl = [sb("wsl%d" % i, [128, 4096], BF16) for i in range(NW)]
    wstate = {"i": 0}
    pc = sb("pc", [128, 8, 240], BF16)
    ident = sb("ident", [128, 128], F32)
    identb = sb("identb", [128, 128], BF16)
    maskf = sb("maskf", [128, 128], F32)
    maskb = sb("maskb", [128, 128], F32)
    bandP = sb("bandP", [128, 4, 4, 128], BF16)
    bandS = sb("bandS", [128, 4, 128], BF16)
    bub = sb("bub", [128, 512], F32)
    lnp = sb("lnp", [128, 4, D], F32)
    binT = sb("binT", [128, 28], F32)
    badaT = sb("badaT", [128, 48], F32)
    bgluT = sb("bgluT", [128, 8], F32)
    boutT = sb("boutT", [128, 8], F32)
    b1T = sb("b1T", [128, 32], F32)
    b2T = sb("b2T", [128, 8], F32)
    pscT = sb("pscT", [128, 4], F32)
    modT = sb("modT", [128, 48, 2], F32)
    mder = sb("mder", [128, 4, 8, 2], F32)
    A4 = sb("A4", [128, 4, 64], F32)
    B4 = sb("B4", [128, 4, 64], F32)
    s0t = sb("s0t", [128, 2, 64], F32)
    neghalf = sb("neghalf", [128, 1], F32)
    DEPTH = 4
    xsl = [sb("xsl%d" % i, [128, D], F32) for i in range(DEPTH)]
    NLN = 8
    lnst = [sb("lnst", [128, 2, 6], F32) for _ in range(NLN)]
    lnmv = [sb("lnmv", [128, 2], F32) for _ in range(NLN)]
    lnve = [sb("lnve", [128, 1], F32) for _ in range(NLN)]
    lnrs = [sb("lnrs", [128, 1], F32) for _ in range(NLN)]
    lnnb = [sb("lnnb", [128, 1], F32) for _ in range(NLN)]
    lnstate = {"i": 0}
    class WP:
        def __init__(self):
            self.free = list(range(NW))
            self.queue = []
            self.issued = {}
            self.hist = []

        def plan(self, specs):
            self.queue.extend(specs)
            self.pump()

        def pump(self):
            while self.free and self.queue:
                key = self.queue[0][0]
                merge = len(key) > 1 and key[1] in ("pb", "pa", "ga", "gb")
                cand = [i for i in self.free if i < NW or merge]
                if not cand:
                    break
                key, src, r0, nrows, c0, ncols = self.queue.pop(0)
                i = cand[0]
                self.free.remove(i)
                kc = nrows // 128
                view = wsl[i][:, 0:kc * ncols].rearrange("p (k c) -> p k c", k=kc)
                wb = B("wsl", i)
                if len(self.hist) >= 2:
                    kb._wait(kb.E["pool"], self.hist[-2])
                tok_ = dma("pool", view, src[r0:r0 + nrows, c0:c0 + ncols].rearrange("(k p) c -> p k c", p=128), w=[wb])
                self.hist.append(tok_)
                self.issued[key] = (view, wb, i)

        def get(self, key):
            self.pump()
            assert key in self.issued, ("weight tile not issued (no free slot)", key, self.free, [q[0] for q in self.queue[:3]])
            v_, b_, _ = self.issued[key]
            return v_, b_

        def release(self, key):
            _, _, i = self.issued.pop(key)
            self.free.append(i)
            self.pump()

    wp = WP()

    def ln_part1(src, srcB):
        i = lnstate["i"] % NLN
        lnstate["i"] += 1
        st, mv, ve = lnst[i], lnmv[i], lnve[i]
        stB, mvB, veB = B("lnst", i), B("lnmv", i), B("lnve", i)
        op("dve", lambda h: h.bn_stats(out=st[:, 0, :], in_=src[:, 0:512]), r=[srcB], w=[stB])
        op("dve", lambda h: h.bn_stats(out=st[:, 1, :], in_=src[:, 512:1024]), r=[srcB], w=[stB])
        op("dve", lambda h: h.bn_aggr(out=mv[:], in_=st[:].rearrange("p a b -> p (a b)")), r=[stB], w=[mvB])
        op("act", lambda h: h.activation(out=ve[:], in_=mv[:, 1:2], func=AF.Sqrt, bias=epsT[:, 0:1], scale=1.0), r=[mvB, B("epsT")], w=[veB])
        return i

    def ln_part2(i, dst, src, srcB, dstB):
        mv, ve, rs, nb = lnmv[i], lnve[i], lnrs[i], lnnb[i]
        mvB, veB, rsB, nbB = B("lnmv", i), B("lnve", i), B("lnrs", i), B("lnnb", i)
        op("dve", lambda h: h.reciprocal(out=rs[:], in_=ve[:]), r=[veB], w=[rsB])
        op("dve", lambda h: h.scalar_tensor_tensor(out=nb[:], in0=mv[:, 0:1], scalar=-1.0, in1=rs[:], op0=ALU.mult, op1=ALU.mult),
           r=[mvB, rsB], w=[nbB])
        op("act", lambda h: h.activation(out=dst, in_=src, func=AF.Identity, scale=rs[:, 0:1], bias=nb[:, 0:1]),
           r=[srcB, rsB, nbB], w=[dstB])

    def run_pipelined(gens, depth, filler=None):
        pending = list(gens)
        active = []
        while pending or active:
            while pending and len(active) < depth:
                active.append(pending.pop(0))
            for g in list(active):
                try:
                    next(g)
                except StopIteration:
                    active.remove(g)
            if filler is not None:
                filler()

    def to_feature_major(src, srcB, dstT, dstB_fn, tb, scale_ap, bias_ap, v):
        for hf in range(2):
            bk, bkB = kb.bank()
            for j in range(4):
                ct = hf * 4 + j
                op("pe", lambda h, ct=ct, j=j: h.transpose(bk[:, j * 128:(j + 1) * 128], src[:, ct * 128:(ct + 1) * 128], ident[:]),
                   r=[srcB, B("ident")], w=[bkB], inc=(j == 3))
            for j in range(4):
                ct = hf * 4 + j
                op("act", lambda h, ct=ct, j=j: h.activation(out=dstT[:, ct, tb * 128:(tb + 1) * 128], in_=bk[:, j * 128:(j + 1) * 128],
                                                             func=AF.Identity, scale=scale_ap[:, ct, v:v + 1], bias=bias_ap[:, ct, v:v + 1]),
                   r=[bkB, B("mod"), B("mod2")], w=[dstB_fn(tb // 4)])

    def ld(dst, src, bname, q="sp", **kw):
        return dma(q, dst, src, w=[B(bname)], **kw)

    ld(ident[:], c_ident, "ident")
    ld(maskf[:], c_maskf, "maskf")
    ld(maskb[:], c_maskb, "maskb")
    op("dve", lambda h: h.memset(neghalf[:], -0.5), w=[B("neghalf")])
    epsT = sb("epsT", [128, 1], F32)
    op("dve", lambda h: h.memset(epsT[:], EPS), w=[B("epsT")])

    NXS = 0
    xsc = ExitStack()
    for _ in range(NXS):
        wsl.append(sb("wslx", [128, 4096], BF16, xsc))
    wp.free = list(range(NW + NXS))

    def half_plan(hf_):
        P = []
        P += [((hf_, "ua", wt), w_in, 0, 1024, 512 * wt, 512) for wt in range(2)]
        P += [((hf_, "ub"), w_in, 0, 1024, 1024, 512), ((hf_, "wpool"), w_pool.rearrange("g i o -> (g i) o"), 0, 512, 0, 128)]
        P += [((hf_, "glu", wt), w_glu, 0, 1024, 512 * wt, 512) for wt in range(2)]
        for wt in range(2):
            P += [((hf_, "pb", wt), w_pb, 0, 512, 512 * wt, 512), ((hf_, "pa", wt), w_pa, 0, 1024, 512 * wt, 512),
                  ((hf_, "ga", wt), w_in, 0, 1024, 1536 + 512 * wt, 512), ((hf_, "gb", wt), w_in, 0, 1024, 2560 + 512 * wt, 512)]
        P += [((hf_, "out", wt), w_out, 0, 1024, 512 * wt, 512) for wt in range(2)]
        P += [((hf_, "w1", wt), w1, 0, 1024, 512 * wt, 512) for wt in range(8)]
        P += [((hf_, "w2", ps_, m_), w2, 0, 4096, 128 * m_, 128) for ps_ in range(2) for m_ in range(8)]
        return P

    with ExitStack() as sc:
        def t64(name):
            return sb(name, [128, 64], F32, sc)
        lre, lim, ldt = t64("lre"), t64("lim"), t64("ldt")
        Bre = sb("Bre", [128, 64, 16], F32, sc)
        Bim = sb("Bim", [128, 64, 16], F32, sc)
        Cre = sb("Cre", [128, 64, 16], F32, sc)
        Cim = sb("Cim", [128, 64, 16], F32, sc)
        dsum = t64("dsum")
        Ln = sb("Ln", [64, 2, 2, 64], F32, sc)
        Sn = sb("Sn", [64, 2, 2, 64], F32, sc)
        Cn = sb("Cn", [128, 2, 8, 2, 64], F32, sc)
        Dn = sb("Dn", [64, 2, 16], F32, sc)
        Dn8 = sb("Dn8", [64, 8, 16], F32, sc)
        dma("sp", Ln[:, 0], lam_re.rearrange("d g p -> g d p"), w=[B("Ln")])
        dma("sp", Ln[:, 1], lam_im.rearrange("d g p -> g d p"), w=[B("Ln")])
        for d in range(2):
            sl = slice(64 * d, 64 * d + 64)
            dma("sp", ldt[sl, :], log_dt[d:d + 1, :].partition_broadcast(64).rearrange("p o f -> p (o f)"), w=[B("ldt")])
        dma("sp", Dn[:], s5_d.rearrange("d (g h) -> g d h", h=16), w=[B("Dn")], allow_slow_non_contiguous=True)
        for r_ in range(2):
            dma("sp", Sn[:, r_], st0[:, r_].rearrange("d g p -> g d p"), w=[B("Sn")])
        for d in range(2):
            sl = slice(64 * d, 64 * d + 64)
            dma("sp", Bre[sl], b_re[d].rearrange("g p h -> p g h"), w=[B("Bre")], allow_slow_non_contiguous=True)
            dma("sp", Bim[sl], b_im[d].rearrange("g p h -> p g h"), w=[B("Bim")], allow_slow_non_contiguous=True)
        for ri_, src_ in enumerate((c_re, c_im)):
            for d in range(2):
                tokC = dma("sp", Cn[:, ri_, :, d, :], src_[d].rearrange("(gh gl) h p -> (gl h) gh p", gl=8), w=[B("Cn")])

        def tr_small(dst, src_ap, nrow, srcB, dstB):
            bk, bkB = kb.bank()
            op("pe", lambda h: h.transpose(bk[:, 0:nrow], src_ap, ident[0:nrow, 0:nrow]), r=[B(srcB), B("ident")], w=[bkB])
            op("act", lambda h: h.activation(out=dst, in_=bk[:, 0:nrow], func=AF.Identity), r=[bkB], w=[B(dstB)])

        tr_small(lre[:], Ln[:, 0].rearrange("g d p -> g (d p)"), 64, "Ln", "lre")
        tr_small(lim[:], Ln[:, 1].rearrange("g d p -> g (d p)"), 64, "Ln", "lim")
        tr_small(s0t[:, 0, :], Sn[:, 0].rearrange("g d p -> g (d p)"), 64, "Sn", "s0t")
        tr_small(s0t[:, 1, :], Sn[:, 1].rearrange("g d p -> g (d p)"), 64, "Sn", "s0t")
        op("dve", lambda h: h.tensor_tensor(out=Dn[:, 0, :], in0=Dn[:, 0, :], in1=Dn[:, 1, :], op=ALU.add), r=[B("Dn")], w=[B("Dn")])
        op("dve", lambda h: h.tensor_copy(out=Dn8[:], in_=Dn[:, 0, :].unsqueeze(1).broadcast_to([64, 8, 16])), r=[B("Dn")], w=[B("Dn8")])
        tr_small(dsum[:], Dn8[:].rearrange("g a b -> g (a b)"), 64, "Dn8", "dsum")
        for ri_, dstC in enumerate((Cre, Cim)):
            for gh4 in range(2):
                bk, bkB = kb.bank()
                for j in range(4):
                    gh = gh4 * 4 + j
                    op("pe", lambda h, j=j, gh=gh, ri_=ri_: h.transpose(bk[:, j * 128:(j + 1) * 128], Cn[:, ri_, gh].rearrange("p d x -> p (d x)"), ident[:]),
                       r=[B("Cn"), B("ident")], w=[bkB], inc=(j == 3))
                op("act", lambda h, gh4=gh4, dstC=dstC: h.activation(out=dstC[:, 32 * gh4:32 * gh4 + 32, :].rearrange("p a b -> p (a b)"), in_=bk[:], func=AF.Identity),
                   r=[bkB], w=[B("Cre"), B("Cim")])

        ld(pc[:].rearrange("p a b -> p (a b)"), c_pc, "pc", q="pool")
        ld(identb[:], c_ident, "identb", q="pool")
        ld(bandP[:].rearrange("p a b c -> p (a b c)"), c_bandp, "band", q="pool")
        ld(bandS[:].rearrange("p a b -> p (a b)"), c_bands, "band", q="pool")
        ld(bub[:], b_in[1024:1536].partition_broadcast(128), "bub")
        for i, src in enumerate((ln1_g, ln1_b, ln2_g, ln2_b)):
            ld(lnp[:, i, :], src.partition_broadcast(128).rearrange("p o f -> p (o f)"), "lnp")
        ld(binT[:], b_in.rearrange("(j p) -> p j", p=128), "binT", allow_slow_non_contiguous=True)
        ld(badaT[:], b_ada.rearrange("(j p) -> p j", p=128), "badaT", allow_slow_non_contiguous=True)
        ld(bgluT[:], b_glu.rearrange("(j p) -> p j", p=128), "bgluT", allow_slow_non_contiguous=True)
        ld(boutT[:], b_out.rearrange("(j p) -> p j", p=128), "boutT", allow_slow_non_contiguous=True)
        ld(b1T[:], b1.rearrange("(j p) -> p j", p=128), "b1T", allow_slow_non_contiguous=True)
        ld(b2T[:], b2.rearrange("(j p) -> p j", p=128), "b2T", allow_slow_non_contiguous=True)
        ld(pscT[:], pool_scale.rearrange("(j p) -> p j", p=128), "pscT", allow_slow_non_contiguous=True)
        for tb_ in range(DEPTH):
            dma("sp", xsl[tb_][:], xin[tb_ * 128:(tb_ + 1) * 128, :], w=[B("xs", tb_)])
        kb._wait(kb.E["pool"], tokC)
        wp.plan([(("ada", wt), w_ada, 0, 1024, 512 * wt, 512) for wt in range(5)])
        cnt = {"i": 0}

        def tt(out, a, b_, o, r, w, eng=None):
            en = eng or ("dve" if cnt["i"] % 2 == 0 else "pool")
            cnt["i"] += 1
            return op(en, lambda h: h.tensor_tensor(out=out, in0=a, in1=b_, op=o), r=[B(x) for x in r], w=[B(x) for x in w])

        def tsc(out, a, s1, s2, o0, o1, r, w):
            return op("dve", lambda h: h.tensor_scalar(out=out, in0=a, scalar1=s1, scalar2=s2, op0=o0, op1=o1) if o1 is not None else
                      h.tensor_scalar(out=out, in0=a, scalar1=s1, scalar2=None, op0=o0), r=[B(x) for x in r], w=[B(x) for x in w])

        dtt, a_, th, ea = t64("dtt"), t64("a_"), t64("th"), t64("ea")
        op("act", lambda h: h.activation(out=dtt[:], in_=ldt[:], func=AF.Exp), r=[B("ldt")], w=[B("dtt")])
        tt(a_[:], lre[:], dtt[:], ALU.mult, ["lre", "dtt"], ["a_"], "dve")
        tt(th[:], lim[:], dtt[:], ALU.mult, ["lim", "dtt"], ["th"], "dve")
        op("act", lambda h: h.activation(out=ea[:], in_=a_[:], func=AF.Exp), r=[B("a_")], w=[B("ea")])

        def sin_of(name, shift):
            x, k, r_ = t64(name + "x"), t64(name + "k"), t64(name + "r")
            res = t64(name)
            tsc(x[:], th[:], shift, None, ALU.add, None, ["th"], [name + "x"])
            tsc(k[:], x[:], PI, None, ALU.is_gt, None, [name + "x"], [name + "k"])
            for m in range(1, 6):
                op("dve", lambda h, m=m: h.scalar_tensor_tensor(out=k[:], in0=x[:], scalar=(2 * m + 1) * PI, in1=k[:], op0=ALU.is_gt, op1=ALU.add),
                   r=[B(name + "x"), B(name + "k")], w=[B(name + "k")])
            c1 = float(np.float32(2.0 * PI))
            c2 = 2.0 * PI - c1
            op("dve", lambda h: h.scalar_tensor_tensor(out=r_[:], in0=k[:], scalar=-c1, in1=x[:], op0=ALU.mult, op1=ALU.add),
               r=[B(name + "x"), B(name + "k")], w=[B(name + "r")])
            op("dve", lambda h: h.scalar_tensor_tensor(out=r_[:], in0=k[:], scalar=-c2, in1=r_[:], op0=ALU.mult, op1=ALU.add),
               r=[B(name + "r"), B(name + "k")], w=[B(name + "r")])
            tsc(r_[:], r_[:], -3.1415925, 3.1415925, ALU.max, ALU.min, [name + "r"], [name + "r"])
            op("act", lambda h: h.activation(out=res[:], in_=r_[:], func=AF.Sin), r=[B(name + "r")], w=[B(name)])
            return res

        sn = sin_of("sn", 0.0)
        cs = sin_of("cs", PI / 2)
        PWr = sb("PWr", [128, 9, 64], F32, sc)
        PWi = sb("PWi", [128, 9, 64], F32, sc)
        op("dve", lambda h: h.memset(PWr[:, 0, :], 1.0), w=[B("PW")])
        op("dve", lambda h: h.memset(PWi[:, 0, :], 0.0), w=[B("PW")])
        tt(PWr[:, 1, :], ea[:], cs[:], ALU.mult, ["ea", "cs"], ["PW"], "dve")
        tt(PWi[:, 1, :], ea[:], sn[:], ALU.mult, ["ea", "sn"], ["PW"], "dve")
        t1, t2 = t64("t1"), t64("t2")

        def cmul(outr, outi, ar, ai, br, bi, rn, wn):
            tt(t1[:], ar, br, ALU.mult, rn, ["t1"], "dve")
            tt(t2[:], ai, bi, ALU.mult, rn, ["t2"], "dve")
            tt(outr, t1[:], t2[:], ALU.subtract, ["t1", "t2"], wn, "dve")
            tt(t1[:], ar, bi, ALU.mult, rn, ["t1"], "dve")
            tt(t2[:], ai, br, ALU.mult, rn, ["t2"], "dve")
            tt(outi, t1[:], t2[:], ALU.add, ["t1", "t2"], wn, "dve")

        for k in range(1, 8):
            cmul(PWr[:, k + 1, :], PWi[:, k + 1, :], PWr[:, k, :], PWi[:, k, :], PWr[:, 1, :], PWi[:, 1, :], ["PW"], ["PW"])
        i8r, i8i, m2 = t64("i8r"), t64("i8i"), t64("m2")
        tt(t1[:], PWr[:, 8, :], PWr[:, 8, :], ALU.mult, ["PW"], ["t1"], "dve")
        tt(t2[:], PWi[:, 8, :], PWi[:, 8, :], ALU.mult, ["PW"], ["t2"], "dve")
        tt(m2[:], t1[:], t2[:], ALU.add, ["t1", "t2"], ["m2"], "dve")
        op("dve", lambda h: h.reciprocal(out=m2[:], in_=m2[:]), r=[B("m2")], w=[B("m2")])
        tt(i8r[:], PWr[:, 8, :], m2[:], ALU.mult, ["PW", "m2"], ["i8"], "dve")
        op("dve", lambda h: h.scalar_tensor_tensor(out=i8i[:], in0=PWi[:, 8, :], scalar=-1.0, in1=m2[:], op0=ALU.mult, op1=ALU.mult),
           r=[B("PW"), B("m2")], w=[B("i8")])
        cr, ci, nr = t64("cr"), t64("ci"), t64("nr")
        tsc(nr[:], PWr[:, 1, :], -1.0, None, ALU.add, None, ["PW"], ["nr"])
        tt(t1[:], lre[:], lre[:], ALU.mult, ["lre"], ["t1"], "dve")
        tt(t2[:], lim[:], lim[:], ALU.mult, ["lim"], ["t2"], "dve")
        tt(m2[:], t1[:], t2[:], ALU.add, ["t1", "t2"], ["m2"], "dve")
        op("dve", lambda h: h.reciprocal(out=m2[:], in_=m2[:]), r=[B("m2")], w=[B("m2")])
        tt(t1[:], nr[:], lre[:], ALU.mult, ["nr", "lre"], ["t1"], "dve")
        tt(t2[:], PWi[:, 1, :], lim[:], ALU.mult, ["PW", "lim"], ["t2"], "dve")
        tt(cr[:], t1[:], t2[:], ALU.add, ["t1", "t2"], ["cr"], "dve")
        tt(cr[:], cr[:], m2[:], ALU.mult, ["cr", "m2"], ["cr"], "dve")
        tt(t1[:], PWi[:, 1, :], lre[:], ALU.mult, ["PW", "lre"], ["t1"], "dve")
        tt(t2[:], nr[:], lim[:], ALU.mult, ["nr", "lim"], ["t2"], "dve")
        tt(ci[:], t1[:], t2[:], ALU.subtract, ["t1", "t2"], ["ci"], "dve")
        tt(ci[:], ci[:], m2[:], ALU.mult, ["ci", "m2"], ["ci"], "dve")
        op("dve", lambda h: h.tensor_copy(out=A4[:], in_=PWr[:, 8, :].unsqueeze(1).broadcast_to([128, 4, 64])), r=[B("PW")], w=[B("A4")])
        op("dve", lambda h: h.tensor_copy(out=B4[:], in_=PWi[:, 8, :].unsqueeze(1).broadcast_to([128, 4, 64])), r=[B("PW")], w=[B("A4")])
        pZr = sb("pZr", [128, 64, 8], F32, sc)
        pZi = sb("pZi", [128, 64, 8], F32, sc)
        pYr = sb("pYr", [128, 64, 8], F32, sc)
        pYi = sb("pYi", [128, 64, 8], F32, sc)
        pYn = sb("pYn", [128, 64, 8], F32, sc)
        lo, hi = slice(0, 64), slice(64, 128)
        for i in range(8):
            for (dst, src) in ((pZr, PWr), (pZi, PWi)):
                op("dve", lambda h, dst=dst, src=src, i=i: h.tensor_copy(out=dst[lo, :, i], in_=src[lo, 7 - i, :]), r=[B("PW")], w=[B("pZ")])
                op("dve", lambda h, dst=dst, src=src, i=i: h.tensor_copy(out=dst[hi, :, i], in_=src[hi, i, :]), r=[B("PW")], w=[B("pZ")])
            for (dst, src) in ((pYr, PWr), (pYi, PWi)):
                op("dve", lambda h, dst=dst, src=src, i=i: h.tensor_copy(out=dst[lo, :, i], in_=src[lo, i + 1, :]), r=[B("PW")], w=[B("pY")])
                op("dve", lambda h, dst=dst, src=src, i=i: h.tensor_copy(out=dst[hi, :, i], in_=src[hi, 8 - i, :]), r=[B("PW")], w=[B("pY")])
        tsc(pYn[:], pYi[:], -1.0, None, ALU.mult, None, ["pY"], ["pYn"])
        Bbr = sb("Bbr", [128, 64, 16], F32, sc)
        Bbi = sb("Bbi", [128, 64, 16], F32, sc)
        tb1 = sb("tb1", [128, 64, 16], F32, sc)
        tb2 = sb("tb2", [128, 64, 16], F32, sc)
        crb = cr[:].unsqueeze(2).broadcast_to([128, 64, 16])
        cib = ci[:].unsqueeze(2).broadcast_to([128, 64, 16])
        tt(tb1[:], Bre[:], crb, ALU.mult, ["Bre", "cr"], ["tb1"], "dve")
        tt(tb2[:], Bim[:], cib, ALU.mult, ["Bim", "ci"], ["tb2"], "dve")
        tt(Bbr[:], tb1[:], tb2[:], ALU.subtract, ["tb1", "tb2"], ["Bb"], "dve")
        tt(tb1[:], Bre[:], cib, ALU.mult, ["Bre", "ci"], ["tb1"], "dve")
        tt(tb2[:], Bim[:], crb, ALU.mult, ["Bim", "cr"], ["tb2"], "dve")
        tt(Bbi[:], tb1[:], tb2[:], ALU.add, ["tb1", "tb2"], ["Bb"], "dve")

        GB = 8
        NE = GB * 128
        WZr = sb("WZr", [128, GB, 8, 16], F32, sc)
        WZi = sb("WZi", [128, GB, 8, 16], F32, sc)
        WYr = sb("WYr", [128, GB, 8, 16], F32, sc)
        WYn = sb("WYn", [128, GB, 8, 16], F32, sc)
        X1 = sb("X1", [128, GB, 8, 16], F32, sc)
        X2 = sb("X2", [128, GB, 8, 16], F32, sc)
        WZTb = sb("WZTb", [128, 2, GB, 128], BF16, sc)
        WYb = sb("WYb", [128, 2, GB, 128], BF16, sc)
        W0f = sb("W0f", [128, GB, 128], F32, sc)
        W0b = sb("W0b", [128, GB, 128], BF16, sc)
        shp = [128, GB, 8, 16]
        for blk in range(64 // GB):
            gs = slice(GB * blk, GB * blk + GB)
            Bbrb = Bbr[:, gs, :].unsqueeze(2).broadcast_to(shp)
            Bbib = Bbi[:, gs, :].unsqueeze(2).broadcast_to(shp)
            Creb = Cre[:, gs, :].unsqueeze(2).broadcast_to(shp)
            Cimb = Cim[:, gs, :].unsqueeze(2).broadcast_to(shp)
            pzr = pZr[:, gs, :].unsqueeze(3).broadcast_to(shp)
            pzi = pZi[:, gs, :].unsqueeze(3).broadcast_to(shp)
            pyr = pYr[:, gs, :].unsqueeze(3).broadcast_to(shp)
            pyi = pYi[:, gs, :].unsqueeze(3).broadcast_to(shp)
            pyn = pYn[:, gs, :].unsqueeze(3).broadcast_to(shp)
            i8rb = i8r[:, gs].unsqueeze(2).unsqueeze(3).broadcast_to(shp)
            i8ib = i8i[:, gs].unsqueeze(2).unsqueeze(3).broadcast_to(shp)
            Y1 = tb1[:].rearrange("p (g i) h -> p g i h", g=GB)
            Z8r = W0f[:].rearrange("p g (a b) -> p g a b", a=8)
            tt(WZr[:], Bbrb, pzr, ALU.mult, ["Bb", "pZ"], ["WZr"], "dve")
            tt(X1[:], Bbib, pzi, ALU.mult, ["Bb", "pZ"], ["X1"], "dve")
            tt(WZi[:], Bbrb, pzi, ALU.mult, ["Bb", "pZ"], ["WZi"], "dve")
            tt(X2[:], Bbib, pzr, ALU.mult, ["Bb", "pZ"], ["X2"], "dve")
            tt(WZr[:], WZr[:], X1[:], ALU.subtract, ["WZr", "X1"], ["WZr"], "dve")
            tt(WZi[:], WZi[:], X2[:], ALU.add, ["WZi", "X2"], ["WZi"], "dve")
            for ri, src, sn_ in ((0, WZr, "WZr"), (1, WZi, "WZi")):
                for g4 in range(GB // 4):
                    bk, bkB = kb.bank()
                    for gg in range(4):
                        g = g4 * 4 + gg
                        op("pe", lambda h, g=g, gg=gg, src=src: h.transpose(bk[:, gg * 128:(gg + 1) * 128], src[:, g].rearrange("p a b -> p (a b)"), ident[:]),
                           r=[B(sn_), B("ident")], w=[bkB], inc=(gg == 3))
                    op("act", lambda h, ri=ri, g4=g4: h.activation(out=WZTb[:, ri, g4 * 4:g4 * 4 + 4, :].rearrange("p a b -> p (a b)"), in_=bk[:], func=AF.Identity),
                       r=[bkB], w=[B("WZTb")])
            dma("sp", wzt_d[:, :, gs, :], WZTb[:], r=[B("WZTb")], w=[B("wzt_d")])
            tt(Z8r, WZr[:], i8rb, ALU.mult, ["WZr", "i8"], ["W0f"], "dve")
            tt(X1[:], WZi[:], i8ib, ALU.mult, ["WZi", "i8"], ["X1"], "dve")
            tt(Y1, WZr[:], i8ib, ALU.mult, ["WZr", "i8"], ["tb1"], "dve")
            tt(X2[:], WZi[:], i8rb, ALU.mult, ["WZi", "i8"], ["X2"], "dve")
            tt(Z8r, Z8r, X1[:], ALU.subtract, ["W0f", "X1"], ["W0f"], "dve")
            tt(Y1, Y1, X2[:], ALU.add, ["tb1", "X2"], ["tb1"], "dve")
            tt(WYr[:], Creb, pyr, ALU.mult, ["Cre", "pY"], ["WYr"], "dve")
            tt(X1[:], Cimb, pyi, ALU.mult, ["Cim", "pY"], ["X1"], "dve")
            tt(WYn[:], Creb, pyn, ALU.mult, ["Cre", "pYn"], ["WYn"], "dve")
            tt(X2[:], Cimb, pyr, ALU.mult, ["Cim", "pY"], ["X2"], "dve")
            tt(WYr[:], WYr[:], X1[:], ALU.subtract, ["WYr", "X1"], ["WYr"], "dve")
            tt(WYn[:], WYn[:], X2[:], ALU.subtract, ["WYn", "X2"], ["WYn"], "dve")
            op("act", lambda h: h.activation(out=WYb[:, 0].rearrange("p g b -> p (g b)"), in_=WYr[:].rearrange("p g a b -> p (g a b)"), func=AF.Identity),
               r=[B("WYr")], w=[B("WYb")])
            op("act", lambda h: h.activation(out=WYb[:, 1].rearrange("p g b -> p (g b)"), in_=WYn[:].rearrange("p g a b -> p (g a b)"), func=AF.Identity),
               r=[B("WYn")], w=[B("WYb")])
            dma("sp", wyb_d[:, :, gs, :], WYb[:], r=[B("WYb")], w=[B("wyb_d")])
            for g4 in range(GB // 4):
                bkf, bkfB = kb.bank()
                bkb, bkbB = kb.bank()
                for gg in range(4):
                    g = g4 * 4 + gg
                    for d, (bk_, bB_) in enumerate(((bkf, bkfB), (bkb, bkbB))):
                        ps_ = slice(64 * d, 64 * d + 64)
                        op("pe", lambda h, g=g, gg=gg, bk_=bk_, ps_=ps_: h.matmul(bk_[:, gg * 128:(gg + 1) * 128], lhsT=W0f[ps_, g, :],
                                                                                  rhs=WYr[ps_, g].rearrange("p a b -> p (a b)"), start=True, stop=False),
                           r=[B("W0f"), B("WYr")], w=[bB_], inc=False)
                        op("pe", lambda h, g=g, gg=gg, bk_=bk_, ps_=ps_: h.matmul(bk_[:, gg * 128:(gg + 1) * 128], lhsT=tb1[ps_, GB * g:GB * g + 8, :].rearrange("p a b -> p (a b)"),
                                                                                  rhs=WYn[ps_, g].rearrange("p a b -> p (a b)"), start=False, stop=True),
                           r=[B("tb1"), B("WYn")], w=[bB_], inc=True)
                g4s = slice(4 * g4, 4 * g4 + 4)
                mfb = maskf[:].unsqueeze(1).broadcast_to([128, 4, 128])
                mbb = maskb[:].unsqueeze(1).broadcast_to([128, 4, 128])
                op("dve", lambda h, g4s=g4s, bkf=bkf: h.tensor_tensor(out=X2[:, g4s].rearrange("p g a b -> p g (a b)"), in0=bkf[:].rearrange("p (g c) -> p g c", g=4), in1=mfb, op=ALU.mult),
                   r=[bkfB, B("maskf")], w=[B("X2")])
                op("dve", lambda h, g4s=g4s, bkb=bkb: h.tensor_tensor(out=X1[:, g4s].rearrange("p g a b -> p g (a b)"), in0=bkb[:].rearrange("p (g c) -> p g c", g=4), in1=mbb, op=ALU.mult),
                   r=[bkbB, B("maskb")], w=[B("X1")])
            op("dve", lambda h: h.tensor_tensor(out=X2[:], in0=X2[:], in1=X1[:], op=ALU.add), r=[B("X2"), B("X1")], w=[B("X2")])
            op("dve", lambda h, gs=gs: h.tensor_tensor(out=X1[:].rearrange("p g a b -> p g (a b)"), in0=ident[:].unsqueeze(1).broadcast_to([128, GB, 128]),
                                                       in1=dsum[:, gs].unsqueeze(2).broadcast_to([128, GB, 128]), op=ALU.mult),
               r=[B("ident"), B("dsum"), B("X1")], w=[B("X1")])
            op("dve", lambda h: h.tensor_tensor(out=W0b[:], in0=X2[:].rearrange("p g a b -> p g (a b)"), in1=X1[:].rearrange("p g a b -> p g (a b)"), op=ALU.add),
               r=[B("X2"), B("X1")], w=[B("W0b")])
            dma("sp", w0b_d[:, gs, :], W0b[:], r=[B("W0b")], w=[B("w0b_d")])
            if blk == 0:
                dump("W0b", W0b[:], ["W0b"])
                dump("WYb", WYb[:], ["WYb"])
                dump("WZTb", WZTb[:], ["WZTb"])
        dump("PWr", PWr[:], ["PW"])
        dump("PWi", PWi[:], ["PW"])
        dump("cr", cr[:], ["cr"])
        dump("ci", ci[:], ["ci"])
        dump("dsum", dsum[:], ["dsum"])
        kb.barrier()
        cut(1)

    cTb = sb("cTb", [128, 8, 2], BF16)
    with ExitStack() as sc:
        cT = sb("cT", [128, 8, 2], F32, sc)
        for v in range(2):
            dma("sp", cT[:, :, v], cvec[v, :].rearrange("(k p) -> p k", p=128), w=[B("cT")], allow_slow_non_contiguous=True)
        op("act", lambda h: h.activation(out=cTb[:], in_=cT[:], func=AF.Silu), r=[B("cT")], w=[B("cTb")])
        kb.barrier()

    def mod_part(tiles, j0, nj, modB):
        bk, bkB = kb.bank()
        for wt in tiles:
            wv, wb = wp.get(("ada", wt))
            for mm in range(4):
                j = wt * 4 + mm
                c0 = 2 * (j - j0)
                for k in range(8):
                    op("pe", lambda h, k=k, mm=mm, c0=c0: h.matmul(bk[:, c0:c0 + 2], lhsT=wv[:, k, mm * 128:(mm + 1) * 128], rhs=cTb[:, k, :],
                                                                  start=(k == 0), stop=(k == 7)),
                       r=[wb, B("cTb")], w=[bkB], inc=(k == 7))
            wp.release(("ada", wt))
        return bk, bkB

    def mod_evac(bk, bkB, j0, nj, modB):
        op("dve", lambda h: h.tensor_tensor(out=modT[:, j0:j0 + nj, :], in0=bk[:, 0:2 * nj].rearrange("p (j v) -> p j v", v=2),
                                            in1=badaT[:, j0:j0 + nj].unsqueeze(2).broadcast_to([128, nj, 2]), op=ALU.add),
           r=[bkB, B("badaT")], w=[modB])

    bk_, bkB_ = mod_part(range(4), 0, 16, B("mod"))
    mod_evac(bk_, bkB_, 0, 16, B("mod"))
    op("dve", lambda h: h.tensor_scalar(out=mder[:, 0], in0=modT[:, 8:16, :], scalar1=1.0, scalar2=None, op0=ALU.add), r=[B("mod")], w=[B("mod")])

    def mod_part2_evac(bk, bkB):
        mod_evac(bk, bkB, 16, 32, B("mod2"))
        op("dve", lambda h: h.tensor_scalar(out=mder[:, 1], in0=modT[:, 32:40, :], scalar1=1.0, scalar2=None, op0=ALU.add), r=[B("mod2")], w=[B("mod2")])
        op("dve", lambda h: h.tensor_tensor(out=mder[:, 2], in0=modT[:, 16:24, :], in1=boutT[:].unsqueeze(2).broadcast_to([128, 8, 2]), op=ALU.mult),
           r=[B("mod2"), B("boutT")], w=[B("mod2")])
        op("dve", lambda h: h.tensor_tensor(out=mder[:, 3], in0=modT[:, 40:48, :], in1=b2T[:].unsqueeze(2).broadcast_to([128, 8, 2]), op=ALU.mult),
           r=[B("mod2"), B("b2T")], w=[B("mod2")])
    cut(0)
    wp.free = [i for i in wp.free if i < NW]
    del wsl[NW:]
    xsc.close()
    hp0 = half_plan(0)
    wp.plan(hp0[:2] + [(("ada", wt), w_ada, 0, 1024, 512 * wt, 512) for wt in range(5, 12)] + hp0[2:])
    sh1 = modT[:, 0:8, :]
    sc1p = mder[:, 0]
    gt1 = modT[:, 16:24, :]
    sh2 = modT[:, 24:32, :]
    sc2p = mder[:, 1]
    gt2 = modT[:, 40:48, :]
    gt1b = mder[:, 2]
    gt2b = mder[:, 3]

    hT = sb("hT", [128, 8, 1024], BF16)
    vT = sb("vT", [128, 8, 1024], BF16)
    h2T = vT
    for hf in range(2):
        v = hf
        row0 = 1024 * hf
        nseq = 4 if hf == 0 else 1
        CS = 128 // nseq
        hTB = lambda th: B("hT", th)
        def chainA(tb):
            par = tb % DEPTH
            xs, xB = xsl[par], B("xs", par)
            if not (hf == 0 and tb < DEPTH):
                dma("sp", xs[:], xin[row0 + tb * 128:row0 + (tb + 1) * 128, :], w=[xB])
            yield
            i = ln_part1(xs, xB)
            yield
            ln_part2(i, xs[:], xs[:], xB, xB)
            yield
            to_feature_major(xs, xB, hT, hTB, tb, sc1p, sh1, v)
        run_pipelined([chainA(tb) for tb in range(8)], DEPTH)
        if hf == 0:
            dump("hT", hT[:], [("hT", 0), ("hT", 1)])
            cut(2)

        with ExitStack() as s5sc:
            U = sb("U", [128, 64, 128], BF16, s5sc)
            ZS = sb("ZS", [128, 2, 128, 64], BF16, s5sc)
            s5w = [sb("s5w%d" % i, [128, 3, 8, 128], BF16, s5sc) for i in range(2)]
            with ExitStack() as uasc:
                uaT = sb("uaT", [128, 8, 1024], BF16, uasc)
                for wt in range(2):
                    wv, wb = wp.get((hf, "ua", wt))
                    for mm in range(4):
                        m = wt * 4 + mm
                        for th in range(2):
                            bk, bkB = kb.bank()
                            for k in range(8):
                                op("pe", lambda h, k=k, mm=mm, th=th: h.matmul(bk[:], lhsT=wv[:, k, mm * 128:(mm + 1) * 128], rhs=hT[:, k, th * 512:(th + 1) * 512],
                                                                              start=(k == 0), stop=(k == 7)),
                                   r=[wb, hTB(th)], w=[bkB], inc=(k == 7))
                            op("act", lambda h, m=m, th=th: h.activation(out=uaT[:, m, th * 512:(th + 1) * 512], in_=bk[:], func=AF.Identity, bias=binT[:, m:m + 1], scale=1.0),
                               r=[bkB, B("binT")], w=[B("uaT", m)])
                    wp.release((hf, "ua", wt))
                xrc = [sb("xrc%d" % i, [128, 1024], BF16, uasc) for i in range(2)]
                for ct in range(8):
                    bk, bkB = kb.bank()
                    bkh = bk[:].bitcast(BF16)
                    for i in range(8):
                        op("pe", lambda h, i=i, ct=ct: h.transpose(bkh[:, i * 128:(i + 1) * 128], uaT[:, ct, i::8], identb[:]),
                           r=[B("uaT", ct), B("identb")], w=[bkB], inc=(i == 7))
                    xr = xrc[ct % 2]
                    xrB = B("xrc", ct % 2)
                    op("dve", lambda h: h.tensor_copy(out=xr[:].rearrange("p (g i o) -> p g i o", g=8, i=8), in_=bkh.rearrange("p (i g o) -> p g i o", i=8, g=8)),
                       r=[bkB], w=[xrB])
                    bk2, bk2B = kb.bank()
                    bk2h = bk2[:].bitcast(BF16)
                    for gl in range(8):
                        op("pe", lambda h, gl=gl: h.transpose(bk2h[:, gl * 128:(gl + 1) * 128], xr[:, gl * 128:(gl + 1) * 128], identb[:]),
                           r=[xrB, B("identb")], w=[bk2B], inc=(gl == 7))
                    op("act", lambda h, ct=ct: h.activation(out=U[:, 8 * ct:8 * ct + 8, :].rearrange("p a b -> p (a b)"), in_=bk2h, func=AF.Identity),
                       r=[bk2B], w=[B("U")])
                if hf == 0:
                    dump("uaT", uaT[:], [("uaT", m_) for m_ in range(8)])
                    dump("U", U[:], ["U"])
                if LIMIT is not None:
                    kb.barrier()
                if hf == 0:
                    cut(3)

            def rev(ap3):
                if NOREV:
                    return ap3
                if nseq == 1:
                    return ap3[:, ::-1]
                return ap3.rearrange("p (q c) -> p q c", q=nseq)[:, :, ::-1]

            if hf == 0:
                nsT = sb("nsT", [64, 8, 128], F32, s5sc)
                finS = [sb("finS%d" % ri, [128, 256], F32, s5sc) for ri in range(2)]
            for gb in range(8):
                sw = s5w[gb % 2]
                swB = B("s5w", gb % 2)
                dma("sp", sw[:, 0:2], wzt_d[:, :, 8 * gb:8 * gb + 8, :], r=[B("wzt_d")], w=[swB])
                for g4_ in range(2):
                    for ri in range(2):
                        bkF, bkFB = kb.bank()
                        for gg in range(4):
                            gl = 4 * g4_ + gg
                            g = 8 * gb + gl
                            op("pe", lambda h, ri=ri, gl=gl, g=g, gg=gg: h.matmul(bkF[:, gg::4], lhsT=sw[:, ri, gl, :], rhs=U[:, g, :], start=True, stop=True),
                               r=[swB, B("U")], w=[bkFB], inc=(gg == 3))
                        g0 = 8 * gb + 4 * g4_
                        en = "act" if ri == 0 else "dve"
                        if en == "act":
                            op("act", lambda h, g0=g0, ri=ri: h.activation(out=ZS[:, ri, :, g0:g0 + 4], in_=bkF[:].rearrange("p (n g) -> p n g", g=4), func=AF.Identity),
                               r=[bkFB], w=[B("ZSz"), B("ZSs")])
                        else:
                            op("dve", lambda h, g0=g0, ri=ri: h.tensor_copy(out=ZS[:, ri, :, g0:g0 + 4], in_=bkF[:].rearrange("p (n g) -> p n g", g=4)),
                               r=[bkFB], w=[B("ZSz"), B("ZSs")])

            if hf == 0:
                dump("Z", ZS[:], ["ZSz"])
                cut(4)
            with ExitStack() as rsc:
                W = 64 * nseq
                if hf == 0:
                    while kb.nbank % 8 < 4:
                        kb.nbank += 1
                    mbk, mbkB = mod_part(range(4, 12), 16, 32, B("mod2"))
                pst = [kb.ps[i] for i in range(4)]
                St = [[pst[a][:, ri * 256:ri * 256 + W] for ri in range(2)] for a in range(2)]
                rt = [pst[2][:, 0:W], None, pst[2][:, 256:256 + W], None, pst[3][:, 0:W], pst[3][:, 256:256 + W]]
                rts = [sb("rts%d" % i, [128, W], F32, rsc) for i in range(2)]
                rt[1], rt[3] = rts[0][:], rts[1][:]
                Aw = A4[:].rearrange("p q g -> p (q g)") if nseq == 4 else A4[:, 0, :]
                Bw = B4[:].rearrange("p q g -> p (q g)") if nseq == 4 else B4[:, 0, :]
                psB = [B("ps", i) for i in range(4)]
                if hf == 0:
                    op("dve", lambda h: h.memset(pst[0][:], 0.0), w=[psB[0], B("St", 0)])
                else:
                    op("dve", lambda h: h.memset(pst[0][:], 0.0), w=[psB[0], B("St", 0)])
                    op("dve", lambda h: h.tensor_copy(out=St[0][0], in_=s0t[:, 0, :]), r=[B("s0t")], w=[B("St", 0)])
                    op("dve", lambda h: h.tensor_copy(out=St[0][1], in_=s0t[:, 1, :]), r=[B("s0t")], w=[B("St", 0)])
                op("dve", lambda h: h.memset(pst[1][:], 0.0), w=[psB[1], B("St", 1)])
                op("dve", lambda h: h.memset(pst[2][:], 0.0), w=[psB[2], B("tA")])
                op("dve", lambda h: h.memset(pst[3][:], 0.0), w=[psB[3], B("tN", 0), B("tN", 1)])

                def zs_slot(ri, c):
                    if nseq == 4:
                        return ZS[:, ri].rearrange("p (q c) g -> p q c g", q=4)[:, :, c, :]
                    return ZS[:, ri, c, :]

                def st_view(t):
                    if nseq == 4:
                        return t.rearrange("p (q g) -> p q g", q=4)
                    return t

                tA = pst[2][:].rearrange("p (r x) -> p r x", r=2)[:, :, 0:W]
                tBs = sb("tBs", [128, 2, W], F32, rsc)
                tN = pst[3][:].rearrange("p (r x) -> p r x", r=2)[:, :, 0:W]
                Abc = Aw.unsqueeze(1).broadcast_to([128, 2, W])
                Bbc = Bw.unsqueeze(1).broadcast_to([128, 2, W])

                def st2(a_):
                    return pst[a_][:].rearrange("p (r x) -> p r x", r=2)[:, :, 0:W]

                def zs2(c, P_):
                    if nseq == 4:
                        return ZS[P_].rearrange("p r (q c) g -> p r q c g", q=4)[:, :, :, c, :]
                    return ZS[P_, :, c, :]

                def v2(ap, P_):
                    if nseq == 4:
                        return ap[P_].rearrange("p r (q g) -> p r q g", q=4)
                    return ap[P_]

                for c in range(CS):
                    cur, nxt = st2(c % 2), st2((c + 1) % 2)
                    cB, nB = B("St", c % 2), B("St", (c + 1) % 2)
                    op("dve", lambda h: h.tensor_tensor(out=tA, in0=cur, in1=Abc, op=ALU.mult), r=[cB, B("A4")], w=[B("tA")])
                    op("dve", lambda h: h.tensor_tensor(out=tBs[:], in0=cur, in1=Bbc, op=ALU.mult), r=[cB, B("A4")], w=[B("tBs")])
                    op("dve", lambda h: h.tensor_tensor(out=tN[:, 0, :], in0=tA[:, 0, :], in1=tBs[:, 1, :], op=ALU.subtract), r=[B("tA"), B("tBs")], w=[B("tN", 0)])
                    op("dve", lambda h: h.tensor_tensor(out=tN[:, 1, :], in0=tA[:, 1, :], in1=tBs[:, 0, :], op=ALU.add), r=[B("tA"), B("tBs")], w=[B("tN", 1)])
                    for d_ in range(2):
                        P_ = slice(64 * d_, 64 * d_ + 64)
                        cc = c if d_ == 0 else CS - 1 - c
                        zt = op("dve", lambda h, P_=P_, cc=cc: h.tensor_tensor(out=v2(nxt, P_), in0=v2(tN, P_), in1=zs2(cc, P_), op=ALU.add),
                                r=[B("tN", 0), B("tN", 1), B("ZSz")], w=[nB])
                        op("act", lambda h, P_=P_, cc=cc: h.activation(out=zs2(cc, P_), in_=v2(cur, P_), func=AF.Identity), r=[cB], w=[B("ZSs")], extra=[zt])
                if hf == 0:
                    mod_part2_evac(mbk, mbkB)
                fin = St[CS % 2]
                fB = B("St", CS % 2)
                fB1 = B("St", CS % 2)
                if hf == 0:
                    while kb.nbank % 8 < 4:
                        kb.nbank += 1
                    for ri in range(2):
                        op("act", lambda h, ri=ri: h.activation(out=finS[ri][:], in_=fin[ri], func=AF.Identity), r=[fB, fB1], w=[B("finS", ri)])
                        fv = finS[ri][:].rearrange("p (q g) -> p q g", q=4)
                        bk, bkB = kb.bank()
                        for q in range(4):
                            op("pe", lambda h, q=q, fv=fv: h.transpose(bk[0:64, q * 128:(q + 1) * 128], fv[:, q, :], ident[:]),
                               r=[B("finS", ri), B("ident")], w=[bkB], inc=(q == 3))
                        op("act", lambda h, ri=ri: h.activation(out=nsT[:, ri * 4:ri * 4 + 4, :].rearrange("p a b -> p (a b)"), in_=bk[0:64, :], func=AF.Identity),
                           r=[bkB], w=[B("nsT")])
                    for ri in range(2):
                        for q in range(4):
                            dma("sp", nsout[q, :, ri, :, :].rearrange("d g p -> g d p"), nsT[:, ri * 4 + q, :].rearrange("p (d x) -> p d x", d=2),
                                r=[B("nsT")], w=[B("nsout")])
                if hf == 0:
                    dump("S", ZS[:], ["ZSs"])
                if LIMIT is not None:
                    kb.barrier(dma=False)
                if hf == 0:
                    cut(5)

            with ExitStack() as ysc:
                Yb = [sb("Yb%d" % i, [128, 8, 128], BF16, ysc) for i in range(2)]
                xrl = [sb("xr%d" % i, [128, 1024], BF16, ysc) for i in range(2)]
                for gb in range(8):
                    sw = s5w[gb % 2]
                    swB = B("s5w", gb % 2)
                    dma("sp", sw[:, 0], w0b_d[:, 8 * gb:8 * gb + 8, :], r=[B("w0b_d")], w=[swB])
                    dma("sp", sw[:, 1:3], wyb_d[:, :, 8 * gb:8 * gb + 8, :], r=[B("wyb_d")], w=[swB])
                    yb = Yb[gb % 2]
                    ybB = B("Yb", gb % 2)
                    for g4 in range(2):
                        bk, bkB = kb.bank()
                        for gg in range(4):
                            gl = 4 * g4 + gg
                            g = 8 * gb + gl
                            cs_ = slice(gg * 128, (gg + 1) * 128)
                            op("pe", lambda h, gl=gl, g=g, cs_=cs_: h.matmul(bk[:, cs_], lhsT=sw[:, 0, gl, :], rhs=U[:, g, :], start=True, stop=False),
                               r=[swB, B("U")], w=[bkB], inc=False)
                            op("pe", lambda h, gl=gl, g=g, cs_=cs_: h.matmul(bk[:, cs_], lhsT=sw[:, 1, gl, :], rhs=ZS[:, 0, :, g], start=False, stop=False),
                               r=[swB, B("ZSs")], w=[bkB], inc=False)
                            op("pe", lambda h, gl=gl, g=g, cs_=cs_: h.matmul(bk[:, cs_], lhsT=sw[:, 2, gl, :], rhs=ZS[:, 1, :, g], start=False, stop=True),
                               r=[swB, B("ZSs")], w=[bkB], inc=True)
                        op("dve", lambda h, g4=g4: h.tensor_copy(out=yb[:, 4 * g4:4 * g4 + 4, :].rearrange("p a b -> p (a b)"), in_=bk[:]),
                           r=[bkB], w=[ybB])
                    ct = gb
                    bk, bkB = kb.bank()
                    bkh = bk[:].bitcast(BF16)
                    for gl in range(8):
                        op("pe", lambda h, gl=gl: h.transpose(bkh[:, gl * 128:(gl + 1) * 128], yb[:, gl, :], identb[:]),
                           r=[ybB, B("identb")], w=[bkB], inc=(gl == 7))
                    xr = xrl[gb % 2]
                    xrB = B("xr", gb % 2)
                    op("dve", lambda h: h.tensor_copy(out=xr[:].rearrange("p (j g o) -> p j g o", j=8, g=8), in_=bkh.rearrange("p (g j o) -> p j g o", g=8, j=8)),
                       r=[bkB], w=[xrB])
                    bk2, bk2B = kb.bank()
                    bk2h = bk2[:].bitcast(BF16)
                    for j in range(8):
                        op("pe", lambda h, j=j: h.transpose(bk2h[:, j * 128:(j + 1) * 128], xr[:, j * 128:(j + 1) * 128], identb[:]),
                           r=[xrB, B("identb")], w=[bk2B], inc=(j == 7))
                    op("act", lambda h, ct=ct: h.activation(out=vT[:, ct, :].rearrange("p (n j) -> p n j", j=8), in_=bk2h.rearrange("p (j n) -> p n j", j=8), func=AF.Gelu_apprx_tanh),
                       r=[bk2B], w=[B("vT", 0), B("vT", 1)])
                if hf == 0:
                    dump("vT", vT[:], [("vT", 0), ("vT", 1)])
                kb.barrier()
                if hf == 0:
                    cut(6)

        with ExitStack() as m1:
            vgT = sb("vgT", [128, 8, 1024], BF16, m1)
            mgT = sb("mgT", [128, 8, 1024], BF16, m1)
            hsc = ExitStack()
            pbT = sb("pbT", [128, 4, 1024], BF16, hsc)
            sig = [sb("sig%d" % i, [128, 512], F32, hsc) for i in range(2)]
            tmpf = [sb("tmpf%d" % i, [128, 512], F32, hsc) for i in range(2)]
            wsl.append(sb("wslx", [128, 4096], BF16, hsc))
            wp.free.append(NW)
            wp.pump()
            with ExitStack() as psc:
                ubt = [sb("ubt", [128, 512], BF16, psc) for _ in range(8)]
                pql = [sb("pq", [128, 512], BF16, psc) for _ in range(2)]
                wv, wb = wp.get((hf, "ub"))
                wpv, wpb = wp.get((hf, "wpool"))
                for tb in range(8):
                    bk, bkB = kb.bank()
                    for k in range(8):
                        op("pe", lambda h, k=k, tb=tb: h.matmul(bk[:], lhsT=hT[:, k, tb * 128:(tb + 1) * 128], rhs=wv[:, k, :], start=(k == 0), stop=(k == 7)),
                           r=[wb, hTB(tb // 4)], w=[bkB], inc=(k == 7))
                    op("dve", lambda h, tb=tb: h.tensor_tensor(out=ubt[tb][:], in0=bk[:], in1=bub[:], op=ALU.add), r=[bkB, B("bub")], w=[B("ubt", tb)])
                wp.release((hf, "ub"))
                pit = 0
                for gi in range(4):
                    gs_ = slice(gi * 128, (gi + 1) * 128)
                    for th in range(2):
                        par = pit % 2
                        pit += 1
                        pq, pqB = pql[par], B("pq", par)
                        bk, bkB = kb.bank()
                        for t4 in range(4):
                            tb = th * 4 + t4
                            cs_ = slice(t4 * 128, (t4 + 1) * 128)
                            if hf == 1:
                                op("pe", lambda h, tb=tb, cs_=cs_: h.matmul(bk[:, cs_], lhsT=ubt[tb][:, gs_], rhs=bandS[:, gi, :], start=True, stop=True),
                                   r=[B("ubt", tb), B("band")], w=[bkB], inc=(t4 == 3))
                            else:
                                k0, k1, ot = (0, 1, tb + 1) if tb % 2 == 0 else (2, 3, tb - 1)
                                op("pe", lambda h, tb=tb, cs_=cs_, k0=k0: h.matmul(bk[:, cs_], lhsT=ubt[tb][:, gs_], rhs=bandP[:, gi, k0, :], start=True, stop=False),
                                   r=[B("ubt", tb), B("band")], w=[bkB], inc=False)
                                op("pe", lambda h, ot=ot, cs_=cs_, k1=k1: h.matmul(bk[:, cs_], lhsT=ubt[ot][:, gs_], rhs=bandP[:, gi, k1, :], start=False, stop=True),
                                   r=[B("ubt", ot), B("band")], w=[bkB], inc=True)
                        op("dve", lambda h: h.tensor_copy(out=pq[:], in_=bk[:]), r=[bkB], w=[pqB])
                        bk2, bk2B = kb.bank()
                        op("pe", lambda h, gi=gi: h.matmul(bk2[:], lhsT=wpv[:, gi, :], rhs=pq[:], start=True, stop=True), r=[wpb, pqB], w=[bk2B])
                        op("act", lambda h, gi=gi, th=th: h.activation(out=pbT[:, gi, th * 512:(th + 1) * 512], in_=bk2[:], func=AF.Identity, scale=pscT[:, gi:gi + 1]),
                           r=[bk2B, B("pscT")], w=[B("pbT", th)])
                wp.release((hf, "wpool"))
            it = 0
            for wt in range(2):
                wv, wb = wp.get((hf, "glu", wt))
                for mm in range(4):
                    m = wt * 4 + mm
                    for th in range(2):
                        bk, bkB = kb.bank()
                        for k in range(8):
                            op("pe", lambda h, k=k, mm=mm, th=th: h.matmul(bk[:], lhsT=wv[:, k, mm * 128:(mm + 1) * 128], rhs=vT[:, k, th * 512:(th + 1) * 512],
                                                                          start=(k == 0), stop=(k == 7)),
                               r=[wb, B("vT", th)], w=[bkB], inc=(k == 7))
                        sg, sgB = sig[it % 2], B("sig", it % 2)
                        it += 1
                        op("act", lambda h, m=m: h.activation(out=sg[:], in_=bk[:], func=AF.Sigmoid, bias=bgluT[:, m:m + 1], scale=1.0),
                           r=[bkB, B("bgluT")], w=[sgB])
                        op("dve", lambda h, m=m, th=th: h.tensor_tensor(out=vgT[:, m, th * 512:(th + 1) * 512], in0=vT[:, m, th * 512:(th + 1) * 512], in1=sg[:], op=ALU.mult),
                           r=[sgB, B("vT", th)], w=[B("vgT", th)])
                wp.release((hf, "glu", wt))
            if hf == 0:
                dump("vgT", vgT[:], [("vgT", 0), ("vgT", 1)])
                cut(7)
            if hf == 0:
                dump("pbT", pbT[:], [("pbT", 0), ("pbT", 1)])
                cut(8)
            for wt in range(2):
                wpbv, wpbB = wp.get((hf, "pb", wt))
                wpa_v, wpaB = wp.get((hf, "pa", wt))
                wga_v, wgaB = wp.get((hf, "ga", wt))
                wgb_v, wgbB = wp.get((hf, "gb", wt))
                for mm in range(4):
                    m = wt * 4 + mm
                    ms = slice(mm * 128, (mm + 1) * 128)
                    for th in range(2):
                        ts_ = slice(th * 512, (th + 1) * 512)
                        bya, byaB = kb.bank()
                        bga, bgaB = kb.bank()
                        byb, bybB = kb.bank()
                        bgb, bgbB = kb.bank()
                        for k in range(8):
                            op("pe", lambda h, k=k: h.matmul(bya[:], lhsT=wpa_v[:, k, ms], rhs=vgT[:, k, ts_], start=(k == 0), stop=(k == 7)),
                               r=[wpaB, B("vgT", th)], w=[byaB], inc=(k == 7))
                        for k in range(8):
                            op("pe", lambda h, k=k: h.matmul(bga[:], lhsT=wga_v[:, k, ms], rhs=hT[:, k, ts_], start=(k == 0), stop=(k == 7)),
                               r=[wgaB, hTB(th)], w=[bgaB], inc=(k == 7))
                        for k in range(4):
                            op("pe", lambda h, k=k: h.matmul(byb[:], lhsT=wpbv[:, k, ms], rhs=pbT[:, k, ts_], start=(k == 0), stop=(k == 3)),
                               r=[wpbB, B("pbT", th)], w=[bybB], inc=(k == 3))
                        for k in range(8):
                            op("pe", lambda h, k=k: h.matmul(bgb[:], lhsT=wgb_v[:, k, ms], rhs=hT[:, k, ts_], start=(k == 0), stop=(k == 7)),
                               r=[wgbB, hTB(th)], w=[bgbB], inc=(k == 7))
                        op("act", lambda h: h.activation(out=sig[0][:], in_=bga[:], func=AF.Sigmoid, bias=binT[:, 12 + m:13 + m], scale=1.0),
                           r=[bgaB, B("binT")], w=[B("sig", 0)])
                        op("act", lambda h: h.activation(out=sig[1][:], in_=bgb[:], func=AF.Sigmoid, bias=binT[:, 20 + m:21 + m], scale=1.0),
                           r=[bgbB, B("binT")], w=[B("sig", 1)])
                        op("dve", lambda h: h.tensor_tensor(out=tmpf[0][:], in0=bya[:], in1=sig[0][:], op=ALU.mult), r=[byaB, B("sig", 0)], w=[B("tmpf", 0)])
                        op("dve", lambda h: h.tensor_tensor(out=tmpf[1][:], in0=byb[:], in1=sig[1][:], op=ALU.mult), r=[bybB, B("sig", 1)], w=[B("tmpf", 1)])
                        op("dve", lambda h: h.tensor_tensor(out=mgT[:, m, ts_], in0=tmpf[0][:], in1=tmpf[1][:], op=ALU.add),
                           r=[B("tmpf", 0), B("tmpf", 1)], w=[B("mgT", th)])
                for nm_ in ("pb", "pa", "ga", "gb"):
                    wp.release((hf, nm_, wt))
            if hf == 0:
                dump("mgT", mgT[:], [("mgT", 0), ("mgT", 1)])
                cut(9)
            kb.barrier()
            assert NW in wp.free
            wp.free.remove(NW)
            del wsl[NW:]
            hsc.close()
            with ExitStack() as jsc:
                tmT = [sb("tmT", [128, 8, 512], F32, jsc),
                       vgT[:].rearrange("p a b -> p (a b)").bitcast(F32).rearrange("p (a b) -> p a b", a=8)]
                rtl = [sb("rtile", [128, D], F32, jsc) for _ in range(DEPTH)]
                x1l = [sb("x1t", [128, D], F32, jsc) for _ in range(DEPTH)]
                wo = [wp.get((hf, "out", wt)) for wt in range(2)]
                for th in range(2):
                    for m in range(8):
                        wv, wb = wo[m // 4]
                        mm = m % 4
                        bk, bkB = kb.bank()
                        for k in range(8):
                            op("pe", lambda h, k=k, mm=mm, wv=wv: h.matmul(bk[:], lhsT=wv[:, k, mm * 128:(mm + 1) * 128], rhs=mgT[:, k, th * 512:(th + 1) * 512],
                                                                          start=(k == 0), stop=(k == 7)),
                               r=[wb, B("mgT", th)], w=[bkB], inc=(k == 7))
                        op("act", lambda h, m=m, th=th: h.activation(out=tmT[th][:, m, :], in_=bk[:], func=AF.Identity, scale=gt1[:, m, v:v + 1], bias=gt1b[:, m, v:v + 1]),
                           r=[bkB, B("mod2")], w=[B("tmT", th)])
                wp.release((hf, "out", 0))
                wp.release((hf, "out", 1))
                def chainJ(tb):
                    th, t4 = divmod(tb, 4)
                    par = tb % DEPTH
                    xs, xB = xsl[par], B("xs", par)
                    rt_, rB_ = rtl[par], B("rtile", par)
                    x1_, x1B_ = x1l[par], B("x1t", par)
                    dma("sp", xs[:], xin[row0 + tb * 128:row0 + (tb + 1) * 128, :], w=[xB])
                    for h2 in range(2):
                        bk, bkB = kb.bank()
                        for j in range(4):
                            ct = h2 * 4 + j
                            op("pe", lambda h, ct=ct, j=j: h.transpose(bk[:, j * 128:(j + 1) * 128], tmT[th][:, ct, t4 * 128:(t4 + 1) * 128], ident[:]),
                               r=[B("tmT", th), B("ident")], w=[bkB], inc=(j == 3))
                        op("dve", lambda h, h2=h2, bk=bk: h.scalar_tensor_tensor(out=rt_[:, h2 * 512:(h2 + 1) * 512], in0=xs[:, h2 * 512:(h2 + 1) * 512], scalar=ALPHA,
                                                                                 in1=bk[:], op0=ALU.mult, op1=ALU.add),
                           r=[bkB, xB], w=[rB_])
                    yield
                    i = ln_part1(rt_, rB_)
                    yield
                    ln_part2(i, x1_[:], rt_[:], rB_, x1B_)
                    yield
                    op("dve", lambda h: h.tensor_tensor(out=x1_[:], in0=x1_[:], in1=lnp[:, 0, :], op=ALU.mult), r=[x1B_, B("lnp")], w=[x1B_])
                    op("dve", lambda h: h.tensor_tensor(out=x1_[:], in0=x1_[:], in1=lnp[:, 1, :], op=ALU.add), r=[x1B_, B("lnp")], w=[x1B_])
                    dma("sp", x1s[row0 + tb * 128:row0 + (tb + 1) * 128, :], x1_[:], r=[x1B_], w=[B("x1s", hf, tb)])
                    i = ln_part1(x1_, x1B_)
                    yield
                    ln_part2(i, rt_[:], x1_[:], x1B_, rB_)
                    yield
                    to_feature_major(rt_, rB_, h2T, lambda th_: B("vT", th_), tb, sc2p, sh2, v)
                run_pipelined([chainJ(tb) for tb in range(8)], DEPTH)
                kb.barrier()
            if hf == 0:
                dump("h2T", h2T[:], [("vT", 0), ("vT", 1)])
            kb.barrier()
            if hf == 0:
                cut(10)

        with ExitStack() as m2:
            hid = sb("hid", [128, 32, 1024], BF16, m2)
            rl = [sb("rl%d" % i, [128, 512], F32, m2) for i in range(2)]
            fT = hT[:].rearrange("p a b -> p (a b)").bitcast(F32).rearrange("p (a b) -> p a b", a=8)
            rtl2 = [sb("rtile2", [128, D], F32, m2) for _ in range(DEPTH)]
            it = 0
            for wt in range(8):
                wv, wb = wp.get((hf, "w1", wt))
                for mm in range(4):
                    m = wt * 4 + mm
                    for th in range(2):
                        bk, bkB = kb.bank()
                        for k in range(8):
                            op("pe", lambda h, k=k, mm=mm, th=th: h.matmul(bk[:], lhsT=wv[:, k, mm * 128:(mm + 1) * 128], rhs=h2T[:, k, th * 512:(th + 1) * 512],
                                                                          start=(k == 0), stop=(k == 7)),
                               r=[wb, B("vT", th)], w=[bkB], inc=(k == 7))
                        r_, rB = rl[it % 2], B("rl", it % 2)
                        it += 1
                        op("act", lambda h, m=m: h.activation(out=r_[:], in_=bk[:], func=AF.Relu, bias=b1T[:, m:m + 1], scale=1.0),
                           r=[bkB, B("b1T")], w=[rB])
                        op("dve", lambda h, m=m, th=th: h.tensor_tensor(out=hid[:, m, th * 512:(th + 1) * 512], in0=r_[:], in1=r_[:], op=ALU.mult),
                           r=[rB], w=[B("hid", th)])
                wp.release((hf, "w1", wt))
            fTl = [fT, h2T[:].rearrange("p a b -> p (a b)").bitcast(F32).rearrange("p (a b) -> p a b", a=8)]
            def mlp2_group(th, m):
                wv, wb = wp.get((hf, "w2", th, m))
                bk, bkB = kb.bank()
                for k in range(32):
                    op("pe", lambda h, k=k: h.matmul(bk[:], lhsT=wv[:, k, :], rhs=hid[:, k, th * 512:(th + 1) * 512], start=(k == 0), stop=(k == 31)),
                       r=[wb, B("hid", th)], w=[bkB], inc=(k == 31))
                op("act", lambda h: h.activation(out=fTl[th][:, m, :], in_=bk[:], func=AF.Identity, scale=gt2[:, m, v:v + 1], bias=gt2b[:, m, v:v + 1]),
                   r=[bkB, B("mod2")], w=[B("fT", th), B("vT", 0), B("vT", 1), B("hT", 0), B("hT", 1)])
                wp.release((hf, "w2", th, m))
                if hf == 0 and th == 1 and m == 4:
                    wp.plan(half_plan(1))
            for m in range(8):
                mlp2_group(0, m)
            def chainF(tb):
                th, t4 = divmod(tb, 4)
                par = tb % DEPTH
                xs, xB = xsl[par], B("xs", par)
                rt_, rB_ = rtl2[par], B("rtile2", par)
                dma("sp", xs[:], x1s[row0 + tb * 128:row0 + (tb + 1) * 128, :], r=[B("x1s", hf, tb)], w=[xB])
                for h2 in range(2):
                    bk, bkB = kb.bank()
                    for j in range(4):
                        ct = h2 * 4 + j
                        op("pe", lambda h, ct=ct, j=j: h.transpose(bk[:, j * 128:(j + 1) * 128], fTl[th][:, ct, t4 * 128:(t4 + 1) * 128], ident[:]),
                           r=[B("fT", th), B("ident")], w=[bkB], inc=(j == 3))
                    op("dve", lambda h, h2=h2, bk=bk: h.scalar_tensor_tensor(out=rt_[:, h2 * 512:(h2 + 1) * 512], in0=xs[:, h2 * 512:(h2 + 1) * 512], scalar=ALPHA,
                                                                             in1=bk[:], op0=ALU.mult, op1=ALU.add),
                       r=[bkB, xB], w=[rB_])
                yield
                i = ln_part1(rt_, rB_)
                yield
                ln_part2(i, rt_[:], rt_[:], rB_, rB_)
                yield
                op("dve", lambda h: h.tensor_tensor(out=rt_[:], in0=rt_[:], in1=lnp[:, 2, :], op=ALU.mult), r=[rB_, B("lnp")], w=[rB_])
                op("dve", lambda h: h.tensor_tensor(out=rt_[:], in0=rt_[:], in1=lnp[:, 3, :], op=ALU.add), r=[rB_, B("lnp")], w=[rB_])
                dma("sp", yout[row0 + tb * 128:row0 + (tb + 1) * 128, :], rt_[:], r=[rB_], w=[B("yout")])
            todo = list(range(8))

            def fill():
                for _ in range(2):
                    if todo:
                        mlp2_group(1, todo.pop(0))
            run_pipelined([chainF(tb) for tb in range(4)], DEPTH, filler=fill)
            while todo:
                mlp2_group(1, todo.pop(0))
            run_pipelined([chainF(tb) for tb in range(4, 8)], DEPTH)
            kb.barrier()
            if hf == 0:
                cut(11)

    kb.barrier(final=True)


_CACHE = {}


def _consts():
    pcm = np.zeros((128, 8, 240), np.float32)
    for a in range(8):
        for h in range(16):
            pcm[16 * a + h, a, 112 + h] = 1.0
    i_of = np.arange(128) // 16
    maskf = (i_of[:, None] <= i_of[None, :]).astype(np.float32)
    maskb = (i_of[:, None] >= i_of[None, :]).astype(np.float32)
    ident = np.eye(128, dtype=np.float32)

    def pm(n):
        out = np.zeros((4, n, n), np.float64)
        t = np.arange(n)
        for gi, w in enumerate((2, 4, 8, 16)):
            lo = np.clip(t - w // 2, 0, n)
            hi = np.clip(t - w // 2 + w, 0, n)
            for to in range(n):
                out[gi, lo[to]:hi[to], to] = 1.0 / float(hi[to] - lo[to])
                out[gi, to, to] -= 1.0
        return out
    P256 = pm(256)
    bandp = np.zeros((128, 4, 4, 128), np.float32)
    for gi in range(4):
        bandp[:, gi, 0, :] = P256[gi, 0:128, 0:128]
        bandp[:, gi, 1, :] = P256[gi, 128:256, 0:128]
        bandp[:, gi, 2, :] = P256[gi, 128:256, 128:256]
        bandp[:, gi, 3, :] = P256[gi, 0:128, 128:256]
    P64 = pm(64)
    bands = np.zeros((128, 4, 128), np.float32)
    for gi in range(4):
        bands[0:64, gi, 0:64] = P64[gi]
        bands[64:128, gi, 64:128] = P64[gi]
    return {
        "c_pc": pcm.reshape(128, 8 * 240), "c_maskf": maskf, "c_maskb": maskb, "c_ident": ident,
        "c_bandp": bandp.reshape(128, -1), "c_bands": bands.reshape(128, -1),
    }


def kernel(x_prompt, x_sample, state_s5, c, c_ctx, w_ada, b_ada, w_in, b_in,
           s5_lam_re, s5_lam_im, s5_log_dt, s5_b_re, s5_b_im, s5_c_re, s5_c_im, s5_d,
           w_glu, b_glu, w_proj_a, w_pool, pool_scale, w_proj_b, w_out, b_out,
           ln1_g, ln1_b, w_mlp1, b_mlp1, w_mlp2, b_mlp2, ln2_g, ln2_b):
    f = lambda a: np.ascontiguousarray(np.asarray(a, dtype=np.float32))
    if "nc" not in _CACHE:
        _CACHE["nc"] = build_program()
    nc = _CACHE["nc"]
    xp = f(x_prompt)
    xs = f(x_sample)
    st = f(state_s5)
    cc = f(c)
    shared = {
        "w_ada": f(w_ada)[0], "b_ada": f(b_ada)[0], "w_in": f(w_in)[0], "b_in": f(b_in)[0],
        "lam_re": f(s5_lam_re)[0], "lam_im": f(s5_lam_im)[0], "log_dt": f(s5_log_dt)[0],
        "b_re": f(s5_b_re)[0], "b_im": f(s5_b_im)[0], "c_re": f(s5_c_re)[0], "c_im": f(s5_c_im)[0],
        "s5_d": f(s5_d)[0], "w_glu": f(w_glu)[0], "b_glu": f(b_glu)[0], "w_pa": f(w_proj_a)[0],
        "w_pool": f(w_pool)[0], "pool_scale": f(pool_scale)[0], "w_pb": f(w_proj_b)[0],
        "w_out": f(w_out)[0], "b_out": f(b_out)[0], "ln1_g": f(ln1_g), "ln1_b": f(ln1_b),
        "w1": f(w_mlp1)[0], "b1": f(b_mlp1)[0], "w2": f(w_mlp2)[0], "b2": f(b_mlp2)[0],
        "ln2_g": f(ln2_g), "ln2_b": f(ln2_b),
    }
    shared.update(_consts())
    in_maps = []
    for i in range(NCORES):
        m = dict(shared)
        m["xin"] = np.concatenate([xp[4 * i:4 * i + 4].reshape(1024, D), xs[i]], axis=0)
        m["st0"] = np.ascontiguousarray(st[i, 0])
        m["cvec"] = np.stack([f(c_ctx), cc[i]], axis=0)
        in_maps.append(m)
    res = run_bass_kernel_spmd(nc, in_maps, core_ids=list(range(NCORES)))
    yp = np.zeros((32, 256, D), np.float32)
    ys = np.zeros((8, 1024, D), np.float32)
    ns = np.zeros((32, 1, 2, 2, 64, 64), np.float32)
    for i in range(NCORES):
        r = res.results[i]
        yo = np.asarray(r["yout"])
        yp[4 * i:4 * i + 4] = yo[0:1024].reshape(4, 256, D)
        ys[i] = yo[1024:2048]
        ns[4 * i:4 * i + 4, 0] = np.asarray(r["nsout"])
    return yp, ys, ns
```
